# Optimizing a Trainium2 kernel written in Bass

```python
import jax, jax.numpy as jnp
from jax import lax
import numpy as np

D_MODEL = 1024
BATCH = 4
SEQ = 4096
DEPTH = 2
DEC_BATCH = 32
DEC_SEQ = 16
PAST_LEN = 1024

CHUNK = 64
N_META = 16
D_POOL = 256
POOL_WINDOWS = (2, 4, 8, 16)
N_POOL_GROUPS = 4
POOL_GROUP = D_POOL // N_POOL_GROUPS
POOL_STATE = 15
D_CONV = 256
CONV_W = 3
N_HEADS = 8
HEAD_DIM = 64
D_ATTN = N_HEADS * HEAD_DIM
Q_BLOCK = 128
N_BRANCH = 3
D_FF = 2816
RMS_EPS = 1e-6
IN_COLS = D_POOL + 3 * D_CONV + 3 * D_ATTN + N_BRANCH * D_MODEL

kernel_name = "hybrid_pool_conv_stickbreak_streaming_step"


def rmsnorm(x, g):
    xf = x.astype(jnp.float32)
    y = xf * lax.rsqrt(jnp.mean(xf * xf, axis=-1, keepdims=True) + RMS_EPS)
    return (y * g.astype(jnp.float32)).astype(x.dtype)


def swiglu(x, w_gate, w_up, w_down):
    return (jax.nn.silu(x @ w_gate) * (x @ w_up)) @ w_down


def pool_mix(a, prev, pos0, w_group, scale):
    bsz, t, _ = a.shape
    ext = jnp.concatenate([prev, a], axis=1)
    c = jnp.cumsum(ext.astype(jnp.float32), axis=1)
    c = jnp.pad(c, ((0, 0), (1, 0), (0, 0)))
    cur = c[:, POOL_STATE + 1:]
    pos = pos0 + jnp.arange(t)
    means = []
    for g, w in enumerate(POOL_WINDOWS):
        sl = slice(g * POOL_GROUP, (g + 1) * POOL_GROUP)
        s = cur[..., sl] - c[:, POOL_STATE + 1 - w:POOL_STATE + 1 - w + t, sl]
        cnt = jnp.minimum(pos + 1, w).astype(jnp.float32)[None, :, None]
        means.append(s / cnt)
    p = (jnp.concatenate(means, axis=-1) - a.astype(jnp.float32)).astype(a.dtype)
    p = p.reshape(bsz, t, N_POOL_GROUPS, POOL_GROUP)
    p = jnp.einsum('btgc,gcd->btgd', p, w_group).reshape(bsz, t, D_POOL) * scale
    return p, ext[:, -POOL_STATE:]


def conv_mix(xb, gate_b, gate_c, prev, conv_w):
    t = xb.shape[1]
    ext = jnp.concatenate([prev, gate_c * xb], axis=1)
    y = ext[:, 0:t] * conv_w[0]
    for j in range(1, CONV_W):
        y = y + ext[:, j:j + t] * conv_w[j]
    return gate_b * y, ext[:, -(CONV_W - 1):]


def stick_breaking(q, k, v, q_pos0):
    tq, tk = q.shape[1], k.shape[1]
    outs = []
    for qs in range(0, tq, Q_BLOCK):
        qe = min(qs + Q_BLOCK, tq)
        n_keys = max(1, min(tk, q_pos0 + qe - 1))
        z = jnp.einsum('bqhd,bkhd->bhqk', q[:, qs:qe].astype(jnp.float32),
                       k[:, :n_keys].astype(jnp.float32)) * (HEAD_DIM ** -0.5)
        tpos = q_pos0 + jnp.arange(qs, qe)
        spos = jnp.arange(n_keys)
        causal = spos[None, :] < tpos[:, None]
        log_keep = jnp.where(causal, jax.nn.log_sigmoid(-z), 0.0)
        later = lax.cumsum(log_keep, axis=3, reverse=True) - log_keep
        a = jnp.where(causal, jnp.exp(jax.nn.log_sigmoid(z) + later), 0.0)
        o = jnp.einsum('bhqk,bkhd->bqhd', a, v[:, :n_keys].astype(jnp.float32))
        outs.append(o.astype(q.dtype))
    return jnp.concatenate(outs, axis=1)


def trunk_layer(x, pool_prev, conv_prev, k_past, v_past, w):
    (n1, f1g, f1u, f1d, nm, w_in, pool_w, pool_s, pool_proj, conv_w, conv_proj,
     qn, kn, attn_proj, w_out, n2, f2g, f2u, f2d) = w
    bsz, t, _ = x.shape
    pos0 = k_past.shape[1]
    h = x + 0.5 * swiglu(rmsnorm(x, n1), f1g, f1u, f1d)
    u = rmsnorm(h, nm)
    proj = u @ w_in
    cuts = np.cumsum([D_POOL, D_CONV, D_CONV, D_CONV, D_ATTN, D_ATTN, D_ATTN]).tolist()
    a_in, xb, gb, gc, q, k, v, g_logits = jnp.split(proj, cuts, axis=-1)
    y_a, pool_state = pool_mix(a_in, pool_prev, pos0, pool_w, pool_s)
    y_a = y_a @ pool_proj
    y_b, conv_state = conv_mix(xb, gb, gc, conv_prev, conv_w)
    y_b = y_b @ conv_proj
    q = rmsnorm(q.reshape(bsz, t, N_HEADS, HEAD_DIM), qn)
    k = rmsnorm(k.reshape(bsz, t, N_HEADS, HEAD_DIM), kn)
    v = v.reshape(bsz, t, N_HEADS, HEAD_DIM)
    k_all = jnp.concatenate([k_past, k], axis=1)
    v_all = jnp.concatenate([v_past, v], axis=1)
    o = stick_breaking(q, k_all, v_all, pos0)
    y_c = o.reshape(bsz, t, D_ATTN) @ attn_proj
    g = jax.nn.sigmoid(g_logits.astype(jnp.float32)).astype(x.dtype).reshape(bsz, t, N_BRANCH, D_MODEL)
    mixed = g[:, :, 0] * y_a + g[:, :, 1] * y_b + g[:, :, 2] * y_c
    h = h + mixed @ w_out
    h = h + 0.5 * swiglu(rmsnorm(h, n2), f2g, f2u, f2d)
    return h, k, v, pool_state, conv_state


def setup_inputs(seed: int = 0) -> dict:
    key = jax.random.key(seed)
    ks = jax.random.split(key, 32)
    f32 = jnp.float32

    def nrm(k, shape, scale):
        return jax.random.normal(k, shape, f32) * scale

    def gain(k, shape):
        return 1.0 + 0.05 * jax.random.normal(k, shape, f32)

    return {
        "x_prompt": nrm(ks[0], (BATCH, SEQ, D_MODEL), 1.0),
        "x_sample": nrm(ks[1], (DEC_BATCH, DEC_SEQ, D_MODEL), 1.0),
        "cache_k": nrm(ks[2], (DEPTH, DEC_BATCH, PAST_LEN, N_HEADS, HEAD_DIM), 1.0),
        "cache_v": nrm(ks[3], (DEPTH, DEC_BATCH, PAST_LEN, N_HEADS, HEAD_DIM), 1.0),
        "state_pool": nrm(ks[4], (DEPTH, DEC_BATCH, POOL_STATE, D_POOL), 1.0),
        "state_conv": nrm(ks[5], (DEPTH, DEC_BATCH, CONV_W - 1, D_CONV), 1.0),
        "meta": nrm(ks[6], (N_META, D_MODEL), 1.0),
        "ffn1_norm": gain(ks[7], (DEPTH, D_MODEL)),
        "ffn1_w_gate": nrm(ks[8], (DEPTH, D_MODEL, D_FF), D_MODEL ** -0.5),
        "ffn1_w_up": nrm(ks[9], (DEPTH, D_MODEL, D_FF), D_MODEL ** -0.5),
        "ffn1_w_down": nrm(ks[10], (DEPTH, D_FF, D_MODEL), D_FF ** -0.5),
        "mix_norm": gain(ks[11], (DEPTH, D_MODEL)),
        "w_in": nrm(ks[12], (DEPTH, D_MODEL, IN_COLS), D_MODEL ** -0.5),
        "pool_w": nrm(ks[13], (DEPTH, N_POOL_GROUPS, POOL_GROUP, POOL_GROUP), POOL_GROUP ** -0.5),
        "pool_scale": gain(ks[14], (DEPTH, D_POOL)),
        "pool_proj": nrm(ks[15], (DEPTH, D_POOL, D_MODEL), D_POOL ** -0.5),
        "conv_w": nrm(ks[16], (DEPTH, CONV_W, D_CONV), CONV_W ** -0.5),
        "conv_proj": nrm(ks[17], (DEPTH, D_CONV, D_MODEL), D_CONV ** -0.5),
        "q_norm": gain(ks[18], (DEPTH, N_HEADS, HEAD_DIM)),
        "k_norm": gain(ks[19], (DEPTH, N_HEADS, HEAD_DIM)),
        "attn_proj": nrm(ks[20], (DEPTH, D_ATTN, D_MODEL), D_ATTN ** -0.5),
        "w_out": nrm(ks[21], (DEPTH, D_MODEL, D_MODEL), D_MODEL ** -0.5),
        "ffn2_norm": gain(ks[22], (DEPTH, D_MODEL)),
        "ffn2_w_gate": nrm(ks[23], (DEPTH, D_MODEL, D_FF), D_MODEL ** -0.5),
        "ffn2_w_up": nrm(ks[24], (DEPTH, D_MODEL, D_FF), D_MODEL ** -0.5),
        "ffn2_w_down": nrm(ks[25], (DEPTH, D_FF, D_MODEL), D_FF ** -0.5),
    }


def reference(x_prompt, x_sample, cache_k, cache_v, state_pool, state_conv, meta,
              ffn1_norm, ffn1_w_gate, ffn1_w_up, ffn1_w_down, mix_norm, w_in,
              pool_w, pool_scale, pool_proj, conv_w, conv_proj, q_norm, k_norm,
              attn_proj, w_out, ffn2_norm, ffn2_w_gate, ffn2_w_up, ffn2_w_down):
    bsz = x_prompt.shape[0]
    dt = x_prompt.dtype
    meta_b = jnp.broadcast_to(meta[None].astype(dt), (bsz, N_META, D_MODEL))
    hp = jnp.concatenate([meta_b, x_prompt], axis=1)
    hs = x_sample
    zero_pool = jnp.zeros((bsz, POOL_STATE, D_POOL), dt)
    zero_conv = jnp.zeros((bsz, CONV_W - 1, D_CONV), dt)
    zero_kv = jnp.zeros((bsz, 0, N_HEADS, HEAD_DIM), dt)
    kp, vp, pp, cp, ks_, vs_, ps_, cs_ = [], [], [], [], [], [], [], []
    for l in range(DEPTH):
        w = (ffn1_norm[l], ffn1_w_gate[l], ffn1_w_up[l], ffn1_w_down[l], mix_norm[l], w_in[l],
             pool_w[l], pool_scale[l], pool_proj[l], conv_w[l], conv_proj[l],
             q_norm[l], k_norm[l], attn_proj[l], w_out[l],
             ffn2_norm[l], ffn2_w_gate[l], ffn2_w_up[l], ffn2_w_down[l])
        hp, k_new, v_new, pool_new, conv_new = trunk_layer(hp, zero_pool, zero_conv, zero_kv, zero_kv, w)
        kp.append(k_new); vp.append(v_new); pp.append(pool_new); cp.append(conv_new)
        hs, k_new, v_new, pool_new, conv_new = trunk_layer(
            hs, state_pool[l], state_conv[l], cache_k[l], cache_v[l], w)
        ks_.append(k_new); vs_.append(v_new); ps_.append(pool_new); cs_.append(conv_new)
    y_prompt = hp[:, N_META:]
    return (y_prompt, hs, jnp.stack(kp), jnp.stack(vp), jnp.stack(pp), jnp.stack(cp),
            jnp.stack(ks_), jnp.stack(vs_), jnp.stack(ps_), jnp.stack(cs_))
```

```python
import numpy as np
from contextlib import ExitStack
import concourse.bass as bass
import concourse.mybir as mybir
from concourse.bass_utils import run_bass_kernel_spmd

F32 = mybir.dt.float32
BF16 = mybir.dt.bfloat16
AF = mybir.ActivationFunctionType
ALU = mybir.AluOpType
AX = mybir.AxisListType

D = 1024
DFF = 2816
NFF = 22
NMETA = 16
TP = 4112
NS = 4
TS = 16
NT = TP + NS * TS
PAST = 1024
DEPTH = 2
EPS = 1e-6
import os
STOP = int(os.environ.get("MK_STOP", "99"))
SUB = int(os.environ.get("MK_SUB", "99"))
ATL = int(os.environ.get("MK_ATT", "99"))
NTILES = int(os.environ.get("MK_TILES", "9"))
NLAYERS = int(os.environ.get("MK_LAYERS", "2"))
TILES = [(i * 512, 512) for i in range(8)] + [(4096, 80)]


class _Op:
    __slots__ = ("idx", "eng", "fn", "deps", "kind", "dkey", "sem", "val", "signal")


class Sched:
    ENGS = ("pe", "act", "dve", "pool", "sp")

    def __init__(self):
        self.ops = []
        self.lastw = {}
        self.readers = {}

    def add(self, eng, fn, reads=(), writes=(), kind="c", dkey=None):
        op = _Op()
        op.idx = len(self.ops)
        op.eng = eng
        op.fn = fn
        op.kind = kind
        op.dkey = dkey
        deps = set()
        excl = [k for k in reads if isinstance(k, tuple) and k[0] in ("bank", "b")]
        if excl:
            reads = [k for k in reads if k not in excl]
            writes = list(writes) + excl
        for k in reads:
            w = self.lastw.get(k)
            if w is not None:
                deps.add(w)
        for k in writes:
            w = self.lastw.get(k)
            if w is not None:
                deps.add(w)
            for r in self.readers.get(k, ()):
                deps.add(r)
        op.deps = deps
        for k in writes:
            self.lastw[k] = op.idx
            self.readers[k] = []
        for k in reads:
            self.readers.setdefault(k, []).append(op.idx)
        op.signal = kind != "c"
        op.sem = None
        op.val = 0
        self.ops.append(op)
        return op.idx

    def emit(self, nc, stack):
        ops = self.ops
        for op in ops:
            for d in op.deps:
                Dd = ops[d]
                if Dd.kind == "c" and Dd.eng == "pe" and op.eng == "pe" and op.kind == "c":
                    continue
                Dd.signal = True
        esem = {e: stack.enter_context(nc.semaphore("s_" + e)) for e in self.ENGS}
        dsem = {}
        cnt = {}
        for op in ops:
            if op.kind == "c":
                if op.signal:
                    cnt[op.eng] = cnt.get(op.eng, 0) + 1
                    op.sem = esem[op.eng]
                    op.val = cnt[op.eng]
            else:
                if op.dkey not in dsem:
                    dsem[op.dkey] = stack.enter_context(nc.semaphore("d_%d" % len(dsem)))
                cnt[("d", op.dkey)] = cnt.get(("d", op.dkey), 0) + 16
                op.sem = dsem[op.dkey]
                op.val = cnt[("d", op.dkey)]
        self.n_sems = len(dsem) + 5
        per = {e: [] for e in self.ENGS}
        for op in ops:
            per[op.eng].append(op)
        block = stack.enter_context(nc.Block())
        handles = {"pe": block.tensor, "act": block.scalar, "dve": block.vector,
                   "pool": block.gpsimd, "sp": block.sync}
        finals = {}
        for op in ops:
            if op.kind == "d":
                finals[id(op.sem)] = (op.sem, op.val)

        def mk(e):
            def body(eng):
                waited = {}
                for op in per[e]:
                    need = {}
                    for d in op.deps:
                        Dd = ops[d]
                        if Dd.kind == "c" and Dd.eng == "pe" and e == "pe" and op.kind == "c":
                            continue
                        k = id(Dd.sem)
                        if Dd.val > need.get(k, (None, 0))[1]:
                            need[k] = (Dd.sem, Dd.val)
                    for k, (s, v) in need.items():
                        if waited.get(k, 0) >= v:
                            continue
                        eng.wait_ge(s, v)
                        waited[k] = v
                    inst = op.fn(eng)
                    if op.signal:
                        inst.then_inc(op.sem, 1 if op.kind == "c" else 16)
                if e == "sp":
                    for s, v in finals.values():
                        eng.wait_ge(s, v)
            return body

        for e in self.ENGS:
            if per[e] or e == "sp":
                handles[e](mk(e))


WNAMES = ["ffn1_norm", "ffn1_w_gate", "ffn1_w_up", "ffn1_w_down", "mix_norm", "w_in", "pool_w",
          "pool_scale", "pool_proj", "conv_w", "conv_proj", "q_norm", "k_norm", "attn_proj",
          "w_out", "ffn2_norm", "ffn2_w_gate", "ffn2_w_up", "ffn2_w_down"]
WSHAPES = {
    "ffn1_norm": [2, D], "ffn1_w_gate": [2, D, DFF], "ffn1_w_up": [2, D, DFF], "ffn1_w_down": [2, DFF, D],
    "mix_norm": [2, D], "w_in": [2, D, 5632], "pool_w": [2, 4, 64, 64], "pool_scale": [2, 256],
    "pool_proj": [2, 256, D], "conv_w": [2, 3, 256], "conv_proj": [2, 256, D], "q_norm": [2, 8, 64],
    "k_norm": [2, 8, 64], "attn_proj": [2, 512, D], "w_out": [2, D, D], "ffn2_norm": [2, D],
    "ffn2_w_gate": [2, D, DFF], "ffn2_w_up": [2, D, DFF], "ffn2_w_down": [2, DFF, D],
}


def build(depth=DEPTH):
    nc = bass.Bass("TRN2", target_bir_lowering=False)
    S = Sched()

    def din(name, shape, dt=F32):
        return nc.dram_tensor(name, shape, dt, kind="ExternalInput").ap()

    def dout(name, shape):
        return nc.dram_tensor(name, shape, F32, kind="ExternalOutput").ap()

    PROBE = os.environ.get("MK_PROBE", "")
    if PROBE:
        pq = din("pq", [128, 4, 128])
        pk = din("pk", [128, 4, 384])
        pv = din("pv", [384, 512])
        pout = dout("pout", [128, 4, 128])
        pdbg = dout("pdbg", [4, 128, 128])
        xin = ck = cv = spool = sconv = None
        W = {}
    else:
        xin = din("xin", [NT, D])
        ck = din("ck", [2, NS, PAST, 512])
        cv = din("cv", [2, NS, PAST, 512])
        spool = din("spool", [2, NS, 15, 256])
        sconv = din("sconv", [2, NS, 2, 256])
        W = {n: din(n, WSHAPES[n]) for n in WNAMES}
    cU = din("cU", [128, 128])
    cL = din("cL", [128, 128])
    cI = din("cI", [128, 128])
    cT = din("cT", [128, 1024])
    cOnes = din("cOnes", [128, 128])
    cInvw = din("cInvw", [128, 2])
    cInvc = din("cInvc", [128, 2, 16])
    if not PROBE:
        y_o = dout("y", [NT, D])
        k_o = dout("ko", [2, NT, 512])
        v_o = dout("vo", [2, NT, 512])
        p_o = dout("po", [2, 5, 15, 256])
        c_o = dout("co", [2, 5, 2, 256])

    sc = {}
    for l in range(depth):
        for f in ("1", "2"):
            sc["g" + f, l] = nc.dram_tensor("sg%s_%d" % (f, l), [NFF, 128, 1024], BF16)
            sc["u" + f, l] = nc.dram_tensor("su%s_%d" % (f, l), [NFF, 128, 1024], BF16)
            sc["d" + f, l] = nc.dram_tensor("sd%s_%d" % (f, l), [8, 128, DFF], BF16)
        sc["wf", l] = nc.dram_tensor("swf_%d" % l, [32, 128, 1024], BF16)
        sc["qkv", l] = nc.dram_tensor("sqkv_%d" % l, [128, 8 * 1536], BF16)
        sc["wo", l] = nc.dram_tensor("swo_%d" % l, [8, 128, 1024], BF16)
        sc["ap", l] = nc.dram_tensor("sap_%d" % l, [8, 128, 512], BF16)
        sc["pp", l] = nc.dram_tensor("spp_%d" % l, [8, 128, 256], BF16)
        sc["cp", l] = nc.dram_tensor("scp_%d" % l, [8, 128, 256], BF16)
    hT_s = nc.dram_tensor("hT_s", [128, 8, NT], F32)
    vb_s = nc.dram_tensor("vb_s", [NT, 512], BF16)
    grp = {}

    with ExitStack() as st:
        def sb(name, shape, dt=F32):
            return st.enter_context(nc.sbuf_tensor(name, shape, dt))

        KT = sb("KT", [128, 4, 4224], BF16)
        KTs = sb("KTs", [128, 4, 1024], BF16)
        KTt = sb("KTt", [128, 4, 80], BF16)
        hT = sb("hT", [128, 8, 512])
        uB = sb("uB", [128, 8, 512], BF16)
        R = sb("R", [128, 32, 512], BF16)
        Rf = R[:, :, :].bitcast(F32)
        gu = [sb("gu%d" % i, [128, 2, 1024], BF16) for i in range(3)]
        bigs = [sb("big%d" % i, [128, DFF], BF16) for i in range(2)]
        wfs = [sb("wfs%d" % i, [128, 1024], BF16) for i in range(3)]
        exta = sb("exta", [128, 2, 160 + 512 - 145])
        extc = sb("extc", [128, 2, 514])
        Sb = [sb("Sb%d" % i, [128, 527]) for i in range(2)]
        gbt = sb("gbt", [128, 2, 512])
        xbt = sb("xbt", [128, 512])
        pin = sb("pin", [128, 2, 512], BF16)
        yain = sb("yain", [128, 2, 512], BF16)
        ybin = sb("ybin", [128, 2, 512], BF16)
        cvt = sb("cvt", [128, 512])
        tkf = [sb("tkf%d" % i, [128, 512]) for i in range(2)]
        tvf = [sb("tvf%d" % i, [128, 512]) for i in range(2)]
        tsq2 = [sb("tsq0", [128, 512]), xbt]
        tkb2 = [sb("tkb0", [128, 512], BF16), pin[:, 0, :]]
        tvb = [sb("tvb%d" % i, [128, 512], BF16) for i in range(2)]
        ssq82 = [sb("ssq8_%d" % i, [128, 8]) for i in range(2)]
        QT = sb("QT", [128, 4, 512], BF16)
        QTo = sb("QTo", [128, 4, 512], BF16)
        OT = sb("OT", [128, 4, 512], BF16)
        xl = sb("xl", [128, 1024])
        rstd = sb("rstd", [128, 512])
        sig = [sb("sig%d" % i, [128, 512]) for i in range(2)]
        mtmp = sb("mtmp", [128, 512])
        macc = sb("macc", [128, 512])
        vsl = [sb("vsl%d" % i, [128, 512], BF16) for i in range(3)]
        ksl = [sb("ksl%d" % i, [128, 512], BF16) for i in range(2)]
        If = sb("If", [128, 128])
        Ib = sb("Ib", [128, 128], BF16)
        Ub = sb("Ub", [128, 128], BF16)
        Lb = sb("Lb", [128, 128], BF16)
        Ob = sb("Ob", [128, 128], BF16)
        O1 = sb("O1", [128, 128], BF16)
        Tf = sb("Tf", [128, 128])
        Tb = sb("Tb", [128, 128], BF16)
        invw = sb("invw", [128, 2])
        invc = sb("invc", [128, 2, 16])
        gn = sb("gn", [128, 3, 8])
        qg = sb("qg", [128, 512])
        kg = sb("kg", [128, 512])
        pscale = sb("pscale", [128, 2])
        cw = sb("cw", [128, 2, 3])
        pwb = sb("pwb", [128, 2, 128], BF16)
        pwf = sb("pwf", [128, 2, 128])
        dummy = sb("dmy", [128, 8])
        P = [st.enter_context(nc.psum_tensor("P%d" % i, [128, 1024], F32)) for i in range(4)]

        def bank(i):
            return P[i // 2][:, (i % 2) * 512:(i % 2) * 512 + 512]

        def bkey(i):
            return ("bank", i)

        def dma(eng, out, in_, reads, writes, dkey, slow=False):
            if slow:
                S.add(eng, lambda e: e.dma_start(out=out, in_=in_, allow_slow_non_contiguous=True),
                      reads=reads, writes=writes, kind="d", dkey=dkey)
            else:
                S.add(eng, lambda e: e.dma_start(out=out, in_=in_), reads=reads, writes=writes, kind="d", dkey=dkey)

        def mm(out, lhsT, rhs, start, stop, reads, writes):
            S.add("pe", lambda e: e.matmul(out, lhsT=lhsT, rhs=rhs, start=start, stop=stop, skip_group_check=True),
                  reads=reads, writes=writes)

        def tr(out, in_, ident, reads, writes):
            S.add("pe", lambda e: e.matmul(out, lhsT=in_, rhs=ident, start=True, stop=True), reads=reads, writes=writes)

        def act(out, in_, func, reads, writes, bias=None, scale=None):
            kw = {}
            if bias is not None:
                kw["bias"] = bias
            if scale is not None:
                kw["scale"] = scale
            S.add("act", lambda e: e.activation(out=out, in_=in_, func=func, **kw), reads=reads, writes=writes)

        def tt(eng, out, a, b, op, reads, writes):
            S.add(eng, lambda e: e.tensor_tensor(out=out, in0=a, in1=b, op=op), reads=reads, writes=writes)

        def stt(eng, out, a, scalar, b, op0, op1, reads, writes):
            S.add(eng, lambda e: e.scalar_tensor_tensor(out=out, in0=a, scalar=scalar, in1=b, op0=op0, op1=op1),
                  reads=reads, writes=writes)

        def ts1(eng, out, a, scalar, op, reads, writes):
            S.add(eng, lambda e: e.tensor_scalar(out=out, in0=a, scalar1=scalar, scalar2=None, op0=op),
                  reads=reads, writes=writes)

        def cp(eng, out, in_, reads, writes):
            if eng == "act":
                S.add(eng, lambda e: e.activation(out=out, in_=in_, func=AF.Copy), reads=reads, writes=writes)
            else:
                S.add(eng, lambda e: e.tensor_copy(out=out, in_=in_), reads=reads, writes=writes)

        def ms(eng, ap, val, writes):
            S.add(eng, lambda e: e.memset(ap, val), writes=writes)

        dma("sp", If[:], cI[:, :], [], ["If"], "c0")
        dma("sp", Tf[:], cT[:, 0:128], [], ["Tf"], "c1")
        dma("sp", invw[:], cInvw[:, :], [], ["invw"], "c2")
        dma("sp", invc[:], cInvc[:, :, :], [], ["invc"], "c3")
        dma("pool", Ib[:], cI[:, :], [], ["Ib"], "c4")
        dma("pool", Ub[:], cU[:, :], [], ["Ub"], "c5")
        dma("pool", Lb[:], cL[:, :], [], ["Lb"], "c6")
        dma("pool", Tb[:], cT[:, 0:128], [], ["Tb"], "c7")
        dma("pool", O1[:], cOnes[:, :], [], ["O1"], "c8")
        ts1("dve", Ob[:], O1[:], 1.0 / 1024.0, ALU.mult, ["O1"], ["Ob"])
        ms("dve", QT[:], 0.0, ["QT"])
        ms("dve", QTo[:], 0.0, ["QT"])

        def convert(l):
            def cdma(key, idx, out, in_):
                k = (key, l, idx)
                grp.setdefault((key, l), []).append(k)
                fine = (l == 0 and key in ("g1", "u1"))
                dma("pool", out, in_, [], [k], ("cv", key, l))
            for f, (gname, uname, dname) in (("1", ("ffn1_w_gate", "ffn1_w_up", "ffn1_w_down")),
                                             ("2", ("ffn2_w_gate", "ffn2_w_up", "ffn2_w_down"))):
                srcs = {nm: W[wn][l].rearrange("(k p) (c m) -> c p k m", p=128, m=128) for nm, wn in (("g", gname), ("u", uname))}
                dsts = {nm: sc[nm + f, l].ap().rearrange("c p (k m) -> c p k m", m=128) for nm in ("g", "u")}
                for c in range(NFF):
                    for nm in ("g", "u"):
                        cdma(nm + f, c, dsts[nm][c], srcs[nm][c])
                src = W[dname][l].rearrange("(c p) (m n) -> m p c n", p=128, n=128)
                dst = sc["d" + f, l].ap().rearrange("m p (c n) -> m p c n", n=128)
                for m in range(8):
                    cdma("d" + f, m, dst[m], src[m])
                if f == "1":
                    src = W["w_in"][l].rearrange("(k p) (c m) -> c p k m", p=128, m=128)
                    dst = sc["wf", l].ap().rearrange("c p (k m) -> c p k m", m=128)
                    for i in range(8):
                        cdma("wf", i, dst[i], src[i])
                    cdma("qkv", 0, sc["qkv", l].ap().rearrange("p (k n) -> p k n", n=1536),
                         W["w_in"][l][:, 1024:2560].rearrange("(k p) n -> p k n", p=128))
                    for i in range(8, 32):
                        cdma("wf", i, dst[i], src[i + 12])
                    src = W["pool_proj"][l].rearrange("(c p) (m n) -> m p c n", p=128, n=128)
                    dst = sc["pp", l].ap().rearrange("m p (c n) -> m p c n", n=128)
                    for m in range(8):
                        cdma("pp", m, dst[m], src[m])
                    src = W["conv_proj"][l].rearrange("(c p) (m n) -> m p c n", p=128, n=128)
                    dst = sc["cp", l].ap().rearrange("m p (c n) -> m p c n", n=128)
                    for m in range(8):
                        cdma("cp", m, dst[m], src[m])
                    src = W["attn_proj"][l].rearrange("(c p) (m n) -> m p c n", p=128, n=128)
                    dst = sc["ap", l].ap().rearrange("m p (c n) -> m p c n", n=128)
                    for m in range(8):
                        cdma("ap", m, dst[m], src[m])
                    src = W["w_out"][l].rearrange("(c p) (m n) -> m p c n", p=128, n=128)
                    dst = sc["wo", l].ap().rearrange("m p (c n) -> m p c n", n=128)
                    for m in range(8):
                        cdma("wo", m, dst[m], src[m])

        def layer_params(l):
            for i, nm in enumerate(("ffn1_norm", "mix_norm", "ffn2_norm")):
                dma("sp", gn[:, i, :], W[nm][l].rearrange("(k p) -> p k", p=128), [], ["gn"], "lp0", slow=True)
            dma("sp", qg[:], W["q_norm"][l].rearrange("h d -> (h d)").partition_broadcast(128), [], ["qg"], "lp1")
            dma("sp", kg[:], W["k_norm"][l].rearrange("h d -> (h d)").partition_broadcast(128), [], ["kg"], "lp2")
            dma("sp", pscale[:], W["pool_scale"][l].rearrange("(c p) -> p c", p=128), [], ["pscale"], "lp3", slow=True)
            for c in range(2):
                dma("sp", cw[:, c, :], W["conv_w"][l][:, c * 128:(c + 1) * 128].rearrange("j p -> p j"), [], ["cw"], "lp4", slow=True)
            ms("dve", pwf[:], 0.0, ["pwf"])
            for g in range(4):
                dma("sp", pwf[(g % 2) * 64:(g % 2) * 64 + 64, g // 2, (g % 2) * 64:(g % 2) * 64 + 64],
                    W["pool_w"][l][g], [], ["pwf"], "lp5")
            cp("dve", pwb[:], pwf[:], ["pwf"], ["pwb"])
            ts1("dve", kg[:], kg[:], 8.0, ALU.mult, ["kg"], ["kg"])

        def rmsnorm(n, gi):
            hk = ["h%d" % k for k in range(8)]
            for k in range(8):
                act(R[:, k, 0:n], hT[:, k, 0:n], AF.Square, [hk[k]], [("R", k)])
            for k in range(8):
                mm(bank(6)[:, 0:n], Ob[:], R[:, k, 0:n], k == 0, k == 7, [("R", k), "Ob"], [bkey(6)])
            act(rstd[:, 0:n], bank(6)[:, 0:n], AF.Ln, [bkey(6)], ["rstd"], bias=EPS, scale=1.0)
            act(rstd[:, 0:n], rstd[:, 0:n], AF.Exp, ["rstd"], ["rstd"], scale=-0.5)
            for k in range(8):
                stt("dve", uB[:, k, 0:n], hT[:, k, 0:n], gn[:, gi, k:k + 1], rstd[:, 0:n], ALU.mult, ALU.mult,
                    [hk[k], "gn", "rstd"], [("u", k)])

        cnt = {"gu": 0, "big": 0, "wf": 0, "vs": 0, "ks": 0, "z": 0, "q3": 0, "pair": 0}

        def ffn(l, f, n):
            ukeys = [("u", k) for k in range(8)]
            for c in range(NFF):
                s = cnt["gu"] % 3
                cnt["gu"] += 1
                fine = (l == 0 and f == "1")
                dma("sp", gu[s][:, 0, :], sc["g" + f, l].ap()[c], grp["g" + f, l], [("gu", s, 0)], ("gu", s, 0))
                dma("sp", gu[s][:, 1, :], sc["u" + f, l].ap()[c], grp["u" + f, l], [("gu", s, 1)], ("gu", s, 1))
                bg, bu = (c % 2), 2 + (c % 2)
                for k in range(8):
                    mm(bank(bg)[:, 0:n], gu[s][:, 0, k * 128:(k + 1) * 128], uB[:, k, 0:n], k == 0, k == 7,
                       [("gu", s, 0), ukeys[k]], [bkey(bg)])
                for k in range(8):
                    mm(bank(bu)[:, 0:n], gu[s][:, 1, k * 128:(k + 1) * 128], uB[:, k, 0:n], k == 0, k == 7,
                       [("gu", s, 1), ukeys[k]], [bkey(bu)])
                sg = sig[c % 2]
                act(sg[:, 0:n], bank(bg)[:, 0:n], AF.Silu, [bkey(bg)], [("sig", c % 2)])
                tt("dve", R[:, c, 0:n], sg[:, 0:n], bank(bu)[:, 0:n], ALU.mult, [("sig", c % 2), bkey(bu)], [("R", c)])
            for m in range(8):
                s = cnt["big"] % 2
                cnt["big"] += 1
                dma("sp", bigs[s][:, :], sc["d" + f, l].ap()[m], grp["d" + f, l], [("big", s)], ("big", s))
                bd = 4 + (m % 2)
                for c in range(NFF):
                    mm(bank(bd)[:, 0:n], bigs[s][:, c * 128:(c + 1) * 128], R[:, c, 0:n], c == 0, c == NFF - 1,
                       [("big", s), ("R", c)], [bkey(bd)])
                stt("dve", hT[:, m, 0:n], bank(bd)[:, 0:n], 0.5, hT[:, m, 0:n], ALU.mult, ALU.add,
                    [bkey(bd), "h%d" % m], ["h%d" % m])

        ZP = [P[0], P[1]]
        ACC = P[2]
        OP = P[3]
        ATT = {}

        def att_views():
            f = R[:, :, :].bitcast(F32).rearrange("p a b -> p (a b)")
            b = R[:, :, :].rearrange("p a b -> p (a b)")
            ATT["SP"] = [f[:, 0:1024], f[:, 1024:2048]]
            ATT["T1"] = [f[:, 2048 + 1024 * i:3072 + 1024 * i] for i in range(3)]
            ATT["SPb"] = [b[:, 10240 + 1024 * i:11264 + 1024 * i] for i in range(3)]
            ATT["A"] = [b[:, 13312 + 1024 * i:14336 + 1024 * i] for i in range(3)]
        att_views()
        RALL = [("R", c) for c in range(32)]
        ZK = [[bkey(0), bkey(1)], [bkey(2), bkey(3)]]
        AK = [bkey(4), bkey(5)]
        OK_ = [bkey(6), bkey(7)]
        AKEYS = [(nm, z) for nm in ("aSP", "aT1", "aSPb", "aSPbh", "aA") for z in range(3)]

        def att_barrier():
            S.add("pool", lambda e: e.memset(dummy[:], 0.0), reads=RALL + AKEYS, writes=RALL + AKEYS)

        def attention(qT, nq, kblocks, out_fn):
            W8 = 8 * nq
            nb = len(kblocks)
            base = cnt["z"]
            cnt["z"] += nb

            def bufs(i):
                g = base + i
                return g % 2, g % 3

            def s_qk_mm(i):
                kb = kblocks[i]; nk = kb["nk"]; z, z3 = bufs(i)
                Z = ZP[z]
                for h in range(8):
                    mm(Z[0:nk, h * nq:(h + 1) * nq], kb["kT"](h), qT(h), True, True, kb["kreads"] + ["QT"], ZK[z])

            def s_qk_act(i):
                kb = kblocks[i]; nk = kb["nk"]; z, z3 = bufs(i)
                Z = ZP[z]
                SPt = ATT["SP"][z]
                act(SPt[0:nk, 0:W8], Z[0:nk, 0:W8], AF.Exp, ZK[z], [("aSP", z)])
                act(SPt[0:nk, 0:W8], SPt[0:nk, 0:W8], AF.Ln, [("aSP", z)], [("aSP", z)], bias=1.0, scale=1.0)

            def s_cast(i):
                kb = kblocks[i]; nk = kb["nk"]; z, z3 = bufs(i)
                Z = ZP[z]; SPt = ATT["SP"][z]; T1t = ATT["T1"][z3]; SPbt = ATT["SPb"][z3]
                kS, kT1, kSb = ("aSP", z), ("aT1", z3), ("aSPb", z3)
                if kb["diag"]:
                    tt("pool", SPbt[0:nk, 0:W8].rearrange("p (h q) -> p h q", q=nq),
                       SPt[0:nk, 0:W8].rearrange("p (h q) -> p h q", q=nq),
                       Tf[0:nk, 0:nq].unsqueeze(1).broadcast_to([nk, 8, nq]), ALU.mult, [kS, "Tf"], [kSb, (kSb[0] + "h", kSb[1])])
                elif W8 >= 1024:
                    cp("pool", SPbt[0:nk, 0:W8 // 2], SPt[0:nk, 0:W8 // 2], [kS], [kSb])
                    cp("dve", SPbt[0:nk, W8 // 2:W8], SPt[0:nk, W8 // 2:W8], [kS], [(kSb[0] + "h", kSb[1])])
                else:
                    cp("dve", SPbt[0:nk, 0:W8], SPt[0:nk, 0:W8], [kS], [kSb])
                tt("dve", T1t[0:nk, 0:W8], Z[0:nk, 0:W8], SPt[0:nk, 0:W8], ALU.subtract, ZK[z] + [kS], [kT1])

            def s_u(i):
                kb = kblocks[i]; nk = kb["nk"]; z, z3 = bufs(i)
                T1t = ATT["T1"][z3]; SPbt = ATT["SPb"][z3]
                kT1, kSb = ("aT1", z3), ("aSPb", z3)
                for half in range(0, W8, 512):
                    w = min(512, W8 - half)
                    mm(ACC[0:nk, half:half + w], Ub[0:nk, 0:nk], SPbt[0:nk, half:half + w], i == 0, True,
                       [kSb, ("aSPbh", z3), "Ub"], AK)
                tt("dve", T1t[0:nk, 0:W8], T1t[0:nk, 0:W8], ACC[0:nk, 0:W8], ALU.subtract, [kT1] + AK, [kT1])
                kb["vload"](z3)

            def s_lw(i):
                kb = kblocks[i]; nk = kb["nk"]; z, z3 = bufs(i)
                SPbt = ATT["SPb"][z3]; kSb = ("aSPb", z3)
                if i < nb - 1:
                    nk2 = kblocks[i + 1]["nk"]
                    for half in range(0, W8, 512):
                        w = min(512, W8 - half)
                        if nk2 == nk:
                            mm(ACC[0:nk, half:half + w], Lb[0:nk, 0:nk], SPbt[0:nk, half:half + w], False, True, [kSb, ("aSPbh", z3), "Lb"], AK)
                        else:
                            mm(ACC[0:nk2, half:half + w], O1[0:nk, 0:nk2], SPbt[0:nk, half:half + w], True, True, [kSb, ("aSPbh", z3), "O1"], AK)

            def s_expa(i):
                kb = kblocks[i]; nk = kb["nk"]; z, z3 = bufs(i)
                T1t = ATT["T1"][z3]; At = ATT["A"][z3]
                kT1, kA = ("aT1", z3), ("aA", z3)
                act(At[0:nk, 0:W8], T1t[0:nk, 0:W8], AF.Exp, [kT1], [kA])
                if kb["diag"]:
                    tt("pool", At[0:nk, 0:W8].rearrange("p (h q) -> p h q", q=nq),
                       At[0:nk, 0:W8].rearrange("p (h q) -> p h q", q=nq),
                       Tb[0:nk, 0:nq].unsqueeze(1).broadcast_to([nk, 8, nq]), ALU.mult, [kA, "Tb"], [kA])

            def s_av(i):
                kb = kblocks[i]; nk = kb["nk"]; z, z3 = bufs(i)
                At = ATT["A"][z3]; kA = ("aA", z3)
                for h in range(8):
                    mm(OP[:, h * nq:(h + 1) * nq], vsl[z3][0:nk, (h // 2) * 128:(h // 2) * 128 + 128],
                       At[0:nk, h * nq:(h + 1) * nq], i == 0 and (h * nq) % 512 == 0, i == nb - 1,
                       [kA, ("vs", z3)], OK_)

            for it in range(nb + 4):
                if it < nb:
                    s_qk_mm(it)
                if 0 <= it - 3 < nb:
                    s_lw(it - 3)
                if 0 <= it - 4 < nb:
                    s_av(it - 4)
                if 0 <= it - 3 < nb:
                    s_expa(it - 3)
                if it < nb:
                    s_qk_act(it)
                if 0 <= it - 1 < nb:
                    s_cast(it - 1)
                if 0 <= it - 2 < nb:
                    s_u(it - 2)
            out_fn()

        def layer_tile(l, ti):
            s0, n = TILES[ti]
            tail = (ti == 8)
            hk = ["h%d" % k for k in range(8)]
            nblk = [(0, 128), (128, 128), (256, 128), (384, 128)] if not tail else [(0, 80)]
            if ti == 0:
                ms("dve", exta[:], 0.0, ["exta"])
                ms("dve", extc[:], 0.0, ["extc"])
            if STOP < 1:
                return
            if l == 0:
                for (bo, bn) in nblk:
                    for half in range(2):
                        dma("sp", xl[0:bn, half * 512:(half + 1) * 512], xin[s0 + bo:s0 + bo + bn, half * 512:(half + 1) * 512],
                            [], ["xl%d" % half], "xl%d" % half)
                    for half in range(2):
                        bk = 6 + half
                        for j in range(4):
                            k = half * 4 + j
                            tr(bank(bk)[:, j * 128:j * 128 + bn], xl[0:bn, k * 128:(k + 1) * 128], If[0:bn, 0:bn],
                               ["xl%d" % half, "If"], [bkey(bk)])
                        for j in range(4):
                            k = half * 4 + j
                            cp("act" if half else "dve", hT[:, k, bo:bo + bn], bank(bk)[:, j * 128:j * 128 + bn],
                               [bkey(bk)], [hk[k]])
            else:
                for k in range(8):
                    dma("sp", hT[:, k, 0:n], hT_s.ap()[:, k, s0:s0 + n], [("hTs", ti)], [hk[k]], ("hld", k))
            if STOP < 2:
                return
            rmsnorm(n, 0)
            if STOP < 3:
                return
            ffn(l, "1", n)
            if STOP < 4:
                return
            rmsnorm(n, 1)
            ukeys = [("u", k) for k in range(8)]
            Rflat = R[:, :, :].rearrange("p a b -> p (a b)")
            for k in range(8):
                dma("sp", Rflat[:, k * 1536:(k + 1) * 1536], sc["qkv", l].ap()[:, k * 1536:(k + 1) * 1536], grp["qkv", l],
                    RALL[3 * k:3 * k + 3], ("qkvw", k))

            def wf_load(idx):
                s = cnt["wf"] % 3
                cnt["wf"] += 1
                dma("sp", wfs[s][:, :], sc["wf", l].ap()[idx], grp["wf", l], [("wf", s)], ("wf", s))
                return s

            def fm_proj(idx, bk):
                s = wf_load(idx)
                for k in range(8):
                    mm(bank(bk)[:, 0:n], wfs[s][:, k * 128:(k + 1) * 128], uB[:, k, 0:n], k == 0, k == 7,
                       [("wf", s), ukeys[k]], [bkey(bk)])

            if not tail:
                segs = [(15, 0, n)]
                W_a = 15 + n
            else:
                segs = [(31 * j + 15, 16 * j, 16) for j in range(5)]
                W_a = 155
            if tail:
                for c in range(2):
                    cp("dve", Sb[0][:, 0:15], exta[:, c, 512:527], ["exta"], ["Sb0"])
                    cp("dve", exta[:, c, 0:15], Sb[0][:, 0:15], ["Sb0"], ["exta"])
                    cp("dve", Sb[0][:, 0:2], extc[:, c, 512:514], ["extc"], ["Sb0"])
                    cp("dve", extc[:, c, 0:2], Sb[0][:, 0:2], ["Sb0"], ["extc"])
                for j in range(1, 5):
                    for c in range(2):
                        dma("sp", exta[:, c, 31 * j:31 * j + 15],
                            spool[l, j - 1][:, c * 128:(c + 1) * 128].rearrange("t p -> p t"),
                            [], ["exta"], "haloA", slow=True)
                        dma("sp", extc[:, c, 18 * j:18 * j + 2],
                            sconv[l, j - 1][:, c * 128:(c + 1) * 128].rearrange("t p -> p t"),
                            [], ["extc"], "haloC", slow=True)
            csegs = [(2, 0, n)] if not tail else [(18 * j + 2, 16 * j, 16) for j in range(5)]
            W_c = 2 + n if not tail else 90
            for c in range(2):
                fm_proj(c, c)
                for (eo, to, sn) in segs:
                    cp("act", exta[:, c, eo:eo + sn], bank(c)[:, to:to + sn], [bkey(c)], ["exta"])
            for c in range(2):
                fm_proj(2 + c, 2)
                cp("act", xbt[:, 0:n], bank(2)[:, 0:n], [bkey(2)], ["xbt"])
                fm_proj(4 + c, 3)
                cp("act", gbt[:, c, 0:n], bank(3)[:, 0:n], [bkey(3)], ["gbt"])
                fm_proj(6 + c, 4)
                for (eo, to, sn) in csegs:
                    tt("dve", extc[:, c, eo:eo + sn], bank(4)[:, to:to + sn], xbt[:, to:to + sn], ALU.mult,
                       [bkey(4), "xbt"], ["extc"])
            if ti == 7 or tail:
                for c in range(2):
                    if tail:
                        lst = [(j, 31 * j + 16, 18 * j + 16) for j in range(5)]
                    else:
                        lst = []
                    for (j, ao, co_) in lst:
                        tr(bank(7)[0:15, 0:128], exta[:, c, ao:ao + 15], If[:, :], ["exta", "If"], [bkey(7)])
                        cp("dve", cvt[0:15, 0:128], bank(7)[0:15, 0:128], [bkey(7)], ["cvt"])
                        dma("sp", p_o[l, j][:, c * 128:(c + 1) * 128], cvt[0:15, 0:128], ["cvt"], [], "po")
                        tr(bank(7)[0:2, 128:256], extc[:, c, co_:co_ + 2], If[:, :], ["extc", "If"], [bkey(7)])
                        cp("dve", cvt[0:2, 128:256], bank(7)[0:2, 128:256], [bkey(7)], ["cvt"])
                        dma("sp", c_o[l, j][:, c * 128:(c + 1) * 128], cvt[0:2, 128:256], ["cvt"], [], "co")
            Rw = R[:, 0:24, :].rearrange("p (k r) b -> p k (r b)", r=3)
            blk_banks = {}

            def qkv_mm(bi):
                bo, bn = nblk[bi]
                g3 = cnt["q3"] % 2
                cnt["q3"] += 1
                b3 = [3 * g3, 3 * g3 + 1, 3 * g3 + 2]
                blk_banks[bi] = b3
                for j in range(3):
                    for k in range(8):
                        mm(bank(b3[j])[0:bn, :], uB[:, k, bo:bo + bn], Rw[:, k, j * 512:(j + 1) * 512], k == 0, k == 7,
                           RALL[3 * k:3 * k + 3] + [ukeys[k]], [bkey(b3[j])])

            def qkv_chain(bi):
                bo, bn = nblk[bi]
                b3 = blk_banks[bi]
                r0 = s0 + bo
                QK = (0, 1)
                src = [bank(b3[j])[0:bn, :] for j in QK]
                tsq = [tsq2[j] for j in QK]
                ssq = [ssq82[j] for j in QK]
                tkb = [tkb2[j] for j in QK]
                ktsq = [("tsq", 0), "xbt"]
                kssq = [("ssq8", 0), ("ssq8", 1)]
                ktkb = [("tkb", 0), "pin"]
                kf = [tkf[bi % 2], tkf[(bi + 1) % 2]]
                kfk = [("tkf", bi % 2), ("tkf", (bi + 1) % 2)]
                gains = [(qg, "qg"), (kg, "kg")]
                for j in QK:
                    act(tsq[j][0:bn, :], src[j], AF.Square, [bkey(b3[j])], [ktsq[j]])
                vf = tvf[bi % 2]
                vb = tvb[bi % 2]
                cp("act", vf[0:bn, :], bank(b3[2])[0:bn, :], [bkey(b3[2])], [("tvf", bi % 2)])
                for j in QK:
                    S.add("dve", lambda e, bn=bn, t=tsq[j], q=ssq[j]: e.tensor_reduce(
                        out=q[0:bn, :], in_=t[0:bn, :].rearrange("p (h d) -> p h d", d=64), axis=AX.X, op=ALU.add),
                        reads=[ktsq[j]], writes=[kssq[j]])
                for j in QK:
                    act(ssq[j][0:bn, :], ssq[j][0:bn, :], AF.Ln, [kssq[j]], [kssq[j]], bias=64.0 * EPS, scale=1.0)
                    act(ssq[j][0:bn, :], ssq[j][0:bn, :], AF.Exp, [kssq[j]], [kssq[j]], scale=-0.5)
                for j in QK:
                    tt("dve", kf[j][0:bn, :].rearrange("p (h d) -> p h d", d=64), src[j].rearrange("p (h d) -> p h d", d=64),
                       ssq[j][0:bn, :].unsqueeze(2).broadcast_to([bn, 8, 64]), ALU.mult, [bkey(b3[j]), kssq[j]], [kfk[j]])
                tt("dve", tkb[0][0:bn, :], kf[0][0:bn, :], qg[0:bn, :], ALU.mult, [kfk[0], "qg"], [ktkb[0]])
                tt("dve", kf[1][0:bn, :], kf[1][0:bn, :], kg[0:bn, :], ALU.mult, [kfk[1], "kg"], [kfk[1]])
                cp("pool", tkb[1][0:bn, :], kf[1][0:bn, :], [kfk[1]], [ktkb[1]])
                dma("sp", k_o[l, r0:r0 + bn, :], kf[1][0:bn, :], [kfk[1]], [], ("ko", (bi + 1) % 2))
                cp("pool", vb[0:bn, :], vf[0:bn, :], [("tvf", bi % 2)], [("tvb", bi % 2)])
                dma("sp", v_o[l, r0:r0 + bn, :], vf[0:bn, :], [("tvf", bi % 2)], [], ("vo", bi % 2))
                dma("sp", vb_s.ap()[r0:r0 + bn, :], vb[0:bn, :], [("tvb", bi % 2)], [("vbs", r0 // 128)], ("vbs", bi % 2))
                for j in QK:
                    bt = 7 - j
                    for c in range(4):
                        tr(bank(bt)[:, c * 128:c * 128 + bn], tkb[j][0:bn, c * 128:(c + 1) * 128], Ib[0:bn, 0:bn],
                           [ktkb[j], "Ib"], [bkey(bt)])
                pq = bank(7).rearrange("p (c t) -> p c t", t=128)
                pk = bank(6).rearrange("p (c t) -> p c t", t=128)
                cp("act", QT[0:64, :, bo:bo + bn], pq[0:64, :, 0:bn], [bkey(7)], ["QT"])
                cp("act", QTo[64:128, :, bo:bo + bn], pq[64:128, :, 0:bn], [bkey(7)], ["QT"])
                if not tail:
                    cp("act", KT[:, :, r0:r0 + bn], pk[:, :, 0:bn], [bkey(6)], [("KT", r0 // 128)])
                else:
                    cp("act", KTt[:, :, 0:bn], pk[:, :, 0:bn], [bkey(6)], ["KTt"])
                    cp("act", KT[:, :, r0:r0 + 16], pk[:, :, 0:16], [bkey(6)], [("KT", 32)])

            qkv_mm(0)
            for c in range(2):
                e = exta[:, c, :]
                tt("dve", Sb[0][:, 1:W_a], e[:, 1:W_a], e[:, 0:W_a - 1], ALU.add, ["exta"], ["Sb0"])
                tt("dve", Sb[1][:, 3:W_a], Sb[0][:, 3:W_a], Sb[0][:, 1:W_a - 2], ALU.add, ["Sb0"], ["Sb1"])
                if c == 0:
                    lo, hi = Sb[0], Sb[1]
                    klo, khi = "Sb0", "Sb1"
                else:
                    tt("dve", Sb[0][:, 7:W_a], Sb[1][:, 7:W_a], Sb[1][:, 3:W_a - 4], ALU.add, ["Sb1"], ["Sb0"])
                    tt("dve", Sb[1][:, 15:W_a], Sb[0][:, 15:W_a], Sb[0][:, 7:W_a - 8], ALU.add, ["Sb0"], ["Sb1"])
                    lo, hi = Sb[0], Sb[1]
                    klo, khi = "Sb0", "Sb1"
                for (eo, to, sn) in segs:
                    stt("dve", pin[0:64, c, to:to + sn], lo[0:64, eo:eo + sn], invw[0:64, c:c + 1],
                        e[0:64, eo:eo + sn], ALU.mult, ALU.subtract, [klo, "invw", "exta"], ["pin"])
                    stt("dve", pin[64:128, c, to:to + sn], hi[64:128, eo:eo + sn], invw[64:128, c:c + 1],
                        e[64:128, eo:eo + sn], ALU.mult, ALU.subtract, [khi, "invw", "exta"], ["pin"])
                if ti == 0:
                    tt("dve", cvt[0:64, 0:16], lo[0:64, 15:31], invc[0:64, c, :], ALU.mult, [klo, "invc"], ["cvt"])
                    tt("dve", pin[0:64, c, 0:16], cvt[0:64, 0:16], e[0:64, 15:31], ALU.subtract, ["cvt", "exta"], ["pin"])
                    tt("dve", cvt[64:128, 0:16], hi[64:128, 15:31], invc[64:128, c, :], ALU.mult, [khi, "invc"], ["cvt"])
                    tt("dve", pin[64:128, c, 0:16], cvt[64:128, 0:16], e[64:128, 15:31], ALU.subtract, ["cvt", "exta"], ["pin"])
            for c in range(2):
                mm(bank(7)[:, 0:n], pwb[:, c, :], pin[:, c, 0:n], True, True, ["pwb", "pin"], [bkey(7)])
                ts1("dve", yain[:, c, 0:n], bank(7)[:, 0:n], pscale[:, c:c + 1], ALU.mult, [bkey(7), "pscale"], ["yain"])
            for c in range(2):
                x_ = extc[:, c, :]
                for (eo, to, sn) in csegs:
                    ts1("dve", cvt[:, to:to + sn], x_[:, eo - 2:eo - 2 + sn], cw[:, c, 0:1], ALU.mult, ["extc", "cw"], ["cvt"])
                    stt("dve", cvt[:, to:to + sn], x_[:, eo - 1:eo - 1 + sn], cw[:, c, 1:2], cvt[:, to:to + sn],
                        ALU.mult, ALU.add, ["extc", "cw", "cvt"], ["cvt"])
                    stt("dve", cvt[:, to:to + sn], x_[:, eo:eo + sn], cw[:, c, 2:3], cvt[:, to:to + sn],
                        ALU.mult, ALU.add, ["extc", "cw", "cvt"], ["cvt"])
                tt("dve", ybin[:, c, 0:n], cvt[:, 0:n], gbt[:, c, 0:n], ALU.mult, ["cvt", "gbt"], ["ybin"])
            if not tail:
                for c in range(2):
                    cp("dve", Sb[0][:, 0:15], exta[:, c, 512:527], ["exta"], ["Sb0"])
                    cp("dve", exta[:, c, 0:15], Sb[0][:, 0:15], ["Sb0"], ["exta"])
                    cp("dve", Sb[0][:, 0:2], extc[:, c, 512:514], ["extc"], ["Sb0"])
                    cp("dve", extc[:, c, 0:2], Sb[0][:, 0:2], ["Sb0"], ["extc"])
            if STOP < 5:
                return
            for bi in range(len(nblk)):
                if bi + 1 < len(nblk):
                    qkv_mm(bi + 1)
                qkv_chain(bi)
            if STOP < 7:
                return
            def qT_fn(off, nq):
                return lambda h: (QTo if h % 2 else QT)[:, h // 2, off:off + nq]

            def kblock_prompt(jb, nk, diag):
                def vload(s):
                    dma("sp", vsl[s][0:nk, :], vb_s.ap()[jb * 128:jb * 128 + nk, :], [("vbs", jb)], [("vs", s)], ("vs", s))
                return {"nk": nk, "diag": diag, "vload": vload, "kreads": [("KT", jb)],
                        "kT": lambda h: KT[:, h // 2, jb * 128:jb * 128 + nk]}

            def out_fn_mk(off, nq):
                def f():
                    for p in range(4):
                        cp("dve", OT[0:64, p, off:off + nq], OP[0:64, (2 * p) * nq:(2 * p) * nq + nq], OK_, ["OT"])
                        cp("act", OT[64:128, p, off:off + nq], OP[64:128, (2 * p + 1) * nq:(2 * p + 1) * nq + nq], OK_, ["OT"])
                return f

            att_barrier()
            if not tail:
                for qb in range(4):
                    gi_ = s0 // 128 + qb
                    kbl = [kblock_prompt(gi_, 128, True)] + [kblock_prompt(j, 128, False) for j in range(gi_ - 1, -1, -1)]
                    attention(qT_fn(qb * 128, 128), 128, kbl, out_fn_mk(qb * 128, 128))
            else:
                kbl = [kblock_prompt(32, 16, True)] + [kblock_prompt(j, 128, False) for j in range(31, -1, -1)]
                attention(qT_fn(0, 16), 16, kbl, out_fn_mk(0, 16))
                for si in range(NS):
                    for jb in range(8):
                        s = cnt["ks"] % 2
                        cnt["ks"] += 1
                        dma("pool", ksl[s][:, :], ck[l, si, jb * 128:(jb + 1) * 128, :], [], [("ks", s)], ("ks", s))
                        pb = bank(7)
                        for c in range(4):
                            tr(pb[:, c * 128:(c + 1) * 128], ksl[s][:, c * 128:(c + 1) * 128], Ib[:, :],
                               [("ks", s), "Ib"], [bkey(7)])
                        for c in range(4):
                            cp("act" if c % 2 else "dve", KTs[:, c, jb * 128:(jb + 1) * 128], pb[:, c * 128:(c + 1) * 128],
                               [bkey(7)], ["KTs"])

                    def kb_cache(jb, si=si):
                        def vload(s):
                            dma("pool", vsl[s][:, :], cv[l, si, jb * 128:(jb + 1) * 128, :], [], [("vs", s)], ("vs", s))
                        return {"nk": 128, "diag": False, "vload": vload, "kreads": ["KTs"],
                                "kT": lambda h: KTs[:, h // 2, jb * 128:(jb + 1) * 128]}

                    def kb_new(si=si):
                        r = TP + 16 * si
                        def vload(s):
                            dma("sp", vsl[s][0:16, :], vb_s.ap()[r:r + 16, :], [("vbs", 32)], [("vs", s)], ("vs", s))
                        return {"nk": 16, "diag": True, "vload": vload, "kreads": ["KTt"],
                                "kT": lambda h: KTt[:, h // 2, 16 + 16 * si:32 + 16 * si]}
                    kbl = [kb_new()] + [kb_cache(j) for j in range(7, -1, -1)]
                    attention(qT_fn(16 + 16 * si, 16), 16, kbl, out_fn_mk(16 + 16 * si, 16))
            att_barrier()
            if STOP < 8:
                return
            for m in range(8):
                ms("pool", macc[:, 0:n], 0.0, ["macc"])
                for br in range(3):
                    pr = cnt["pair"] % 3
                    cnt["pair"] += 1
                    by, bg = 2 * pr, 2 * pr + 1
                    s = cnt["big"] % 2
                    cnt["big"] += 1
                    if br == 0:
                        dma("sp", bigs[s][:, 0:256], sc["pp", l].ap()[m], grp["pp", l], [("big", s)], ("big", s))
                        for c in range(2):
                            mm(bank(by)[:, 0:n], bigs[s][:, c * 128:(c + 1) * 128], yain[:, c, 0:n], c == 0, c == 1,
                               [("big", s), "yain"], [bkey(by)])
                    elif br == 1:
                        dma("sp", bigs[s][:, 0:256], sc["cp", l].ap()[m], grp["cp", l], [("big", s)], ("big", s))
                        for c in range(2):
                            mm(bank(by)[:, 0:n], bigs[s][:, c * 128:(c + 1) * 128], ybin[:, c, 0:n], c == 0, c == 1,
                               [("big", s), "ybin"], [bkey(by)])
                    else:
                        dma("sp", bigs[s][:, 0:512], sc["ap", l].ap()[m], grp["ap", l], [("big", s)], ("big", s))
                        for c in range(4):
                            mm(bank(by)[:, 0:n], bigs[s][:, c * 128:(c + 1) * 128], OT[:, c, 0:n], c == 0, c == 3,
                               [("big", s), "OT"], [bkey(by)])
                    fm_proj(8 + br * 8 + m, bg)
                    sg = sig[br % 2]
                    act(sg[:, 0:n], bank(bg)[:, 0:n], AF.Sigmoid, [bkey(bg)], [("sig", br % 2)])
                    tt("dve", mtmp[:, 0:n], sg[:, 0:n], bank(by)[:, 0:n], ALU.mult, [("sig", br % 2), bkey(by)], ["mtmp"])
                    if br < 2:
                        tt("pool", macc[:, 0:n], macc[:, 0:n], mtmp[:, 0:n], ALU.add, ["macc", "mtmp"], ["macc"])
                    else:
                        tt("pool", R[:, 8 + m, 0:n], macc[:, 0:n], mtmp[:, 0:n], ALU.add, ["macc", "mtmp"], [("R", 8 + m)])
            for m2 in range(8):
                s = cnt["big"] % 2
                cnt["big"] += 1
                dma("sp", bigs[s][:, 0:1024], sc["wo", l].ap()[m2], grp["wo", l], [("big", s)], ("big", s))
                bd = 6 + (m2 % 2)
                for m in range(8):
                    mm(bank(bd)[:, 0:n], bigs[s][:, m * 128:(m + 1) * 128], R[:, 8 + m, 0:n], m == 0, m == 7,
                       [("big", s), ("R", 8 + m)], [bkey(bd)])
                tt("dve", hT[:, m2, 0:n], hT[:, m2, 0:n], bank(bd)[:, 0:n], ALU.add, [hk[m2], bkey(bd)], [hk[m2]])
            rmsnorm(n, 2)
            ffn(l, "2", n)
            if l < depth - 1:
                for k in range(8):
                    dma("sp", hT_s.ap()[:, k, s0:s0 + n], hT[:, k, 0:n], [hk[k]], [("hTs", ti)], ("hst", k))
            else:
                for (bo, bn) in nblk:
                    for half in range(2):
                        bk = 4 + half
                        for j in range(4):
                            k = half * 4 + j
                            tr(bank(bk)[0:bn, j * 128:(j + 1) * 128], hT[:, k, bo:bo + bn], If[:, :], [hk[k], "If"], [bkey(bk)])
                        cp("act" if half else "dve", xl[0:bn, half * 512:(half + 1) * 512], bank(bk)[0:bn, :], [bkey(bk)],
                           ["xl%d" % half])
                        dma("sp", y_o[s0 + bo:s0 + bo + bn, half * 512:(half + 1) * 512], xl[0:bn, half * 512:(half + 1) * 512],
                            ["xl%d" % half], [], "yo%d" % half)

        if PROBE:
            dma("pool", QT[0:64, :, 0:128], pq[0:64, :, :], ["QT"], ["QT"], "pq")
            dma("pool", QTo[64:128, :, 0:128], pq[64:128, :, :], ["QT"], ["QT"], "pq")
            dma("pool", KT[:, :, 0:384], pk[:, :, :], [], [("KT", 0), ("KT", 1), ("KT", 2)], "pk")
            dma("pool", vb_s.ap()[0:384, :], pv[:, :], [], [("vbs", 0), ("vbs", 1), ("vbs", 2)], "pv")

            def qT_fn(off, nq):
                return lambda h: (QTo if h % 2 else QT)[:, h // 2, off:off + nq]

            def kblock_prompt(jb, nk, diag):
                def vload(s):
                    dma("sp", vsl[s][0:nk, :], vb_s.ap()[jb * 128:jb * 128 + nk, :], [("vbs", jb)], [("vs", s)], ("vs", s))
                return {"nk": nk, "diag": diag, "vload": vload, "kreads": [("KT", jb)],
                        "kT": lambda h: KT[:, h // 2, jb * 128:jb * 128 + nk]}

            def out_fn():
                for p in range(4):
                    cp("dve", OT[0:64, p, 0:128], OP[0:64, (2 * p) * 128:(2 * p) * 128 + 128], OK_, ["OT"])
                    cp("act", OT[64:128, p, 0:128], OP[64:128, (2 * p + 1) * 128:(2 * p + 1) * 128 + 128], OK_, ["OT"])
            att_barrier()
            attention(qT_fn(0, 128), 128, [kblock_prompt(2, 128, True), kblock_prompt(1, 128, False), kblock_prompt(0, 128, False)], out_fn)
            att_barrier()
            for ii, (nm, z) in enumerate((("SP", 0), ("SP", 1), ("T1", 0), ("T1", 1))):
                dma("sp", pdbg[ii], ATT[nm][z][:, 0:128], RALL, [], "pdbg")
            for p in range(4):
                cp("dve", xl[:, p * 128:(p + 1) * 128], OT[:, p, 0:128], ["OT"], ["xl"])
            dma("sp", pout[:, :, :], xl[:, 0:512].rearrange("p (a b) -> p a b", b=128), ["xl"], [], "pout")
            S.emit(nc, st)
            return nc
        for l in range(depth):
            if os.environ.get("MK_CONV", "1") == "1":
                convert(l)
        for l in range(min(depth, NLAYERS)):
            if os.environ.get("MK_LP", "1") == "1":
                layer_params(l)
            tl = list(range(len(TILES)))
            if NTILES < 9:
                tl = tl[:NTILES - 1] + [8] if NTILES > 1 else [0]
            for ti in tl:
                layer_tile(l, ti)
        S.emit(nc, st)
    return nc


_CACHE = {}


def _consts():
    j = np.arange(128)
    U = (j[:, None] > j[None, :]).astype(np.float32)
    L = (j[:, None] <= j[None, :]).astype(np.float32)
    I = np.eye(128, dtype=np.float32)
    T = (j[:, None] < j[None, :]).astype(np.float32)
    T8 = np.tile(T, (1, 8))
    ones = np.ones((128, 128), np.float32)
    wins = np.array([[2, 8], [4, 16]], np.float32)
    invw = np.zeros((128, 2), np.float32)
    invc = np.zeros((128, 2, 16), np.float32)
    pos = np.arange(16, dtype=np.float32)
    for half in range(2):
        for c in range(2):
            w = wins[half, c]
            invw[half * 64:(half + 1) * 64, c] = 1.0 / w
            invc[half * 64:(half + 1) * 64, c, :] = 1.0 / np.minimum(pos + 1.0, w)
    return {"cU": U, "cL": L, "cI": I, "cT": T8, "cOnes": ones, "cInvw": invw, "cInvc": invc}


def kernel(**inputs):
    f32 = lambda a: np.ascontiguousarray(np.asarray(a, dtype=np.float32))
    inp = {k: f32(v) for k, v in inputs.items()}
    if "nc" not in _CACHE:
        _CACHE["nc"] = build()
    nc = _CACHE["nc"]
    consts = _consts()
    in_maps = []
    for c in range(8):
        sq = c % 4
        xin = np.concatenate([inp["meta"], inp["x_prompt"][sq], inp["x_sample"][4 * c:4 * c + 4].reshape(64, D)], axis=0)
        m = {"xin": np.ascontiguousarray(xin),
             "ck": np.ascontiguousarray(inp["cache_k"][:, 4 * c:4 * c + 4].reshape(2, NS, PAST, 512)),
             "cv": np.ascontiguousarray(inp["cache_v"][:, 4 * c:4 * c + 4].reshape(2, NS, PAST, 512)),
             "spool": np.ascontiguousarray(inp["state_pool"][:, 4 * c:4 * c + 4]),
             "sconv": np.ascontiguousarray(inp["state_conv"][:, 4 * c:4 * c + 4])}
        for n in WNAMES:
            m[n] = inp[n]
        m.update(consts)
        in_maps.append(m)
    res = run_bass_kernel_spmd(nc, in_maps, core_ids=list(range(8)))
    r = res.results
    y_prompt = np.stack([r[b]["y"][NMETA:TP] for b in range(4)]).reshape(4, 4096, D)
    y_sample = np.concatenate([r[c]["y"][TP:NT].reshape(4, 16, D) for c in range(8)], axis=0)
    k_prompt = np.stack([np.stack([r[b]["ko"][l, 0:TP] for b in range(4)]) for l in range(2)]).reshape(2, 4, TP, 8, 64)
    v_prompt = np.stack([np.stack([r[b]["vo"][l, 0:TP] for b in range(4)]) for l in range(2)]).reshape(2, 4, TP, 8, 64)
    pool_prompt = np.stack([np.stack([r[b]["po"][l, 0] for b in range(4)]) for l in range(2)])
    conv_prompt = np.stack([np.stack([r[b]["co"][l, 0] for b in range(4)]) for l in range(2)])
    k_sample = np.stack([np.concatenate([r[c]["ko"][l, TP:NT].reshape(4, 16, 8, 64) for c in range(8)], axis=0) for l in range(2)])
    v_sample = np.stack([np.concatenate([r[c]["vo"][l, TP:NT].reshape(4, 16, 8, 64) for c in range(8)], axis=0) for l in range(2)])
    pool_sample = np.stack([np.concatenate([r[c]["po"][l, 1:5] for c in range(8)], axis=0) for l in range(2)])
    conv_sample = np.stack([np.concatenate([r[c]["co"][l, 1:5] for c in range(8)], axis=0) for l in range(2)])
    outs = (y_prompt, y_sample, k_prompt, v_prompt, pool_prompt, conv_prompt, k_sample, v_sample, pool_sample, conv_sample)
    return tuple(np.ascontiguousarray(o, dtype=np.float32) for o in outs)
```

```python
import numpy as np
from contextlib import ExitStack
import concourse.bass as bass
import concourse.mybir as mybir
from concourse.bass_utils import run_bass_kernel_spmd

F32 = mybir.dt.float32
BF16 = mybir.dt.bfloat16
AF = mybir.ActivationFunctionType
ALU = mybir.AluOpType
AX = mybir.AxisListType

D = 1024
DFF = 2816
NFF = 22
NMETA = 16
TP = 4112
NS = 4
TS = 16
NT = TP + NS * TS
PAST = 1024
DEPTH = 2
EPS = 1e-6
import os
STOP = int(os.environ.get("MK_STOP", "99"))
SUB = int(os.environ.get("MK_SUB", "99"))
ATL = int(os.environ.get("MK_ATT", "99"))
NTILES = int(os.environ.get("MK_TILES", "9"))
NLAYERS = int(os.environ.get("MK_LAYERS", "2"))
TILES = [(i * 512, 512) for i in range(8)] + [(4096, 80)]


class _Op:
    __slots__ = ("idx", "eng", "fn", "deps", "kind", "dkey", "sem", "val", "signal")


class Sched:
    ENGS = ("pe", "act", "dve", "pool", "sp")

    def __init__(self):
        self.ops = []
        self.lastw = {}
        self.readers = {}

    def add(self, eng, fn, reads=(), writes=(), kind="c", dkey=None):
        op = _Op()
        op.idx = len(self.ops)
        op.eng = eng
        op.fn = fn
        op.kind = kind
        op.dkey = dkey
        deps = set()
        excl = [k for k in reads if isinstance(k, tuple) and k[0] in ("bank", "b")]
        if excl:
            reads = [k for k in reads if k not in excl]
            writes = list(writes) + excl
        for k in reads:
            w = self.lastw.get(k)
            if w is not None:
                deps.add(w)
        for k in writes:
            w = self.lastw.get(k)
            if w is not None:
                deps.add(w)
            for r in self.readers.get(k, ()):
                deps.add(r)
        op.deps = deps
        for k in writes:
            self.lastw[k] = op.idx
            self.readers[k] = []
        for k in reads:
            self.readers.setdefault(k, []).append(op.idx)
        op.signal = kind != "c"
        op.sem = None
        op.val = 0
        self.ops.append(op)
        return op.idx

    def emit(self, nc, stack):
        ops = self.ops
        for op in ops:
            for d in op.deps:
                Dd = ops[d]
                if Dd.kind == "c" and Dd.eng == "pe" and op.eng == "pe" and op.kind == "c":
                    continue
                Dd.signal = True
        esem = {e: stack.enter_context(nc.semaphore("s_" + e)) for e in self.ENGS}
        dsem = {}
        cnt = {}
        for op in ops:
            if op.kind == "c":
                if op.signal:
                    cnt[op.eng] = cnt.get(op.eng, 0) + 1
                    op.sem = esem[op.eng]
                    op.val = cnt[op.eng]
            else:
                if op.dkey not in dsem:
                    dsem[op.dkey] = stack.enter_context(nc.semaphore("d_%d" % len(dsem)))
                cnt[("d", op.dkey)] = cnt.get(("d", op.dkey), 0) + 16
                op.sem = dsem[op.dkey]
                op.val = cnt[("d", op.dkey)]
        self.n_sems = len(dsem) + 5
        per = {e: [] for e in self.ENGS}
        for op in ops:
            per[op.eng].append(op)
        block = stack.enter_context(nc.Block())
        handles = {"pe": block.tensor, "act": block.scalar, "dve": block.vector,
                   "pool": block.gpsimd, "sp": block.sync}
        finals = {}
        for op in ops:
            if op.kind == "d":
                finals[id(op.sem)] = (op.sem, op.val)

        def mk(e):
            def body(eng):
                waited = {}
                for op in per[e]:
                    need = {}
                    for d in op.deps:
                        Dd = ops[d]
                        if Dd.kind == "c" and Dd.eng == "pe" and e == "pe" and op.kind == "c":
                            continue
                        k = id(Dd.sem)
                        if Dd.val > need.get(k, (None, 0))[1]:
                            need[k] = (Dd.sem, Dd.val)
                    for k, (s, v) in need.items():
                        if waited.get(k, 0) >= v:
                            continue
                        eng.wait_ge(s, v)
                        waited[k] = v
                    inst = op.fn(eng)
                    if op.signal:
                        inst.then_inc(op.sem, 1 if op.kind == "c" else 16)
                if e == "sp":
                    for s, v in finals.values():
                        eng.wait_ge(s, v)
            return body

        for e in self.ENGS:
            if per[e] or e == "sp":
                handles[e](mk(e))


WNAMES = ["ffn1_norm", "ffn1_w_gate", "ffn1_w_up", "ffn1_w_down", "mix_norm", "w_in", "pool_w",
          "pool_scale", "pool_proj", "conv_w", "conv_proj", "q_norm", "k_norm", "attn_proj",
          "w_out", "ffn2_norm", "ffn2_w_gate", "ffn2_w_up", "ffn2_w_down"]
WSHAPES = {
    "ffn1_norm": [2, D], "ffn1_w_gate": [2, D, DFF], "ffn1_w_up": [2, D, DFF], "ffn1_w_down": [2, DFF, D],
    "mix_norm": [2, D], "w_in": [2, D, 5632], "pool_w": [2, 4, 64, 64], "pool_scale": [2, 256],
    "pool_proj": [2, 256, D], "conv_w": [2, 3, 256], "conv_proj": [2, 256, D], "q_norm": [2, 8, 64],
    "k_norm": [2, 8, 64], "attn_proj": [2, 512, D], "w_out": [2, D, D], "ffn2_norm": [2, D],
    "ffn2_w_gate": [2, D, DFF], "ffn2_w_up": [2, D, DFF], "ffn2_w_down": [2, DFF, D],
}


def build(depth=DEPTH):
    nc = bass.Bass("TRN2", target_bir_lowering=False)
    S = Sched()

    def din(name, shape, dt=F32):
        return nc.dram_tensor(name, shape, dt, kind="ExternalInput").ap()

    def dout(name, shape):
        return nc.dram_tensor(name, shape, F32, kind="ExternalOutput").ap()

    PROBE = os.environ.get("MK_PROBE", "")
    if PROBE:
        pq = din("pq", [128, 4, 128])
        pk = din("pk", [128, 4, 384])
        pv = din("pv", [384, 512])
        pout = dout("pout", [128, 4, 128])
        pdbg = dout("pdbg", [4, 128, 128])
        xin = ck = cv = spool = sconv = None
        W = {}
    else:
        xin = din("xin", [NT, D])
        ck = din("ck", [2, NS, PAST, 512])
        cv = din("cv", [2, NS, PAST, 512])
        spool = din("spool", [2, NS, 15, 256])
        sconv = din("sconv", [2, NS, 2, 256])
        W = {n: din(n, WSHAPES[n]) for n in WNAMES}
    cU = din("cU", [128, 128])
    cL = din("cL", [128, 128])
    cI = din("cI", [128, 128])
    cT = din("cT", [128, 1024])
    cOnes = din("cOnes", [128, 128])
    cInvw = din("cInvw", [128, 2])
    cInvc = din("cInvc", [128, 2, 16])
    if not PROBE:
        y_o = dout("y", [NT, D])
        k_o = dout("ko", [2, NT, 512])
        v_o = dout("vo", [2, NT, 512])
        p_o = dout("po", [2, 5, 15, 256])
        c_o = dout("co", [2, 5, 2, 256])

    sc = {}
    for l in range(depth):
        for f in ("1", "2"):
            sc["g" + f, l] = nc.dram_tensor("sg%s_%d" % (f, l), [NFF, 128, 1024], BF16)
            sc["u" + f, l] = nc.dram_tensor("su%s_%d" % (f, l), [NFF, 128, 1024], BF16)
            sc["d" + f, l] = nc.dram_tensor("sd%s_%d" % (f, l), [8, 128, DFF], BF16)
        sc["wf", l] = nc.dram_tensor("swf_%d" % l, [32, 128, 1024], BF16)
        sc["qkv", l] = nc.dram_tensor("sqkv_%d" % l, [128, 8 * 1536], BF16)
        sc["wo", l] = nc.dram_tensor("swo_%d" % l, [8, 128, 1024], BF16)
        sc["ap", l] = nc.dram_tensor("sap_%d" % l, [8, 128, 512], BF16)
        sc["pp", l] = nc.dram_tensor("spp_%d" % l, [8, 128, 256], BF16)
        sc["cp", l] = nc.dram_tensor("scp_%d" % l, [8, 128, 256], BF16)
    hT_s = nc.dram_tensor("hT_s", [128, 8, NT], F32)
    vb_s = nc.dram_tensor("vb_s", [NT, 512], BF16)
    grp = {}

    with ExitStack() as st:
        def sb(name, shape, dt=F32):
            return st.enter_context(nc.sbuf_tensor(name, shape, dt))

        KT = sb("KT", [128, 4, 4224], BF16)
        KTs = sb("KTs", [128, 4, 1024], BF16)
        KTt = sb("KTt", [128, 4, 80], BF16)
        hT = sb("hT", [128, 8, 512])
        uB = sb("uB", [128, 8, 512], BF16)
        R = sb("R", [128, 32, 512], BF16)
        Rf = R[:, :, :].bitcast(F32)
        gu = [sb("gu%d" % i, [128, 2, 1024], BF16) for i in range(3)]
        bigs = [sb("big%d" % i, [128, DFF], BF16) for i in range(2)]
        wfs = [sb("wfs%d" % i, [128, 1024], BF16) for i in range(3)]
        exta = sb("exta", [128, 2, 160 + 512 - 145])
        extc = sb("extc", [128, 2, 514])
        Sb = [sb("Sb%d" % i, [128, 527]) for i in range(2)]
        gbt = sb("gbt", [128, 2, 512])
        xbt = sb("xbt", [128, 512])
        pin = sb("pin", [128, 2, 512], BF16)
        yain = sb("yain", [128, 2, 512], BF16)
        ybin = sb("ybin", [128, 2, 512], BF16)
        cvt = sb("cvt", [128, 512])
        tkf = [sb("tkf%d" % i, [128, 512]) for i in range(2)]
        tvf = [sb("tvf%d" % i, [128, 512]) for i in range(2)]
        tsq2 = [sb("tsq0", [128, 512]), xbt]
        tkb2 = [sb("tkb0", [128, 512], BF16), pin[:, 0, :]]
        tvb = [sb("tvb%d" % i, [128, 512], BF16) for i in range(2)]
        ssq82 = [sb("ssq8_%d" % i, [128, 8]) for i in range(2)]
        QT = sb("QT", [128, 4, 512], BF16)
        QTo = sb("QTo", [128, 4, 512], BF16)
        OT = sb("OT", [128, 4, 512], BF16)
        xl = sb("xl", [128, 1024])
        rstd = sb("rstd", [128, 512])
        sig = [sb("sig%d" % i, [128, 512]) for i in range(2)]
        mtmp = sb("mtmp", [128, 512])
        macc = sb("macc", [128, 512])
        vsl = [sb("vsl%d" % i, [128, 512], BF16) for i in range(3)]
        ksl = [sb("ksl%d" % i, [128, 512], BF16) for i in range(2)]
        If = sb("If", [128, 128])
        Ib = sb("Ib", [128, 128], BF16)
        Ub = sb("Ub", [128, 128], BF16)
        Lb = sb("Lb", [128, 128], BF16)
        Ob = sb("Ob", [128, 128], BF16)
        O1 = sb("O1", [128, 128], BF16)
        Tf = sb("Tf", [128, 128])
        Tb = sb("Tb", [128, 128], BF16)
        invw = sb("invw", [128, 2])
        invc = sb("invc", [128, 2, 16])
        gn = sb("gn", [128, 3, 8])
        qg = sb("qg", [128, 512])
        kg = sb("kg", [128, 512])
        pscale = sb("pscale", [128, 2])
        cw = sb("cw", [128, 2, 3])
        pwb = sb("pwb", [128, 2, 128], BF16)
        pwf = sb("pwf", [128, 2, 128])
        dummy = sb("dmy", [128, 8])
        P = [st.enter_context(nc.psum_tensor("P%d" % i, [128, 1024], F32)) for i in range(4)]

        def bank(i):
            return P[i // 2][:, (i % 2) * 512:(i % 2) * 512 + 512]

        def bkey(i):
            return ("bank", i)

        def dma(eng, out, in_, reads, writes, dkey, slow=False):
            if slow:
                S.add(eng, lambda e: e.dma_start(out=out, in_=in_, allow_slow_non_contiguous=True),
                      reads=reads, writes=writes, kind="d", dkey=dkey)
            else:
                S.add(eng, lambda e: e.dma_start(out=out, in_=in_), reads=reads, writes=writes, kind="d", dkey=dkey)

        def mm(out, lhsT, rhs, start, stop, reads, writes):
            S.add("pe", lambda e: e.matmul(out, lhsT=lhsT, rhs=rhs, start=start, stop=stop, skip_group_check=True),
                  reads=reads, writes=writes)

        def tr(out, in_, ident, reads, writes):
            S.add("pe", lambda e: e.matmul(out, lhsT=in_, rhs=ident, start=True, stop=True), reads=reads, writes=writes)

        def act(out, in_, func, reads, writes, bias=None, scale=None):
            kw = {}
            if bias is not None:
                kw["bias"] = bias
            if scale is not None:
                kw["scale"] = scale
            S.add("act", lambda e: e.activation(out=out, in_=in_, func=func, **kw), reads=reads, writes=writes)

        def tt(eng, out, a, b, op, reads, writes):
            S.add(eng, lambda e: e.tensor_tensor(out=out, in0=a, in1=b, op=op), reads=reads, writes=writes)

        def stt(eng, out, a, scalar, b, op0, op1, reads, writes):
            S.add(eng, lambda e: e.scalar_tensor_tensor(out=out, in0=a, scalar=scalar, in1=b, op0=op0, op1=op1),
                  reads=reads, writes=writes)

        def ts1(eng, out, a, scalar, op, reads, writes):
            S.add(eng, lambda e: e.tensor_scalar(out=out, in0=a, scalar1=scalar, scalar2=None, op0=op),
                  reads=reads, writes=writes)

        def cp(eng, out, in_, reads, writes):
            if eng == "act":
                S.add(eng, lambda e: e.activation(out=out, in_=in_, func=AF.Copy), reads=reads, writes=writes)
            else:
                S.add(eng, lambda e: e.tensor_copy(out=out, in_=in_), reads=reads, writes=writes)

        def ms(eng, ap, val, writes):
            S.add(eng, lambda e: e.memset(ap, val), writes=writes)

        dma("sp", If[:], cI[:, :], [], ["If"], "c0")
        dma("sp", Tf[:], cT[:, 0:128], [], ["Tf"], "c1")
        dma("sp", invw[:], cInvw[:, :], [], ["invw"], "c2")
        dma("sp", invc[:], cInvc[:, :, :], [], ["invc"], "c3")
        dma("pool", Ib[:], cI[:, :], [], ["Ib"], "c4")
        dma("pool", Ub[:], cU[:, :], [], ["Ub"], "c5")
        dma("pool", Lb[:], cL[:, :], [], ["Lb"], "c6")
        dma("pool", Tb[:], cT[:, 0:128], [], ["Tb"], "c7")
        dma("pool", O1[:], cOnes[:, :], [], ["O1"], "c8")
        ts1("dve", Ob[:], O1[:], 1.0 / 1024.0, ALU.mult, ["O1"], ["Ob"])
        ms("dve", QT[:], 0.0, ["QT"])
        ms("dve", QTo[:], 0.0, ["QT"])

        def convert(l):
            def cdma(key, idx, out, in_):
                k = (key, l, idx)
                grp.setdefault((key, l), []).append(k)
                fine = (l == 0 and key in ("g1", "u1"))
                dma("pool", out, in_, [], [k], ("cv", key, l))
            for f, (gname, uname, dname) in (("1", ("ffn1_w_gate", "ffn1_w_up", "ffn1_w_down")),
                                             ("2", ("ffn2_w_gate", "ffn2_w_up", "ffn2_w_down"))):
                srcs = {nm: W[wn][l].rearrange("(k p) (c m) -> c p k m", p=128, m=128) for nm, wn in (("g", gname), ("u", uname))}
                dsts = {nm: sc[nm + f, l].ap().rearrange("c p (k m) -> c p k m", m=128) for nm in ("g", "u")}
                for c in range(NFF):
                    for nm in ("g", "u"):
                        cdma(nm + f, c, dsts[nm][c], srcs[nm][c])
                src = W[dname][l].rearrange("(c p) (m n) -> m p c n", p=128, n=128)
                dst = sc["d" + f, l].ap().rearrange("m p (c n) -> m p c n", n=128)
                for m in range(8):
                    cdma("d" + f, m, dst[m], src[m])
                if f == "1":
                    src = W["w_in"][l].rearrange("(k p) (c m) -> c p k m", p=128, m=128)
                    dst = sc["wf", l].ap().rearrange("c p (k m) -> c p k m", m=128)
                    for i in range(8):
                        cdma("wf", i, dst[i], src[i])
                    cdma("qkv", 0, sc["qkv", l].ap().rearrange("p (k n) -> p k n", n=1536),
                         W["w_in"][l][:, 1024:2560].rearrange("(k p) n -> p k n", p=128))
                    for i in range(8, 32):
                        cdma("wf", i, dst[i], src[i + 12])
                    src = W["pool_proj"][l].rearrange("(c p) (m n) -> m p c n", p=128, n=128)
                    dst = sc["pp", l].ap().rearrange("m p (c n) -> m p c n", n=128)
                    for m in range(8):
                        cdma("pp", m, dst[m], src[m])
                    src = W["conv_proj"][l].rearrange("(c p) (m n) -> m p c n", p=128, n=128)
                    dst = sc["cp", l].ap().rearrange("m p (c n) -> m p c n", n=128)
                    for m in range(8):
                        cdma("cp", m, dst[m], src[m])
                    src = W["attn_proj"][l].rearrange("(c p) (m n) -> m p c n", p=128, n=128)
                    dst = sc["ap", l].ap().rearrange("m p (c n) -> m p c n", n=128)
                    for m in range(8):
                        cdma("ap", m, dst[m], src[m])
                    src = W["w_out"][l].rearrange("(c p) (m n) -> m p c n", p=128, n=128)
                    dst = sc["wo", l].ap().rearrange("m p (c n) -> m p c n", n=128)
                    for m in range(8):
                        cdma("wo", m, dst[m], src[m])

        def layer_params(l):
            for i, nm in enumerate(("ffn1_norm", "mix_norm", "ffn2_norm")):
                dma("sp", gn[:, i, :], W[nm][l].rearrange("(k p) -> p k", p=128), [], ["gn"], "lp0", slow=True)
            dma("sp", qg[:], W["q_norm"][l].rearrange("h d -> (h d)").partition_broadcast(128), [], ["qg"], "lp1")
            dma("sp", kg[:], W["k_norm"][l].rearrange("h d -> (h d)").partition_broadcast(128), [], ["kg"], "lp2")
            dma("sp", pscale[:], W["pool_scale"][l].rearrange("(c p) -> p c", p=128), [], ["pscale"], "lp3", slow=True)
            for c in range(2):
                dma("sp", cw[:, c, :], W["conv_w"][l][:, c * 128:(c + 1) * 128].rearrange("j p -> p j"), [], ["cw"], "lp4", slow=True)
            ms("dve", pwf[:], 0.0, ["pwf"])
            for g in range(4):
                dma("sp", pwf[(g % 2) * 64:(g % 2) * 64 + 64, g // 2, (g % 2) * 64:(g % 2) * 64 + 64],
                    W["pool_w"][l][g], [], ["pwf"], "lp5")
            cp("dve", pwb[:], pwf[:], ["pwf"], ["pwb"])
            ts1("dve", kg[:], kg[:], 8.0, ALU.mult, ["kg"], ["kg"])

        def rmsnorm(n, gi):
            hk = ["h%d" % k for k in range(8)]
            for k in range(8):
                act(R[:, k, 0:n], hT[:, k, 0:n], AF.Square, [hk[k]], [("R", k)])
            for k in range(8):
                mm(bank(6)[:, 0:n], Ob[:], R[:, k, 0:n], k == 0, k == 7, [("R", k), "Ob"], [bkey(6)])
            act(rstd[:, 0:n], bank(6)[:, 0:n], AF.Ln, [bkey(6)], ["rstd"], bias=EPS, scale=1.0)
            act(rstd[:, 0:n], rstd[:, 0:n], AF.Exp, ["rstd"], ["rstd"], scale=-0.5)
            for k in range(8):
                stt("dve", uB[:, k, 0:n], hT[:, k, 0:n], gn[:, gi, k:k + 1], rstd[:, 0:n], ALU.mult, ALU.mult,
                    [hk[k], "gn", "rstd"], [("u", k)])

        cnt = {"gu": 0, "big": 0, "wf": 0, "vs": 0, "ks": 0, "z": 0, "q3": 0, "pair": 0}

        def ffn(l, f, n):
            ukeys = [("u", k) for k in range(8)]
            for c in range(NFF):
                s = cnt["gu"] % 3
                cnt["gu"] += 1
                fine = (l == 0 and f == "1")
                dma("sp", gu[s][:, 0, :], sc["g" + f, l].ap()[c], grp["g" + f, l], [("gu", s, 0)], ("gu", s, 0))
                dma("sp", gu[s][:, 1, :], sc["u" + f, l].ap()[c], grp["u" + f, l], [("gu", s, 1)], ("gu", s, 1))
                bg, bu = (c % 2), 2 + (c % 2)
                for k in range(8):
                    mm(bank(bg)[:, 0:n], gu[s][:, 0, k * 128:(k + 1) * 128], uB[:, k, 0:n], k == 0, k == 7,
                       [("gu", s, 0), ukeys[k]], [bkey(bg)])
                for k in range(8):
                    mm(bank(bu)[:, 0:n], gu[s][:, 1, k * 128:(k + 1) * 128], uB[:, k, 0:n], k == 0, k == 7,
                       [("gu", s, 1), ukeys[k]], [bkey(bu)])
                sg = sig[c % 2]
                act(sg[:, 0:n], bank(bg)[:, 0:n], AF.Silu, [bkey(bg)], [("sig", c % 2)])
                tt("dve", R[:, c, 0:n], sg[:, 0:n], bank(bu)[:, 0:n], ALU.mult, [("sig", c % 2), bkey(bu)], [("R", c)])
            for m in range(8):
                s = cnt["big"] % 2
                cnt["big"] += 1
                dma("sp", bigs[s][:, :], sc["d" + f, l].ap()[m], grp["d" + f, l], [("big", s)], ("big", s))
                bd = 4 + (m % 2)
                for c in range(NFF):
                    mm(bank(bd)[:, 0:n], bigs[s][:, c * 128:(c + 1) * 128], R[:, c, 0:n], c == 0, c == NFF - 1,
                       [("big", s), ("R", c)], [bkey(bd)])
                stt("dve", hT[:, m, 0:n], bank(bd)[:, 0:n], 0.5, hT[:, m, 0:n], ALU.mult, ALU.add,
                    [bkey(bd), "h%d" % m], ["h%d" % m])

        ZP = [P[0], P[1]]
        ACC = P[2]
        OP = P[3]
        ATT = {}

        def att_views():
            f = R[:, :, :].bitcast(F32).rearrange("p a b -> p (a b)")
            b = R[:, :, :].rearrange("p a b -> p (a b)")
            ATT["SP"] = [f[:, 0:1024], f[:, 1024:2048]]
            ATT["T1"] = [f[:, 2048 + 1024 * i:3072 + 1024 * i] for i in range(3)]
            ATT["SPb"] = [b[:, 10240 + 1024 * i:11264 + 1024 * i] for i in range(3)]
            ATT["A"] = [b[:, 13312 + 1024 * i:14336 + 1024 * i] for i in range(3)]
        att_views()
        RALL = [("R", c) for c in range(32)]
        ZK = [[bkey(0), bkey(1)], [bkey(2), bkey(3)]]
        AK = [bkey(4), bkey(5)]
        OK_ = [bkey(6), bkey(7)]
        AKEYS = [(nm, z) for nm in ("aSP", "aT1", "aSPb", "aSPbh", "aA") for z in range(3)]

        def att_barrier():
            S.add("pool", lambda e: e.memset(dummy[:], 0.0), reads=RALL + AKEYS, writes=RALL + AKEYS)

        def attention(qT, nq, kblocks, out_fn):
            W8 = 8 * nq
            nb = len(kblocks)
            base = cnt["z"]
            cnt["z"] += nb

            def bufs(i):
                g = base + i
                return g % 2, g % 3

            def s_qk_mm(i):
                kb = kblocks[i]; nk = kb["nk"]; z, z3 = bufs(i)
                Z = ZP[z]
                for h in range(8):
                    mm(Z[0:nk, h * nq:(h + 1) * nq], kb["kT"](h), qT(h), True, True, kb["kreads"] + ["QT"], ZK[z])

            def s_qk_act(i):
                kb = kblocks[i]; nk = kb["nk"]; z, z3 = bufs(i)
                Z = ZP[z]
                SPt = ATT["SP"][z]
                act(SPt[0:nk, 0:W8], Z[0:nk, 0:W8], AF.Exp, ZK[z], [("aSP", z)])
                act(SPt[0:nk, 0:W8], SPt[0:nk, 0:W8], AF.Ln, [("aSP", z)], [("aSP", z)], bias=1.0, scale=1.0)

            def s_cast(i):
                kb = kblocks[i]; nk = kb["nk"]; z, z3 = bufs(i)
                Z = ZP[z]; SPt = ATT["SP"][z]; T1t = ATT["T1"][z3]; SPbt = ATT["SPb"][z3]
                kS, kT1, kSb = ("aSP", z), ("aT1", z3), ("aSPb", z3)
                if kb["diag"]:
                    tt("pool", SPbt[0:nk, 0:W8].rearrange("p (h q) -> p h q", q=nq),
                       SPt[0:nk, 0:W8].rearrange("p (h q) -> p h q", q=nq),
                       Tf[0:nk, 0:nq].unsqueeze(1).broadcast_to([nk, 8, nq]), ALU.mult, [kS, "Tf"], [kSb, (kSb[0] + "h", kSb[1])])
                elif W8 >= 1024:
                    cp("pool", SPbt[0:nk, 0:W8 // 2], SPt[0:nk, 0:W8 // 2], [kS], [kSb])
                    cp("dve", SPbt[0:nk, W8 // 2:W8], SPt[0:nk, W8 // 2:W8], [kS], [(kSb[0] + "h", kSb[1])])
                else:
                    cp("dve", SPbt[0:nk, 0:W8], SPt[0:nk, 0:W8], [kS], [kSb])
                tt("dve", T1t[0:nk, 0:W8], Z[0:nk, 0:W8], SPt[0:nk, 0:W8], ALU.subtract, ZK[z] + [kS], [kT1])

            def s_u(i):
                kb = kblocks[i]; nk = kb["nk"]; z, z3 = bufs(i)
                T1t = ATT["T1"][z3]; SPbt = ATT["SPb"][z3]
                kT1, kSb = ("aT1", z3), ("aSPb", z3)
                for half in range(0, W8, 512):
                    w = min(512, W8 - half)
                    mm(ACC[0:nk, half:half + w], Ub[0:nk, 0:nk], SPbt[0:nk, half:half + w], i == 0, True,
                       [kSb, ("aSPbh", z3), "Ub"], AK)
                tt("dve", T1t[0:nk, 0:W8], T1t[0:nk, 0:W8], ACC[0:nk, 0:W8], ALU.subtract, [kT1] + AK, [kT1])
                kb["vload"](z3)

            def s_lw(i):
                kb = kblocks[i]; nk = kb["nk"]; z, z3 = bufs(i)
                SPbt = ATT["SPb"][z3]; kSb = ("aSPb", z3)
                if i < nb - 1:
                    nk2 = kblocks[i + 1]["nk"]
                    for half in range(0, W8, 512):
                        w = min(512, W8 - half)
                        if nk2 == nk:
                            mm(ACC[0:nk, half:half + w], Lb[0:nk, 0:nk], SPbt[0:nk, half:half + w], False, True, [kSb, ("aSPbh", z3), "Lb"], AK)
                        else:
                            mm(ACC[0:nk2, half:half + w], O1[0:nk, 0:nk2], SPbt[0:nk, half:half + w], True, True, [kSb, ("aSPbh", z3), "O1"], AK)

            def s_expa(i):
                kb = kblocks[i]; nk = kb["nk"]; z, z3 = bufs(i)
                T1t = ATT["T1"][z3]; At = ATT["A"][z3]
                kT1, kA = ("aT1", z3), ("aA", z3)
                act(At[0:nk, 0:W8], T1t[0:nk, 0:W8], AF.Exp, [kT1], [kA])
                if kb["diag"]:
                    tt("pool", At[0:nk, 0:W8].rearrange("p (h q) -> p h q", q=nq),
                       At[0:nk, 0:W8].rearrange("p (h q) -> p h q", q=nq),
                       Tb[0:nk, 0:nq].unsqueeze(1).broadcast_to([nk, 8, nq]), ALU.mult, [kA, "Tb"], [kA])

            def s_av(i):
                kb = kblocks[i]; nk = kb["nk"]; z, z3 = bufs(i)
                At = ATT["A"][z3]; kA = ("aA", z3)
                for h in range(8):
                    mm(OP[:, h * nq:(h + 1) * nq], vsl[z3][0:nk, (h // 2) * 128:(h // 2) * 128 + 128],
                       At[0:nk, h * nq:(h + 1) * nq], i == 0 and (h * nq) % 512 == 0, i == nb - 1,
                       [kA, ("vs", z3)], OK_)

            for it in range(nb + 4):
                if it < nb:
                    s_qk_mm(it)
                if 0 <= it - 3 < nb:
                    s_lw(it - 3)
                if 0 <= it - 4 < nb:
                    s_av(it - 4)
                if 0 <= it - 3 < nb:
                    s_expa(it - 3)
                if it < nb:
                    s_qk_act(it)
                if 0 <= it - 1 < nb:
                    s_cast(it - 1)
                if 0 <= it - 2 < nb:
                    s_u(it - 2)
            out_fn()

        def layer_tile(l, ti):
            s0, n = TILES[ti]
            tail = (ti == 8)
            hk = ["h%d" % k for k in range(8)]
            nblk = [(0, 128), (128, 128), (256, 128), (384, 128)] if not tail else [(0, 80)]
            if ti == 0:
                ms("dve", exta[:], 0.0, ["exta"])
                ms("dve", extc[:], 0.0, ["extc"])
            if STOP < 1:
                return
            if l == 0:
                for (bo, bn) in nblk:
                    for half in range(2):
                        dma("sp", xl[0:bn, half * 512:(half + 1) * 512], xin[s0 + bo:s0 + bo + bn, half * 512:(half + 1) * 512],
                            [], ["xl%d" % half], "xl%d" % half)
                    for half in range(2):
                        bk = 6 + half
                        for j in range(4):
                            k = half * 4 + j
                            tr(bank(bk)[:, j * 128:j * 128 + bn], xl[0:bn, k * 128:(k + 1) * 128], If[0:bn, 0:bn],
                               ["xl%d" % half, "If"], [bkey(bk)])
                        for j in range(4):
                            k = half * 4 + j
                            cp("act" if half else "dve", hT[:, k, bo:bo + bn], bank(bk)[:, j * 128:j * 128 + bn],
                               [bkey(bk)], [hk[k]])
            else:
                for k in range(8):
                    dma("sp", hT[:, k, 0:n], hT_s.ap()[:, k, s0:s0 + n], [("hTs", ti)], [hk[k]], ("hld", k))
            if STOP < 2:
                return
            rmsnorm(n, 0)
            if STOP < 3:
                return
            ffn(l, "1", n)
            if STOP < 4:
                return
            rmsnorm(n, 1)
            ukeys = [("u", k) for k in range(8)]
            Rflat = R[:, :, :].rearrange("p a b -> p (a b)")
            for k in range(8):
                dma("sp", Rflat[:, k * 1536:(k + 1) * 1536], sc["qkv", l].ap()[:, k * 1536:(k + 1) * 1536], grp["qkv", l],
                    RALL[3 * k:3 * k + 3], ("qkvw", k))

            def wf_load(idx):
                s = cnt["wf"] % 3
                cnt["wf"] += 1
                dma("sp", wfs[s][:, :], sc["wf", l].ap()[idx], grp["wf", l], [("wf", s)], ("wf", s))
                return s

            def fm_proj(idx, bk):
                s = wf_load(idx)
                for k in range(8):
                    mm(bank(bk)[:, 0:n], wfs[s][:, k * 128:(k + 1) * 128], uB[:, k, 0:n], k == 0, k == 7,
                       [("wf", s), ukeys[k]], [bkey(bk)])

            if not tail:
                segs = [(15, 0, n)]
                W_a = 15 + n
            else:
                segs = [(31 * j + 15, 16 * j, 16) for j in range(5)]
                W_a = 155
            if tail:
                for c in range(2):
                    cp("dve", Sb[0][:, 0:15], exta[:, c, 512:527], ["exta"], ["Sb0"])
                    cp("dve", exta[:, c, 0:15], Sb[0][:, 0:15], ["Sb0"], ["exta"])
                    cp("dve", Sb[0][:, 0:2], extc[:, c, 512:514], ["extc"], ["Sb0"])
                    cp("dve", extc[:, c, 0:2], Sb[0][:, 0:2], ["Sb0"], ["extc"])
                for j in range(1, 5):
                    for c in range(2):
                        dma("sp", exta[:, c, 31 * j:31 * j + 15],
                            spool[l, j - 1][:, c * 128:(c + 1) * 128].rearrange("t p -> p t"),
                            [], ["exta"], "haloA", slow=True)
                        dma("sp", extc[:, c, 18 * j:18 * j + 2],
                            sconv[l, j - 1][:, c * 128:(c + 1) * 128].rearrange("t p -> p t"),
                            [], ["extc"], "haloC", slow=True)
            csegs = [(2, 0, n)] if not tail else [(18 * j + 2, 16 * j, 16) for j in range(5)]
            W_c = 2 + n if not tail else 90
            for c in range(2):
                fm_proj(c, c)
                for (eo, to, sn) in segs:
                    cp("act", exta[:, c, eo:eo + sn], bank(c)[:, to:to + sn], [bkey(c)], ["exta"])
            for c in range(2):
                fm_proj(2 + c, 2)
                cp("act", xbt[:, 0:n], bank(2)[:, 0:n], [bkey(2)], ["xbt"])
                fm_proj(4 + c, 3)
                cp("act", gbt[:, c, 0:n], bank(3)[:, 0:n], [bkey(3)], ["gbt"])
                fm_proj(6 + c, 4)
                for (eo, to, sn) in csegs:
                    tt("dve", extc[:, c, eo:eo + sn], bank(4)[:, to:to + sn], xbt[:, to:to + sn], ALU.mult,
                       [bkey(4), "xbt"], ["extc"])
            if ti == 7 or tail:
                for c in range(2):
                    if tail:
                        lst = [(j, 31 * j + 16, 18 * j + 16) for j in range(5)]
                    else:
                        lst = []
                    for (j, ao, co_) in lst:
                        tr(bank(7)[0:15, 0:128], exta[:, c, ao:ao + 15], If[:, :], ["exta", "If"], [bkey(7)])
                        cp("dve", cvt[0:15, 0:128], bank(7)[0:15, 0:128], [bkey(7)], ["cvt"])
                        dma("sp", p_o[l, j][:, c * 128:(c + 1) * 128], cvt[0:15, 0:128], ["cvt"], [], "po")
                        tr(bank(7)[0:2, 128:256], extc[:, c, co_:co_ + 2], If[:, :], ["extc", "If"], [bkey(7)])
                        cp("dve", cvt[0:2, 128:256], bank(7)[0:2, 128:256], [bkey(7)], ["cvt"])
                        dma("sp", c_o[l, j][:, c * 128:(c + 1) * 128], cvt[0:2, 128:256], ["cvt"], [], "co")
            Rw = R[:, 0:24, :].rearrange("p (k r) b -> p k (r b)", r=3)
            blk_banks = {}

            def qkv_mm(bi):
                bo, bn = nblk[bi]
                g3 = cnt["q3"] % 2
                cnt["q3"] += 1
                b3 = [3 * g3, 3 * g3 + 1, 3 * g3 + 2]
                blk_banks[bi] = b3
                for j in range(3):
                    for k in range(8):
                        mm(bank(b3[j])[0:bn, :], uB[:, k, bo:bo + bn], Rw[:, k, j * 512:(j + 1) * 512], k == 0, k == 7,
                           RALL[3 * k:3 * k + 3] + [ukeys[k]], [bkey(b3[j])])

            def qkv_chain(bi):
                bo, bn = nblk[bi]
                b3 = blk_banks[bi]
                r0 = s0 + bo
                QK = (0, 1)
                src = [bank(b3[j])[0:bn, :] for j in QK]
                tsq = [tsq2[j] for j in QK]
                ssq = [ssq82[j] for j in QK]
                tkb = [tkb2[j] for j in QK]
                ktsq = [("tsq", 0), "xbt"]
                kssq = [("ssq8", 0), ("ssq8", 1)]
                ktkb = [("tkb", 0), "pin"]
                kf = [tkf[bi % 2], tkf[(bi + 1) % 2]]
                kfk = [("tkf", bi % 2), ("tkf", (bi + 1) % 2)]
                gains = [(qg, "qg"), (kg, "kg")]
                for j in QK:
                    act(tsq[j][0:bn, :], src[j], AF.Square, [bkey(b3[j])], [ktsq[j]])
                vf = tvf[bi % 2]
                vb = tvb[bi % 2]
                cp("act", vf[0:bn, :], bank(b3[2])[0:bn, :], [bkey(b3[2])], [("tvf", bi % 2)])
                for j in QK:
                    S.add("dve", lambda e, bn=bn, t=tsq[j], q=ssq[j]: e.tensor_reduce(
                        out=q[0:bn, :], in_=t[0:bn, :].rearrange("p (h d) -> p h d", d=64), axis=AX.X, op=ALU.add),
                        reads=[ktsq[j]], writes=[kssq[j]])
                for j in QK:
                    act(ssq[j][0:bn, :], ssq[j][0:bn, :], AF.Ln, [kssq[j]], [kssq[j]], bias=64.0 * EPS, scale=1.0)
                    act(ssq[j][0:bn, :], ssq[j][0:bn, :], AF.Exp, [kssq[j]], [kssq[j]], scale=-0.5)
                for j in QK:
                    tt("dve", kf[j][0:bn, :].rearrange("p (h d) -> p h d", d=64), src[j].rearrange("p (h d) -> p h d", d=64),
                       ssq[j][0:bn, :].unsqueeze(2).broadcast_to([bn, 8, 64]), ALU.mult, [bkey(b3[j]), kssq[j]], [kfk[j]])
                tt("dve", tkb[0][0:bn, :], kf[0][0:bn, :], qg[0:bn, :], ALU.mult, [kfk[0], "qg"], [ktkb[0]])
                tt("dve", kf[1][0:bn, :], kf[1][0:bn, :], kg[0:bn, :], ALU.mult, [kfk[1], "kg"], [kfk[1]])
                cp("pool", tkb[1][0:bn, :], kf[1][0:bn, :], [kfk[1]], [ktkb[1]])
                dma("sp", k_o[l, r0:r0 + bn, :], kf[1][0:bn, :], [kfk[1]], [], ("ko", (bi + 1) % 2))
                cp("pool", vb[0:bn, :], vf[0:bn, :], [("tvf", bi % 2)], [("tvb", bi % 2)])
                dma("sp", v_o[l, r0:r0 + bn, :], vf[0:bn, :], [("tvf", bi % 2)], [], ("vo", bi % 2))
                dma("sp", vb_s.ap()[r0:r0 + bn, :], vb[0:bn, :], [("tvb", bi % 2)], [("vbs", r0 // 128)], ("vbs", bi % 2))
                for j in QK:
                    bt = 7 - j
                    for c in range(4):
                        tr(bank(bt)[:, c * 128:c * 128 + bn], tkb[j][0:bn, c * 128:(c + 1) * 128], Ib[0:bn, 0:bn],
                           [ktkb[j], "Ib"], [bkey(bt)])
                pq = bank(7).rearrange("p (c t) -> p c t", t=128)
                pk = bank(6).rearrange("p (c t) -> p c t", t=128)
                cp("act", QT[0:64, :, bo:bo + bn], pq[0:64, :, 0:bn], [bkey(7)], ["QT"])
                cp("act", QTo[64:128, :, bo:bo + bn], pq[64:128, :, 0:bn], [bkey(7)], ["QT"])
                if not tail:
                    cp("act", KT[:, :, r0:r0 + bn], pk[:, :, 0:bn], [bkey(6)], [("KT", r0 // 128)])
                else:
                    cp("act", KTt[:, :, 0:bn], pk[:, :, 0:bn], [bkey(6)], ["KTt"])
                    cp("act", KT[:, :, r0:r0 + 16], pk[:, :, 0:16], [bkey(6)], [("KT", 32)])

            qkv_mm(0)
            for c in range(2):
                e = exta[:, c, :]
                tt("dve", Sb[0][:, 1:W_a], e[:, 1:W_a], e[:, 0:W_a - 1], ALU.add, ["exta"], ["Sb0"])
                tt("dve", Sb[1][:, 3:W_a], Sb[0][:, 3:W_a], Sb[0][:, 1:W_a - 2], ALU.add, ["Sb0"], ["Sb1"])
                if c == 0:
                    lo, hi = Sb[0], Sb[1]
                    klo, khi = "Sb0", "Sb1"
                else:
                    tt("dve", Sb[0][:, 7:W_a], Sb[1][:, 7:W_a], Sb[1][:, 3:W_a - 4], ALU.add, ["Sb1"], ["Sb0"])
                    tt("dve", Sb[1][:, 15:W_a], Sb[0][:, 15:W_a], Sb[0][:, 7:W_a - 8], ALU.add, ["Sb0"], ["Sb1"])
                    lo, hi = Sb[0], Sb[1]
                    klo, khi = "Sb0", "Sb1"
                for (eo, to, sn) in segs:
                    stt("dve", pin[0:64, c, to:to + sn], lo[0:64, eo:eo + sn], invw[0:64, c:c + 1],
                        e[0:64, eo:eo + sn], ALU.mult, ALU.subtract, [klo, "invw", "exta"], ["pin"])
                    stt("dve", pin[64:128, c, to:to + sn], hi[64:128, eo:eo + sn], invw[64:128, c:c + 1],
                        e[64:128, eo:eo + sn], ALU.mult, ALU.subtract, [khi, "invw", "exta"], ["pin"])
                if ti == 0:
                    tt("dve", cvt[0:64, 0:16], lo[0:64, 15:31], invc[0:64, c, :], ALU.mult, [klo, "invc"], ["cvt"])
                    tt("dve", pin[0:64, c, 0:16], cvt[0:64, 0:16], e[0:64, 15:31], ALU.subtract, ["cvt", "exta"], ["pin"])
                    tt("dve", cvt[64:128, 0:16], hi[64:128, 15:31], invc[64:128, c, :], ALU.mult, [khi, "invc"], ["cvt"])
                    tt("dve", pin[64:128, c, 0:16], cvt[64:128, 0:16], e[64:128, 15:31], ALU.subtract, ["cvt", "exta"], ["pin"])
            for c in range(2):
                mm(bank(7)[:, 0:n], pwb[:, c, :], pin[:, c, 0:n], True, True, ["pwb", "pin"], [bkey(7)])
                ts1("dve", yain[:, c, 0:n], bank(7)[:, 0:n], pscale[:, c:c + 1], ALU.mult, [bkey(7), "pscale"], ["yain"])
            for c in range(2):
                x_ = extc[:, c, :]
                for (eo, to, sn) in csegs:
                    ts1("dve", cvt[:, to:to + sn], x_[:, eo - 2:eo - 2 + sn], cw[:, c, 0:1], ALU.mult, ["extc", "cw"], ["cvt"])
                    stt("dve", cvt[:, to:to + sn], x_[:, eo - 1:eo - 1 + sn], cw[:, c, 1:2], cvt[:, to:to + sn],
                        ALU.mult, ALU.add, ["extc", "cw", "cvt"], ["cvt"])
                    stt("dve", cvt[:, to:to + sn], x_[:, eo:eo + sn], cw[:, c, 2:3], cvt[:, to:to + sn],
                        ALU.mult, ALU.add, ["extc", "cw", "cvt"], ["cvt"])
                tt("dve", ybin[:, c, 0:n], cvt[:, 0:n], gbt[:, c, 0:n], ALU.mult, ["cvt", "gbt"], ["ybin"])
            if not tail:
                for c in range(2):
                    cp("dve", Sb[0][:, 0:15], exta[:, c, 512:527], ["exta"], ["Sb0"])
                    cp("dve", exta[:, c, 0:15], Sb[0][:, 0:15], ["Sb0"], ["exta"])
                    cp("dve", Sb[0][:, 0:2], extc[:, c, 512:514], ["extc"], ["Sb0"])
                    cp("dve", extc[:, c, 0:2], Sb[0][:, 0:2], ["Sb0"], ["extc"])
            if STOP < 5:
                return
            for bi in range(len(nblk)):
                if bi + 1 < len(nblk):
                    qkv_mm(bi + 1)
                qkv_chain(bi)
            if STOP < 7:
                return
            def qT_fn(off, nq):
                return lambda h: (QTo if h % 2 else QT)[:, h // 2, off:off + nq]

            def kblock_prompt(jb, nk, diag):
                def vload(s):
                    dma("sp", vsl[s][0:nk, :], vb_s.ap()[jb * 128:jb * 128 + nk, :], [("vbs", jb)], [("vs", s)], ("vs", s))
                return {"nk": nk, "diag": diag, "vload": vload, "kreads": [("KT", jb)],
                        "kT": lambda h: KT[:, h // 2, jb * 128:jb * 128 + nk]}

            def out_fn_mk(off, nq):
                def f():
                    OPv = OP[:, 0:8 * nq].rearrange("p (c two q) -> p c two q", two=2, q=nq)
                    cp("dve", OT[0:64, :, off:off + nq], OPv[0:64, :, 0, :], OK_, ["OT"])
                    cp("act", OT[64:128, :, off:off + nq], OPv[64:128, :, 1, :], OK_, ["OT"])
                return f

            att_barrier()
            if not tail:
                for qb in range(4):
                    gi_ = s0 // 128 + qb
                    kbl = [kblock_prompt(gi_, 128, True)] + [kblock_prompt(j, 128, False) for j in range(gi_ - 1, -1, -1)]
                    attention(qT_fn(qb * 128, 128), 128, kbl, out_fn_mk(qb * 128, 128))
            else:
                kbl = [kblock_prompt(32, 16, True)] + [kblock_prompt(j, 128, False) for j in range(31, -1, -1)]
                attention(qT_fn(0, 16), 16, kbl, out_fn_mk(0, 16))
                for si in range(NS):
                    for jb in range(8):
                        s = cnt["ks"] % 2
                        cnt["ks"] += 1
                        dma("pool", ksl[s][:, :], ck[l, si, jb * 128:(jb + 1) * 128, :], [], [("ks", s)], ("ks", s))
                        bt = 6 + (jb % 2)
                        pb = bank(bt)
                        for c in range(4):
                            tr(pb[:, c * 128:(c + 1) * 128], ksl[s][:, c * 128:(c + 1) * 128], Ib[:, :],
                               [("ks", s), "Ib"], [bkey(bt)])
                        cp("act" if jb % 2 else "dve", KTs[:, :, jb * 128:(jb + 1) * 128],
                           pb.rearrange("p (c t) -> p c t", t=128), [bkey(bt)], ["KTs"])

                    def kb_cache(jb, si=si):
                        def vload(s):
                            dma("pool", vsl[s][:, :], cv[l, si, jb * 128:(jb + 1) * 128, :], [], [("vs", s)], ("vs", s))
                        return {"nk": 128, "diag": False, "vload": vload, "kreads": ["KTs"],
                                "kT": lambda h: KTs[:, h // 2, jb * 128:(jb + 1) * 128]}

                    def kb_new(si=si):
                        r = TP + 16 * si
                        def vload(s):
                            dma("sp", vsl[s][0:16, :], vb_s.ap()[r:r + 16, :], [("vbs", 32)], [("vs", s)], ("vs", s))
                        return {"nk": 16, "diag": True, "vload": vload, "kreads": ["KTt"],
                                "kT": lambda h: KTt[:, h // 2, 16 + 16 * si:32 + 16 * si]}
                    kbl = [kb_new()] + [kb_cache(j) for j in range(7, -1, -1)]
                    attention(qT_fn(16 + 16 * si, 16), 16, kbl, out_fn_mk(16 + 16 * si, 16))
            att_barrier()
            if STOP < 8:
                return
            for m in range(8):
                ms("pool", macc[:, 0:n], 0.0, ["macc"])
                for br in range(3):
                    pr = cnt["pair"] % 3
                    cnt["pair"] += 1
                    by, bg = 2 * pr, 2 * pr + 1
                    s = cnt["big"] % 2
                    cnt["big"] += 1
                    if br == 0:
                        dma("sp", bigs[s][:, 0:256], sc["pp", l].ap()[m], grp["pp", l], [("big", s)], ("big", s))
                        for c in range(2):
                            mm(bank(by)[:, 0:n], bigs[s][:, c * 128:(c + 1) * 128], yain[:, c, 0:n], c == 0, c == 1,
                               [("big", s), "yain"], [bkey(by)])
                    elif br == 1:
                        dma("sp", bigs[s][:, 0:256], sc["cp", l].ap()[m], grp["cp", l], [("big", s)], ("big", s))
                        for c in range(2):
                            mm(bank(by)[:, 0:n], bigs[s][:, c * 128:(c + 1) * 128], ybin[:, c, 0:n], c == 0, c == 1,
                               [("big", s), "ybin"], [bkey(by)])
                    else:
                        dma("sp", bigs[s][:, 0:512], sc["ap", l].ap()[m], grp["ap", l], [("big", s)], ("big", s))
                        for c in range(4):
                            mm(bank(by)[:, 0:n], bigs[s][:, c * 128:(c + 1) * 128], OT[:, c, 0:n], c == 0, c == 3,
                               [("big", s), "OT"], [bkey(by)])
                    fm_proj(8 + br * 8 + m, bg)
                    sg = sig[br % 2]
                    act(sg[:, 0:n], bank(bg)[:, 0:n], AF.Sigmoid, [bkey(bg)], [("sig", br % 2)])
                    tt("dve", mtmp[:, 0:n], sg[:, 0:n], bank(by)[:, 0:n], ALU.mult, [("sig", br % 2), bkey(by)], ["mtmp"])
                    if br < 2:
                        tt("pool", macc[:, 0:n], macc[:, 0:n], mtmp[:, 0:n], ALU.add, ["macc", "mtmp"], ["macc"])
                    else:
                        tt("pool", R[:, 8 + m, 0:n], macc[:, 0:n], mtmp[:, 0:n], ALU.add, ["macc", "mtmp"], [("R", 8 + m)])
            for m2 in range(8):
                s = cnt["big"] % 2
                cnt["big"] += 1
                dma("sp", bigs[s][:, 0:1024], sc["wo", l].ap()[m2], grp["wo", l], [("big", s)], ("big", s))
                bd = 6 + (m2 % 2)
                for m in range(8):
                    mm(bank(bd)[:, 0:n], bigs[s][:, m * 128:(m + 1) * 128], R[:, 8 + m, 0:n], m == 0, m == 7,
                       [("big", s), ("R", 8 + m)], [bkey(bd)])
                tt("dve", hT[:, m2, 0:n], hT[:, m2, 0:n], bank(bd)[:, 0:n], ALU.add, [hk[m2], bkey(bd)], [hk[m2]])
            rmsnorm(n, 2)
            ffn(l, "2", n)
            if l < depth - 1:
                for k in range(8):
                    dma("sp", hT_s.ap()[:, k, s0:s0 + n], hT[:, k, 0:n], [hk[k]], [("hTs", ti)], ("hst", k))
            else:
                for (bo, bn) in nblk:
                    for half in range(2):
                        bk = 4 + half
                        for j in range(4):
                            k = half * 4 + j
                            tr(bank(bk)[0:bn, j * 128:(j + 1) * 128], hT[:, k, bo:bo + bn], If[:, :], [hk[k], "If"], [bkey(bk)])
                        cp("act" if half else "dve", xl[0:bn, half * 512:(half + 1) * 512], bank(bk)[0:bn, :], [bkey(bk)],
                           ["xl%d" % half])
                        dma("sp", y_o[s0 + bo:s0 + bo + bn, half * 512:(half + 1) * 512], xl[0:bn, half * 512:(half + 1) * 512],
                            ["xl%d" % half], [], "yo%d" % half)

        if PROBE:
            dma("pool", QT[0:64, :, 0:128], pq[0:64, :, :], ["QT"], ["QT"], "pq")
            dma("pool", QTo[64:128, :, 0:128], pq[64:128, :, :], ["QT"], ["QT"], "pq")
            dma("pool", KT[:, :, 0:384], pk[:, :, :], [], [("KT", 0), ("KT", 1), ("KT", 2)], "pk")
            dma("pool", vb_s.ap()[0:384, :], pv[:, :], [], [("vbs", 0), ("vbs", 1), ("vbs", 2)], "pv")

            def qT_fn(off, nq):
                return lambda h: (QTo if h % 2 else QT)[:, h // 2, off:off + nq]

            def kblock_prompt(jb, nk, diag):
                def vload(s):
                    dma("sp", vsl[s][0:nk, :], vb_s.ap()[jb * 128:jb * 128 + nk, :], [("vbs", jb)], [("vs", s)], ("vs", s))
                return {"nk": nk, "diag": diag, "vload": vload, "kreads": [("KT", jb)],
                        "kT": lambda h: KT[:, h // 2, jb * 128:jb * 128 + nk]}

            def out_fn():
                for p in range(4):
                    cp("dve", OT[0:64, p, 0:128], OP[0:64, (2 * p) * 128:(2 * p) * 128 + 128], OK_, ["OT"])
                    cp("act", OT[64:128, p, 0:128], OP[64:128, (2 * p + 1) * 128:(2 * p + 1) * 128 + 128], OK_, ["OT"])
            att_barrier()
            attention(qT_fn(0, 128), 128, [kblock_prompt(2, 128, True), kblock_prompt(1, 128, False), kblock_prompt(0, 128, False)], out_fn)
            att_barrier()
            for ii, (nm, z) in enumerate((("SP", 0), ("SP", 1), ("T1", 0), ("T1", 1))):
                dma("sp", pdbg[ii], ATT[nm][z][:, 0:128], RALL, [], "pdbg")
            for p in range(4):
                cp("dve", xl[:, p * 128:(p + 1) * 128], OT[:, p, 0:128], ["OT"], ["xl"])
            dma("sp", pout[:, :, :], xl[:, 0:512].rearrange("p (a b) -> p a b", b=128), ["xl"], [], "pout")
            S.emit(nc, st)
            return nc
        for l in range(depth):
            if os.environ.get("MK_CONV", "1") == "1":
                convert(l)
        for l in range(min(depth, NLAYERS)):
            if os.environ.get("MK_LP", "1") == "1":
                layer_params(l)
            tl = list(range(len(TILES)))
            if NTILES < 9:
                tl = tl[:NTILES - 1] + [8] if NTILES > 1 else [0]
            for ti in tl:
                layer_tile(l, ti)
        S.emit(nc, st)
    return nc


_CACHE = {}


def _consts():
    j = np.arange(128)
    U = (j[:, None] > j[None, :]).astype(np.float32)
    L = (j[:, None] <= j[None, :]).astype(np.float32)
    I = np.eye(128, dtype=np.float32)
    T = (j[:, None] < j[None, :]).astype(np.float32)
    T8 = np.tile(T, (1, 8))
    ones = np.ones((128, 128), np.float32)
    wins = np.array([[2, 8], [4, 16]], np.float32)
    invw = np.zeros((128, 2), np.float32)
    invc = np.zeros((128, 2, 16), np.float32)
    pos = np.arange(16, dtype=np.float32)
    for half in range(2):
        for c in range(2):
            w = wins[half, c]
            invw[half * 64:(half + 1) * 64, c] = 1.0 / w
            invc[half * 64:(half + 1) * 64, c, :] = 1.0 / np.minimum(pos + 1.0, w)
    return {"cU": U, "cL": L, "cI": I, "cT": T8, "cOnes": ones, "cInvw": invw, "cInvc": invc}


def kernel(**inputs):
    f32 = lambda a: np.ascontiguousarray(np.asarray(a, dtype=np.float32))
    inp = {k: f32(v) for k, v in inputs.items()}
    if "nc" not in _CACHE:
        _CACHE["nc"] = build()
    nc = _CACHE["nc"]
    consts = _consts()
    in_maps = []
    for c in range(8):
        sq = c % 4
        xin = np.concatenate([inp["meta"], inp["x_prompt"][sq], inp["x_sample"][4 * c:4 * c + 4].reshape(64, D)], axis=0)
        m = {"xin": np.ascontiguousarray(xin),
             "ck": np.ascontiguousarray(inp["cache_k"][:, 4 * c:4 * c + 4].reshape(2, NS, PAST, 512)),
             "cv": np.ascontiguousarray(inp["cache_v"][:, 4 * c:4 * c + 4].reshape(2, NS, PAST, 512)),
             "spool": np.ascontiguousarray(inp["state_pool"][:, 4 * c:4 * c + 4]),
             "sconv": np.ascontiguousarray(inp["state_conv"][:, 4 * c:4 * c + 4])}
        for n in WNAMES:
            m[n] = inp[n]
        m.update(consts)
        in_maps.append(m)
    res = run_bass_kernel_spmd(nc, in_maps, core_ids=list(range(8)))
    r = res.results
    y_prompt = np.stack([r[b]["y"][NMETA:TP] for b in range(4)]).reshape(4, 4096, D)
    y_sample = np.concatenate([r[c]["y"][TP:NT].reshape(4, 16, D) for c in range(8)], axis=0)
    k_prompt = np.stack([np.stack([r[b]["ko"][l, 0:TP] for b in range(4)]) for l in range(2)]).reshape(2, 4, TP, 8, 64)
    v_prompt = np.stack([np.stack([r[b]["vo"][l, 0:TP] for b in range(4)]) for l in range(2)]).reshape(2, 4, TP, 8, 64)
    pool_prompt = np.stack([np.stack([r[b]["po"][l, 0] for b in range(4)]) for l in range(2)])
    conv_prompt = np.stack([np.stack([r[b]["co"][l, 0] for b in range(4)]) for l in range(2)])
    k_sample = np.stack([np.concatenate([r[c]["ko"][l, TP:NT].reshape(4, 16, 8, 64) for c in range(8)], axis=0) for l in range(2)])
    v_sample = np.stack([np.concatenate([r[c]["vo"][l, TP:NT].reshape(4, 16, 8, 64) for c in range(8)], axis=0) for l in range(2)])
    pool_sample = np.stack([np.concatenate([r[c]["po"][l, 1:5] for c in range(8)], axis=0) for l in range(2)])
    conv_sample = np.stack([np.concatenate([r[c]["co"][l, 1:5] for c in range(8)], axis=0) for l in range(2)])
    outs = (y_prompt, y_sample, k_prompt, v_prompt, pool_prompt, conv_prompt, k_sample, v_sample, pool_sample, conv_sample)
    return tuple(np.ascontiguousarray(o, dtype=np.float32) for o in outs)
```

```python
import numpy as np
from contextlib import ExitStack
import concourse.bass as bass
import concourse.mybir as mybir
from concourse.bass_utils import run_bass_kernel_spmd

F32 = mybir.dt.float32
BF16 = mybir.dt.bfloat16
AF = mybir.ActivationFunctionType
ALU = mybir.AluOpType
AX = mybir.AxisListType

D = 1024
DFF = 2816
NFF = 22
NMETA = 16
TP = 4112
NS = 4
TS = 16
NT = TP + NS * TS
PAST = 1024
DEPTH = 2
EPS = 1e-6
import os
STOP = int(os.environ.get("MK_STOP", "99"))
SUB = int(os.environ.get("MK_SUB", "99"))
ATL = int(os.environ.get("MK_ATT", "99"))
NTILES = int(os.environ.get("MK_TILES", "9"))
NLAYERS = int(os.environ.get("MK_LAYERS", "2"))
TILES = [(i * 512, 512) for i in range(8)] + [(4096, 80)]


class _Op:
    __slots__ = ("idx", "eng", "fn", "deps", "kind", "dkey", "sem", "val", "signal")


class Sched:
    ENGS = ("pe", "act", "dve", "pool", "sp")

    def __init__(self):
        self.ops = []
        self.lastw = {}
        self.readers = {}

    def add(self, eng, fn, reads=(), writes=(), kind="c", dkey=None):
        op = _Op()
        op.idx = len(self.ops)
        op.eng = eng
        op.fn = fn
        op.kind = kind
        op.dkey = dkey
        deps = set()
        excl = [k for k in reads if isinstance(k, tuple) and k[0] in ("bank", "b")]
        if excl:
            reads = [k for k in reads if k not in excl]
            writes = list(writes) + excl
        for k in reads:
            w = self.lastw.get(k)
            if w is not None:
                deps.add(w)
        for k in writes:
            w = self.lastw.get(k)
            if w is not None:
                deps.add(w)
            for r in self.readers.get(k, ()):
                deps.add(r)
        op.deps = deps
        for k in writes:
            self.lastw[k] = op.idx
            self.readers[k] = []
        for k in reads:
            self.readers.setdefault(k, []).append(op.idx)
        op.signal = kind != "c"
        op.sem = None
        op.val = 0
        self.ops.append(op)
        return op.idx

    def emit(self, nc, stack):
        ops = self.ops
        for op in ops:
            for d in op.deps:
                Dd = ops[d]
                if Dd.kind == "c" and Dd.eng == "pe" and op.eng == "pe" and op.kind == "c":
                    continue
                Dd.signal = True
        esem = {e: stack.enter_context(nc.semaphore("s_" + e)) for e in self.ENGS}
        dsem = {}
        cnt = {}
        for op in ops:
            if op.kind == "c":
                if op.signal:
                    cnt[op.eng] = cnt.get(op.eng, 0) + 1
                    op.sem = esem[op.eng]
                    op.val = cnt[op.eng]
            else:
                if op.dkey not in dsem:
                    dsem[op.dkey] = stack.enter_context(nc.semaphore("d_%d" % len(dsem)))
                cnt[("d", op.dkey)] = cnt.get(("d", op.dkey), 0) + 16
                op.sem = dsem[op.dkey]
                op.val = cnt[("d", op.dkey)]
        self.n_sems = len(dsem) + 5
        per = {e: [] for e in self.ENGS}
        for op in ops:
            per[op.eng].append(op)
        block = stack.enter_context(nc.Block())
        handles = {"pe": block.tensor, "act": block.scalar, "dve": block.vector,
                   "pool": block.gpsimd, "sp": block.sync}
        finals = {}
        for op in ops:
            if op.kind == "d":
                finals[id(op.sem)] = (op.sem, op.val)

        def mk(e):
            def body(eng):
                waited = {}
                for op in per[e]:
                    need = {}
                    for d in op.deps:
                        Dd = ops[d]
                        if Dd.kind == "c" and Dd.eng == "pe" and e == "pe" and op.kind == "c":
                            continue
                        k = id(Dd.sem)
                        if Dd.val > need.get(k, (None, 0))[1]:
                            need[k] = (Dd.sem, Dd.val)
                    for k, (s, v) in need.items():
                        if waited.get(k, 0) >= v:
                            continue
                        eng.wait_ge(s, v)
                        waited[k] = v
                    inst = op.fn(eng)
                    if op.signal:
                        inst.then_inc(op.sem, 1 if op.kind == "c" else 16)
                if e == "sp":
                    for s, v in finals.values():
                        eng.wait_ge(s, v)
            return body

        for e in self.ENGS:
            if per[e] or e == "sp":
                handles[e](mk(e))


WNAMES = ["ffn1_norm", "ffn1_w_gate", "ffn1_w_up", "ffn1_w_down", "mix_norm", "w_in", "pool_w",
          "pool_scale", "pool_proj", "conv_w", "conv_proj", "q_norm", "k_norm", "attn_proj",
          "w_out", "ffn2_norm", "ffn2_w_gate", "ffn2_w_up", "ffn2_w_down"]
WSHAPES = {
    "ffn1_norm": [2, D], "ffn1_w_gate": [2, D, DFF], "ffn1_w_up": [2, D, DFF], "ffn1_w_down": [2, DFF, D],
    "mix_norm": [2, D], "w_in": [2, D, 5632], "pool_w": [2, 4, 64, 64], "pool_scale": [2, 256],
    "pool_proj": [2, 256, D], "conv_w": [2, 3, 256], "conv_proj": [2, 256, D], "q_norm": [2, 8, 64],
    "k_norm": [2, 8, 64], "attn_proj": [2, 512, D], "w_out": [2, D, D], "ffn2_norm": [2, D],
    "ffn2_w_gate": [2, D, DFF], "ffn2_w_up": [2, D, DFF], "ffn2_w_down": [2, DFF, D],
}


def build(depth=DEPTH):
    nc = bass.Bass("TRN2", target_bir_lowering=False)
    S = Sched()

    def din(name, shape, dt=F32):
        return nc.dram_tensor(name, shape, dt, kind="ExternalInput").ap()

    def dout(name, shape):
        return nc.dram_tensor(name, shape, F32, kind="ExternalOutput").ap()

    PROBE = os.environ.get("MK_PROBE", "")
    if PROBE:
        pq = din("pq", [128, 4, 128])
        pk = din("pk", [128, 4, 384])
        pv = din("pv", [384, 512])
        pout = dout("pout", [128, 4, 128])
        pdbg = dout("pdbg", [4, 128, 128])
        xin = ck = cv = spool = sconv = None
        W = {}
    else:
        xin = din("xin", [NT, D])
        ck = din("ck", [2, NS, PAST, 512])
        cv = din("cv", [2, NS, PAST, 512])
        spool = din("spool", [2, NS, 15, 256])
        sconv = din("sconv", [2, NS, 2, 256])
        W = {n: din(n, WSHAPES[n]) for n in WNAMES}
    cU = din("cU", [128, 128])
    cL = din("cL", [128, 128])
    cI = din("cI", [128, 128])
    cT = din("cT", [128, 1024])
    cOnes = din("cOnes", [128, 128])
    cInvw = din("cInvw", [128, 2])
    cInvc = din("cInvc", [128, 2, 16])
    if not PROBE:
        y_o = dout("y", [NT, D])
        k_o = dout("ko", [2, NT, 512])
        v_o = dout("vo", [2, NT, 512])
        p_o = dout("po", [2, 5, 15, 256])
        c_o = dout("co", [2, 5, 2, 256])

    sc = {}
    for l in range(depth):
        for f in ("1", "2"):
            sc["g" + f, l] = nc.dram_tensor("sg%s_%d" % (f, l), [NFF, 128, 1024], BF16)
            sc["u" + f, l] = nc.dram_tensor("su%s_%d" % (f, l), [NFF, 128, 1024], BF16)
            sc["d" + f, l] = nc.dram_tensor("sd%s_%d" % (f, l), [8, 128, DFF], BF16)
        sc["wf", l] = nc.dram_tensor("swf_%d" % l, [32, 128, 1024], BF16)
        sc["qkv", l] = nc.dram_tensor("sqkv_%d" % l, [128, 8 * 1536], BF16)
        sc["wo", l] = nc.dram_tensor("swo_%d" % l, [8, 128, 1024], BF16)
        sc["ap", l] = nc.dram_tensor("sap_%d" % l, [8, 128, 512], BF16)
        sc["pp", l] = nc.dram_tensor("spp_%d" % l, [8, 128, 256], BF16)
        sc["cp", l] = nc.dram_tensor("scp_%d" % l, [8, 128, 256], BF16)
    hT_s = nc.dram_tensor("hT_s", [128, 8, NT], F32)
    vb_s = nc.dram_tensor("vb_s", [NT, 512], BF16)
    grp = {}

    with ExitStack() as st:
        def sb(name, shape, dt=F32):
            return st.enter_context(nc.sbuf_tensor(name, shape, dt))

        KT = sb("KT", [128, 4, 4224], BF16)
        KTs = sb("KTs", [128, 4, 1024], BF16)
        KTt = sb("KTt", [128, 4, 80], BF16)
        hT = sb("hT", [128, 8, 512])
        uB = sb("uB", [128, 8, 512], BF16)
        R = sb("R", [128, 32, 512], BF16)
        Rf = R[:, :, :].bitcast(F32)
        gu = [sb("gu%d" % i, [128, 2, 1024], BF16) for i in range(3)]
        bigs = [sb("big%d" % i, [128, DFF], BF16) for i in range(2)]
        wfs = [sb("wfs%d" % i, [128, 1024], BF16) for i in range(3)]
        exta = sb("exta", [128, 2, 160 + 512 - 145])
        extc = sb("extc", [128, 2, 514])
        Sb = [sb("Sb%d" % i, [128, 527]) for i in range(2)]
        gbt = sb("gbt", [128, 2, 512])
        xbt = sb("xbt", [128, 512])
        pin = sb("pin", [128, 2, 512], BF16)
        yain = sb("yain", [128, 2, 512], BF16)
        ybin = sb("ybin", [128, 2, 512], BF16)
        cvt = sb("cvt", [128, 512])
        tkf = [sb("tkf%d" % i, [128, 512]) for i in range(2)]
        tvf = [sb("tvf%d" % i, [128, 512]) for i in range(2)]
        tsq2 = [sb("tsq0", [128, 512]), xbt]
        tkb2 = [sb("tkb0", [128, 512], BF16), pin[:, 0, :]]
        tvb = [sb("tvb%d" % i, [128, 512], BF16) for i in range(2)]
        ssq82 = [sb("ssq8_%d" % i, [128, 8]) for i in range(2)]
        QT = sb("QT", [128, 4, 512], BF16)
        QTo = sb("QTo", [128, 4, 512], BF16)
        OT = sb("OT", [128, 4, 512], BF16)
        xl = sb("xl", [128, 1024])
        rstd = sb("rstd", [128, 512])
        sig = [sb("sig%d" % i, [128, 512]) for i in range(2)]
        mtmp = sb("mtmp", [128, 512])
        macc = sb("macc", [128, 512])
        vsl = [sb("vsl%d" % i, [128, 512], BF16) for i in range(3)]
        ksl = [sb("ksl%d" % i, [128, 512], BF16) for i in range(2)]
        If = sb("If", [128, 128])
        Ib = sb("Ib", [128, 128], BF16)
        Ub = sb("Ub", [128, 128], BF16)
        Lb = sb("Lb", [128, 128], BF16)
        Ob = sb("Ob", [128, 128], BF16)
        O1 = sb("O1", [128, 128], BF16)
        Tf = sb("Tf", [128, 128])
        Tb = sb("Tb", [128, 128], BF16)
        invw = sb("invw", [128, 2])
        invc = sb("invc", [128, 2, 16])
        gn = sb("gn", [128, 3, 8])
        qg = sb("qg", [128, 512])
        kg = sb("kg", [128, 512])
        pscale = sb("pscale", [128, 2])
        cw = sb("cw", [128, 2, 3])
        pwb = sb("pwb", [128, 2, 128], BF16)
        pwf = sb("pwf", [128, 2, 128])
        dummy = sb("dmy", [128, 8])
        P = [st.enter_context(nc.psum_tensor("P%d" % i, [128, 1024], F32)) for i in range(4)]

        def bank(i):
            return P[i // 2][:, (i % 2) * 512:(i % 2) * 512 + 512]

        def bkey(i):
            return ("bank", i)

        def dma(eng, out, in_, reads, writes, dkey, slow=False):
            if slow:
                S.add(eng, lambda e: e.dma_start(out=out, in_=in_, allow_slow_non_contiguous=True),
                      reads=reads, writes=writes, kind="d", dkey=dkey)
            else:
                S.add(eng, lambda e: e.dma_start(out=out, in_=in_), reads=reads, writes=writes, kind="d", dkey=dkey)

        def mm(out, lhsT, rhs, start, stop, reads, writes):
            S.add("pe", lambda e: e.matmul(out, lhsT=lhsT, rhs=rhs, start=start, stop=stop, skip_group_check=True),
                  reads=reads, writes=writes)

        def tr(out, in_, ident, reads, writes):
            S.add("pe", lambda e: e.matmul(out, lhsT=in_, rhs=ident, start=True, stop=True), reads=reads, writes=writes)

        def act(out, in_, func, reads, writes, bias=None, scale=None):
            kw = {}
            if bias is not None:
                kw["bias"] = bias
            if scale is not None:
                kw["scale"] = scale
            S.add("act", lambda e: e.activation(out=out, in_=in_, func=func, **kw), reads=reads, writes=writes)

        def tt(eng, out, a, b, op, reads, writes):
            S.add(eng, lambda e: e.tensor_tensor(out=out, in0=a, in1=b, op=op), reads=reads, writes=writes)

        def stt(eng, out, a, scalar, b, op0, op1, reads, writes):
            S.add(eng, lambda e: e.scalar_tensor_tensor(out=out, in0=a, scalar=scalar, in1=b, op0=op0, op1=op1),
                  reads=reads, writes=writes)

        def ts1(eng, out, a, scalar, op, reads, writes):
            S.add(eng, lambda e: e.tensor_scalar(out=out, in0=a, scalar1=scalar, scalar2=None, op0=op),
                  reads=reads, writes=writes)

        def cp(eng, out, in_, reads, writes):
            if eng == "act":
                S.add(eng, lambda e: e.activation(out=out, in_=in_, func=AF.Copy), reads=reads, writes=writes)
            else:
                S.add(eng, lambda e: e.tensor_copy(out=out, in_=in_), reads=reads, writes=writes)

        def ms(eng, ap, val, writes):
            S.add(eng, lambda e: e.memset(ap, val), writes=writes)

        dma("sp", If[:], cI[:, :], [], ["If"], "c0")
        dma("sp", Tf[:], cT[:, 0:128], [], ["Tf"], "c1")
        dma("sp", invw[:], cInvw[:, :], [], ["invw"], "c2")
        dma("sp", invc[:], cInvc[:, :, :], [], ["invc"], "c3")
        dma("pool", Ib[:], cI[:, :], [], ["Ib"], "c4")
        dma("pool", Ub[:], cU[:, :], [], ["Ub"], "c5")
        dma("pool", Lb[:], cL[:, :], [], ["Lb"], "c6")
        dma("pool", Tb[:], cT[:, 0:128], [], ["Tb"], "c7")
        dma("pool", O1[:], cOnes[:, :], [], ["O1"], "c8")
        ts1("dve", Ob[:], O1[:], 1.0 / 1024.0, ALU.mult, ["O1"], ["Ob"])
        ms("dve", QT[:], 0.0, ["QT"])
        ms("dve", QTo[:], 0.0, ["QT"])

        def convert(l):
            def cdma(key, idx, out, in_):
                k = (key, l, idx)
                grp.setdefault((key, l), []).append(k)
                fine = (l == 0 and key in ("g1", "u1"))
                dma("pool", out, in_, [], [k], ("cv", key, l))
            for f, (gname, uname, dname) in (("1", ("ffn1_w_gate", "ffn1_w_up", "ffn1_w_down")),
                                             ("2", ("ffn2_w_gate", "ffn2_w_up", "ffn2_w_down"))):
                srcs = {nm: W[wn][l].rearrange("(k p) (c m) -> c p k m", p=128, m=128) for nm, wn in (("g", gname), ("u", uname))}
                dsts = {nm: sc[nm + f, l].ap().rearrange("c p (k m) -> c p k m", m=128) for nm in ("g", "u")}
                for c in range(NFF):
                    for nm in ("g", "u"):
                        cdma(nm + f, c, dsts[nm][c], srcs[nm][c])
                src = W[dname][l].rearrange("(c p) (m n) -> m p c n", p=128, n=128)
                dst = sc["d" + f, l].ap().rearrange("m p (c n) -> m p c n", n=128)
                for m in range(8):
                    cdma("d" + f, m, dst[m], src[m])
                if f == "1":
                    src = W["w_in"][l].rearrange("(k p) (c m) -> c p k m", p=128, m=128)
                    dst = sc["wf", l].ap().rearrange("c p (k m) -> c p k m", m=128)
                    for i in range(8):
                        cdma("wf", i, dst[i], src[i])
                    cdma("qkv", 0, sc["qkv", l].ap().rearrange("p (k n) -> p k n", n=1536),
                         W["w_in"][l][:, 1024:2560].rearrange("(k p) n -> p k n", p=128))
                    for i in range(8, 32):
                        cdma("wf", i, dst[i], src[i + 12])
                    src = W["pool_proj"][l].rearrange("(c p) (m n) -> m p c n", p=128, n=128)
                    dst = sc["pp", l].ap().rearrange("m p (c n) -> m p c n", n=128)
                    for m in range(8):
                        cdma("pp", m, dst[m], src[m])
                    src = W["conv_proj"][l].rearrange("(c p) (m n) -> m p c n", p=128, n=128)
                    dst = sc["cp", l].ap().rearrange("m p (c n) -> m p c n", n=128)
                    for m in range(8):
                        cdma("cp", m, dst[m], src[m])
                    src = W["attn_proj"][l].rearrange("(c p) (m n) -> m p c n", p=128, n=128)
                    dst = sc["ap", l].ap().rearrange("m p (c n) -> m p c n", n=128)
                    for m in range(8):
                        cdma("ap", m, dst[m], src[m])
                    src = W["w_out"][l].rearrange("(c p) (m n) -> m p c n", p=128, n=128)
                    dst = sc["wo", l].ap().rearrange("m p (c n) -> m p c n", n=128)
                    for m in range(8):
                        cdma("wo", m, dst[m], src[m])

        def layer_params(l):
            for i, nm in enumerate(("ffn1_norm", "mix_norm", "ffn2_norm")):
                dma("sp", gn[:, i, :], W[nm][l].rearrange("(k p) -> p k", p=128), [], ["gn"], "lp0", slow=True)
            dma("sp", qg[:], W["q_norm"][l].rearrange("h d -> (h d)").partition_broadcast(128), [], ["qg"], "lp1")
            dma("sp", kg[:], W["k_norm"][l].rearrange("h d -> (h d)").partition_broadcast(128), [], ["kg"], "lp2")
            dma("sp", pscale[:], W["pool_scale"][l].rearrange("(c p) -> p c", p=128), [], ["pscale"], "lp3", slow=True)
            for c in range(2):
                dma("sp", cw[:, c, :], W["conv_w"][l][:, c * 128:(c + 1) * 128].rearrange("j p -> p j"), [], ["cw"], "lp4", slow=True)
            ms("dve", pwf[:], 0.0, ["pwf"])
            for g in range(4):
                dma("sp", pwf[(g % 2) * 64:(g % 2) * 64 + 64, g // 2, (g % 2) * 64:(g % 2) * 64 + 64],
                    W["pool_w"][l][g], [], ["pwf"], "lp5")
            cp("dve", pwb[:], pwf[:], ["pwf"], ["pwb"])
            ts1("dve", kg[:], kg[:], 8.0, ALU.mult, ["kg"], ["kg"])

        def rmsnorm(n, gi):
            hk = ["h%d" % k for k in range(8)]
            for k in range(8):
                act(R[:, k, 0:n], hT[:, k, 0:n], AF.Square, [hk[k]], [("R", k)])
            for k in range(8):
                mm(bank(6)[:, 0:n], Ob[:], R[:, k, 0:n], k == 0, k == 7, [("R", k), "Ob"], [bkey(6)])
            act(rstd[:, 0:n], bank(6)[:, 0:n], AF.Ln, [bkey(6)], ["rstd"], bias=EPS, scale=1.0)
            act(rstd[:, 0:n], rstd[:, 0:n], AF.Exp, ["rstd"], ["rstd"], scale=-0.5)
            for k in range(8):
                stt("dve", uB[:, k, 0:n], hT[:, k, 0:n], gn[:, gi, k:k + 1], rstd[:, 0:n], ALU.mult, ALU.mult,
                    [hk[k], "gn", "rstd"], [("u", k)])

        cnt = {"gu": 0, "big": 0, "wf": 0, "vs": 0, "ks": 0, "z": 0, "q3": 0, "pair": 0}

        def ffn(l, f, n):
            ukeys = [("u", k) for k in range(8)]
            for c in range(NFF):
                s = cnt["gu"] % 3
                cnt["gu"] += 1
                fine = (l == 0 and f == "1")
                dma("sp", gu[s][:, 0, :], sc["g" + f, l].ap()[c], grp["g" + f, l], [("gu", s, 0)], ("gu", s, 0))
                dma("sp", gu[s][:, 1, :], sc["u" + f, l].ap()[c], grp["u" + f, l], [("gu", s, 1)], ("gu", s, 1))
                bg, bu = (c % 2), 2 + (c % 2)
                for k in range(8):
                    mm(bank(bg)[:, 0:n], gu[s][:, 0, k * 128:(k + 1) * 128], uB[:, k, 0:n], k == 0, k == 7,
                       [("gu", s, 0), ukeys[k]], [bkey(bg)])
                for k in range(8):
                    mm(bank(bu)[:, 0:n], gu[s][:, 1, k * 128:(k + 1) * 128], uB[:, k, 0:n], k == 0, k == 7,
                       [("gu", s, 1), ukeys[k]], [bkey(bu)])
                sg = sig[c % 2]
                act(sg[:, 0:n], bank(bg)[:, 0:n], AF.Silu, [bkey(bg)], [("sig", c % 2)])
                tt("dve", R[:, c, 0:n], sg[:, 0:n], bank(bu)[:, 0:n], ALU.mult, [("sig", c % 2), bkey(bu)], [("R", c)])
            for m in range(8):
                s = cnt["big"] % 2
                cnt["big"] += 1
                dma("sp", bigs[s][:, :], sc["d" + f, l].ap()[m], grp["d" + f, l], [("big", s)], ("big", s))
                bd = 4 + (m % 2)
                for c in range(NFF):
                    mm(bank(bd)[:, 0:n], bigs[s][:, c * 128:(c + 1) * 128], R[:, c, 0:n], c == 0, c == NFF - 1,
                       [("big", s), ("R", c)], [bkey(bd)])
                stt("dve", hT[:, m, 0:n], bank(bd)[:, 0:n], 0.5, hT[:, m, 0:n], ALU.mult, ALU.add,
                    [bkey(bd), "h%d" % m], ["h%d" % m])

        ZP = [P[0], P[1]]
        ACC = P[2]
        OP = P[3]
        ATT = {}

        def att_views():
            f = R[:, :, :].bitcast(F32).rearrange("p a b -> p (a b)")
            b = R[:, :, :].rearrange("p a b -> p (a b)")
            ATT["SP"] = [f[:, 0:1024], f[:, 1024:2048]]
            ATT["T1"] = [f[:, 2048 + 1024 * i:3072 + 1024 * i] for i in range(3)]
            ATT["SPb"] = [b[:, 10240 + 1024 * i:11264 + 1024 * i] for i in range(3)]
            ATT["A"] = [b[:, 13312 + 1024 * i:14336 + 1024 * i] for i in range(3)]
        att_views()
        RALL = [("R", c) for c in range(32)]
        ZK = [[bkey(0), bkey(1)], [bkey(2), bkey(3)]]
        AK = [bkey(4), bkey(5)]
        OK_ = [bkey(6), bkey(7)]
        AKEYS = [(nm, z) for nm in ("aSP", "aT1", "aSPb", "aSPbh", "aA") for z in range(3)]

        def att_barrier():
            S.add("pool", lambda e: e.memset(dummy[:], 0.0), reads=RALL + AKEYS, writes=RALL + AKEYS)

        def attention(segs):
            nq = segs[0][1]
            W8 = 8 * nq
            steps = [(si, i) for si, sg in enumerate(segs) for i in range(len(sg[2]))]
            G = len(steps)
            base = cnt["z"]
            cnt["z"] += G

            def ctx(g):
                si, i = steps[g]
                qT, _, kblocks, out_fn = segs[si]
                gg = base + g
                return qT, kblocks, len(kblocks), out_fn, i, gg % 2, gg % 3

            def s_qk_mm(g):
                qT, kblocks, nb, out_fn, i, z, z3 = ctx(g)
                kb = kblocks[i]; nk = kb["nk"]
                Z = ZP[z]
                for h in range(8):
                    mm(Z[0:nk, h * nq:(h + 1) * nq], kb["kT"](h), qT(h), True, True, kb["kreads"] + ["QT"], ZK[z])

            def s_qk_act(g):
                qT, kblocks, nb, out_fn, i, z, z3 = ctx(g)
                kb = kblocks[i]; nk = kb["nk"]
                Z = ZP[z]
                SPt = ATT["SP"][z]
                act(SPt[0:nk, 0:W8], Z[0:nk, 0:W8], AF.Exp, ZK[z], [("aSP", z)])
                act(SPt[0:nk, 0:W8], SPt[0:nk, 0:W8], AF.Ln, [("aSP", z)], [("aSP", z)], bias=1.0, scale=1.0)

            def s_cast(g):
                qT, kblocks, nb, out_fn, i, z, z3 = ctx(g)
                kb = kblocks[i]; nk = kb["nk"]
                Z = ZP[z]; SPt = ATT["SP"][z]; T1t = ATT["T1"][z3]; SPbt = ATT["SPb"][z3]
                kS, kT1, kSb = ("aSP", z), ("aT1", z3), ("aSPb", z3)
                if kb["diag"]:
                    tt("pool", SPbt[0:nk, 0:W8].rearrange("p (h q) -> p h q", q=nq),
                       SPt[0:nk, 0:W8].rearrange("p (h q) -> p h q", q=nq),
                       Tf[0:nk, 0:nq].unsqueeze(1).broadcast_to([nk, 8, nq]), ALU.mult, [kS, "Tf"], [kSb, (kSb[0] + "h", kSb[1])])
                elif W8 >= 1024:
                    cp("pool", SPbt[0:nk, 0:W8 // 2], SPt[0:nk, 0:W8 // 2], [kS], [kSb])
                    cp("dve", SPbt[0:nk, W8 // 2:W8], SPt[0:nk, W8 // 2:W8], [kS], [(kSb[0] + "h", kSb[1])])
                else:
                    cp("dve", SPbt[0:nk, 0:W8], SPt[0:nk, 0:W8], [kS], [kSb])
                tt("dve", T1t[0:nk, 0:W8], Z[0:nk, 0:W8], SPt[0:nk, 0:W8], ALU.subtract, ZK[z] + [kS], [kT1])

            def s_u(g):
                qT, kblocks, nb, out_fn, i, z, z3 = ctx(g)
                kb = kblocks[i]; nk = kb["nk"]
                T1t = ATT["T1"][z3]; SPbt = ATT["SPb"][z3]
                kT1, kSb = ("aT1", z3), ("aSPb", z3)
                for half in range(0, W8, 512):
                    w = min(512, W8 - half)
                    mm(ACC[0:nk, half:half + w], Ub[0:nk, 0:nk], SPbt[0:nk, half:half + w], i == 0, True,
                       [kSb, ("aSPbh", z3), "Ub"], AK)
                tt("dve", T1t[0:nk, 0:W8], T1t[0:nk, 0:W8], ACC[0:nk, 0:W8], ALU.subtract, [kT1] + AK, [kT1])
                kb["vload"](z3)

            def s_lw(g):
                qT, kblocks, nb, out_fn, i, z, z3 = ctx(g)
                kb = kblocks[i]; nk = kb["nk"]
                SPbt = ATT["SPb"][z3]; kSb = ("aSPb", z3)
                if i < nb - 1:
                    nk2 = kblocks[i + 1]["nk"]
                    for half in range(0, W8, 512):
                        w = min(512, W8 - half)
                        if nk2 == nk:
                            mm(ACC[0:nk, half:half + w], Lb[0:nk, 0:nk], SPbt[0:nk, half:half + w], False, True, [kSb, ("aSPbh", z3), "Lb"], AK)
                        else:
                            mm(ACC[0:nk2, half:half + w], O1[0:nk, 0:nk2], SPbt[0:nk, half:half + w], True, True, [kSb, ("aSPbh", z3), "O1"], AK)

            def s_expa(g):
                qT, kblocks, nb, out_fn, i, z, z3 = ctx(g)
                kb = kblocks[i]; nk = kb["nk"]
                T1t = ATT["T1"][z3]; At = ATT["A"][z3]
                kT1, kA = ("aT1", z3), ("aA", z3)
                act(At[0:nk, 0:W8], T1t[0:nk, 0:W8], AF.Exp, [kT1], [kA])
                if kb["diag"]:
                    tt("pool", At[0:nk, 0:W8].rearrange("p (h q) -> p h q", q=nq),
                       At[0:nk, 0:W8].rearrange("p (h q) -> p h q", q=nq),
                       Tb[0:nk, 0:nq].unsqueeze(1).broadcast_to([nk, 8, nq]), ALU.mult, [kA, "Tb"], [kA])

            def s_av(g):
                qT, kblocks, nb, out_fn, i, z, z3 = ctx(g)
                kb = kblocks[i]; nk = kb["nk"]
                At = ATT["A"][z3]; kA = ("aA", z3)
                for h in range(8):
                    mm(OP[:, h * nq:(h + 1) * nq], vsl[z3][0:nk, (h // 2) * 128:(h // 2) * 128 + 128],
                       At[0:nk, h * nq:(h + 1) * nq], i == 0 and (h * nq) % 512 == 0, i == nb - 1,
                       [kA, ("vs", z3)], OK_)
                if i == nb - 1:
                    out_fn()

            nb = G
            for it in range(nb + 4):
                if it < nb:
                    s_qk_mm(it)
                if 0 <= it - 3 < nb:
                    s_lw(it - 3)
                if 0 <= it - 4 < nb:
                    s_av(it - 4)
                if 0 <= it - 3 < nb:
                    s_expa(it - 3)
                if it < nb:
                    s_qk_act(it)
                if 0 <= it - 1 < nb:
                    s_cast(it - 1)
                if 0 <= it - 2 < nb:
                    s_u(it - 2)

        def layer_tile(l, ti):
            s0, n = TILES[ti]
            tail = (ti == 8)
            hk = ["h%d" % k for k in range(8)]
            nblk = [(0, 128), (128, 128), (256, 128), (384, 128)] if not tail else [(0, 80)]
            if ti == 0:
                ms("dve", exta[:], 0.0, ["exta"])
                ms("dve", extc[:], 0.0, ["extc"])
            if STOP < 1:
                return
            if l == 0:
                for (bo, bn) in nblk:
                    for half in range(2):
                        dma("sp", xl[0:bn, half * 512:(half + 1) * 512], xin[s0 + bo:s0 + bo + bn, half * 512:(half + 1) * 512],
                            [], ["xl%d" % half], "xl%d" % half)
                    for half in range(2):
                        bk = 6 + half
                        for j in range(4):
                            k = half * 4 + j
                            tr(bank(bk)[:, j * 128:j * 128 + bn], xl[0:bn, k * 128:(k + 1) * 128], If[0:bn, 0:bn],
                               ["xl%d" % half, "If"], [bkey(bk)])
                        for j in range(4):
                            k = half * 4 + j
                            cp("act" if half else "dve", hT[:, k, bo:bo + bn], bank(bk)[:, j * 128:j * 128 + bn],
                               [bkey(bk)], [hk[k]])
            else:
                for k in range(8):
                    dma("sp", hT[:, k, 0:n], hT_s.ap()[:, k, s0:s0 + n], [("hTs", ti)], [hk[k]], ("hld", k))
            if STOP < 2:
                return
            rmsnorm(n, 0)
            if STOP < 3:
                return
            ffn(l, "1", n)
            if STOP < 4:
                return
            rmsnorm(n, 1)
            ukeys = [("u", k) for k in range(8)]
            Rflat = R[:, :, :].rearrange("p a b -> p (a b)")
            for k in range(8):
                dma("sp", Rflat[:, k * 1536:(k + 1) * 1536], sc["qkv", l].ap()[:, k * 1536:(k + 1) * 1536], grp["qkv", l],
                    RALL[3 * k:3 * k + 3], ("qkvw", k))

            def wf_load(idx):
                s = cnt["wf"] % 3
                cnt["wf"] += 1
                dma("sp", wfs[s][:, :], sc["wf", l].ap()[idx], grp["wf", l], [("wf", s)], ("wf", s))
                return s

            def fm_proj(idx, bk):
                s = wf_load(idx)
                for k in range(8):
                    mm(bank(bk)[:, 0:n], wfs[s][:, k * 128:(k + 1) * 128], uB[:, k, 0:n], k == 0, k == 7,
                       [("wf", s), ukeys[k]], [bkey(bk)])

            if not tail:
                segs = [(15, 0, n)]
                W_a = 15 + n
            else:
                segs = [(31 * j + 15, 16 * j, 16) for j in range(5)]
                W_a = 155
            if tail:
                for c in range(2):
                    cp("dve", Sb[0][:, 0:15], exta[:, c, 512:527], ["exta"], ["Sb0"])
                    cp("dve", exta[:, c, 0:15], Sb[0][:, 0:15], ["Sb0"], ["exta"])
                    cp("dve", Sb[0][:, 0:2], extc[:, c, 512:514], ["extc"], ["Sb0"])
                    cp("dve", extc[:, c, 0:2], Sb[0][:, 0:2], ["Sb0"], ["extc"])
                for j in range(1, 5):
                    for c in range(2):
                        dma("sp", exta[:, c, 31 * j:31 * j + 15],
                            spool[l, j - 1][:, c * 128:(c + 1) * 128].rearrange("t p -> p t"),
                            [], ["exta"], "haloA", slow=True)
                        dma("sp", extc[:, c, 18 * j:18 * j + 2],
                            sconv[l, j - 1][:, c * 128:(c + 1) * 128].rearrange("t p -> p t"),
                            [], ["extc"], "haloC", slow=True)
            csegs = [(2, 0, n)] if not tail else [(18 * j + 2, 16 * j, 16) for j in range(5)]
            W_c = 2 + n if not tail else 90
            for c in range(2):
                fm_proj(c, c)
                for (eo, to, sn) in segs:
                    cp("act", exta[:, c, eo:eo + sn], bank(c)[:, to:to + sn], [bkey(c)], ["exta"])
            for c in range(2):
                fm_proj(2 + c, 2)
                cp("act", xbt[:, 0:n], bank(2)[:, 0:n], [bkey(2)], ["xbt"])
                fm_proj(4 + c, 3)
                cp("act", gbt[:, c, 0:n], bank(3)[:, 0:n], [bkey(3)], ["gbt"])
                fm_proj(6 + c, 4)
                for (eo, to, sn) in csegs:
                    tt("dve", extc[:, c, eo:eo + sn], bank(4)[:, to:to + sn], xbt[:, to:to + sn], ALU.mult,
                       [bkey(4), "xbt"], ["extc"])
            if ti == 7 or tail:
                for c in range(2):
                    if tail:
                        lst = [(j, 31 * j + 16, 18 * j + 16) for j in range(5)]
                    else:
                        lst = []
                    for (j, ao, co_) in lst:
                        tr(bank(7)[0:15, 0:128], exta[:, c, ao:ao + 15], If[:, :], ["exta", "If"], [bkey(7)])
                        cp("dve", cvt[0:15, 0:128], bank(7)[0:15, 0:128], [bkey(7)], ["cvt"])
                        dma("sp", p_o[l, j][:, c * 128:(c + 1) * 128], cvt[0:15, 0:128], ["cvt"], [], "po")
                        tr(bank(7)[0:2, 128:256], extc[:, c, co_:co_ + 2], If[:, :], ["extc", "If"], [bkey(7)])
                        cp("dve", cvt[0:2, 128:256], bank(7)[0:2, 128:256], [bkey(7)], ["cvt"])
                        dma("sp", c_o[l, j][:, c * 128:(c + 1) * 128], cvt[0:2, 128:256], ["cvt"], [], "co")
            Rw = R[:, 0:24, :].rearrange("p (k r) b -> p k (r b)", r=3)
            blk_banks = {}

            def qkv_mm(bi):
                bo, bn = nblk[bi]
                g3 = cnt["q3"] % 2
                cnt["q3"] += 1
                b3 = [3 * g3, 3 * g3 + 1, 3 * g3 + 2]
                blk_banks[bi] = b3
                for j in range(3):
                    for k in range(8):
                        mm(bank(b3[j])[0:bn, :], uB[:, k, bo:bo + bn], Rw[:, k, j * 512:(j + 1) * 512], k == 0, k == 7,
                           RALL[3 * k:3 * k + 3] + [ukeys[k]], [bkey(b3[j])])

            def qkv_chain(bi):
                bo, bn = nblk[bi]
                b3 = blk_banks[bi]
                r0 = s0 + bo
                QK = (0, 1)
                src = [bank(b3[j])[0:bn, :] for j in QK]
                tsq = [tsq2[j] for j in QK]
                ssq = [ssq82[j] for j in QK]
                tkb = [tkb2[j] for j in QK]
                ktsq = [("tsq", 0), "xbt"]
                kssq = [("ssq8", 0), ("ssq8", 1)]
                ktkb = [("tkb", 0), "pin"]
                kf = [tkf[bi % 2], tkf[(bi + 1) % 2]]
                kfk = [("tkf", bi % 2), ("tkf", (bi + 1) % 2)]
                gains = [(qg, "qg"), (kg, "kg")]
                for j in QK:
                    act(tsq[j][0:bn, :], src[j], AF.Square, [bkey(b3[j])], [ktsq[j]])
                vf = tvf[bi % 2]
                vb = tvb[bi % 2]
                cp("act", vf[0:bn, :], bank(b3[2])[0:bn, :], [bkey(b3[2])], [("tvf", bi % 2)])
                for j in QK:
                    S.add("dve", lambda e, bn=bn, t=tsq[j], q=ssq[j]: e.tensor_reduce(
                        out=q[0:bn, :], in_=t[0:bn, :].rearrange("p (h d) -> p h d", d=64), axis=AX.X, op=ALU.add),
                        reads=[ktsq[j]], writes=[kssq[j]])
                for j in QK:
                    act(ssq[j][0:bn, :], ssq[j][0:bn, :], AF.Ln, [kssq[j]], [kssq[j]], bias=64.0 * EPS, scale=1.0)
                    act(ssq[j][0:bn, :], ssq[j][0:bn, :], AF.Exp, [kssq[j]], [kssq[j]], scale=-0.5)
                for j in QK:
                    tt("dve", kf[j][0:bn, :].rearrange("p (h d) -> p h d", d=64), src[j].rearrange("p (h d) -> p h d", d=64),
                       ssq[j][0:bn, :].unsqueeze(2).broadcast_to([bn, 8, 64]), ALU.mult, [bkey(b3[j]), kssq[j]], [kfk[j]])
                tt("dve", tkb[0][0:bn, :], kf[0][0:bn, :], qg[0:bn, :], ALU.mult, [kfk[0], "qg"], [ktkb[0]])
                tt("dve", kf[1][0:bn, :], kf[1][0:bn, :], kg[0:bn, :], ALU.mult, [kfk[1], "kg"], [kfk[1]])
                cp("pool", tkb[1][0:bn, :], kf[1][0:bn, :], [kfk[1]], [ktkb[1]])
                dma("sp", k_o[l, r0:r0 + bn, :], kf[1][0:bn, :], [kfk[1]], [], ("ko", (bi + 1) % 2))
                cp("pool", vb[0:bn, :], vf[0:bn, :], [("tvf", bi % 2)], [("tvb", bi % 2)])
                dma("sp", v_o[l, r0:r0 + bn, :], vf[0:bn, :], [("tvf", bi % 2)], [], ("vo", bi % 2))
                dma("sp", vb_s.ap()[r0:r0 + bn, :], vb[0:bn, :], [("tvb", bi % 2)], [("vbs", r0 // 128)], ("vbs", bi % 2))
                for j in QK:
                    bt = 7 - j
                    for c in range(4):
                        tr(bank(bt)[:, c * 128:c * 128 + bn], tkb[j][0:bn, c * 128:(c + 1) * 128], Ib[0:bn, 0:bn],
                           [ktkb[j], "Ib"], [bkey(bt)])
                pq = bank(7).rearrange("p (c t) -> p c t", t=128)
                pk = bank(6).rearrange("p (c t) -> p c t", t=128)
                cp("act", QT[0:64, :, bo:bo + bn], pq[0:64, :, 0:bn], [bkey(7)], ["QT"])
                cp("act", QTo[64:128, :, bo:bo + bn], pq[64:128, :, 0:bn], [bkey(7)], ["QT"])
                if not tail:
                    cp("act", KT[:, :, r0:r0 + bn], pk[:, :, 0:bn], [bkey(6)], [("KT", r0 // 128)])
                else:
                    cp("act", KTt[:, :, 0:bn], pk[:, :, 0:bn], [bkey(6)], ["KTt"])
                    cp("act", KT[:, :, r0:r0 + 16], pk[:, :, 0:16], [bkey(6)], [("KT", 32)])

            qkv_mm(0)
            for c in range(2):
                e = exta[:, c, :]
                tt("dve", Sb[0][:, 1:W_a], e[:, 1:W_a], e[:, 0:W_a - 1], ALU.add, ["exta"], ["Sb0"])
                tt("dve", Sb[1][:, 3:W_a], Sb[0][:, 3:W_a], Sb[0][:, 1:W_a - 2], ALU.add, ["Sb0"], ["Sb1"])
                if c == 0:
                    lo, hi = Sb[0], Sb[1]
                    klo, khi = "Sb0", "Sb1"
                else:
                    tt("dve", Sb[0][:, 7:W_a], Sb[1][:, 7:W_a], Sb[1][:, 3:W_a - 4], ALU.add, ["Sb1"], ["Sb0"])
                    tt("dve", Sb[1][:, 15:W_a], Sb[0][:, 15:W_a], Sb[0][:, 7:W_a - 8], ALU.add, ["Sb0"], ["Sb1"])
                    lo, hi = Sb[0], Sb[1]
                    klo, khi = "Sb0", "Sb1"
                for (eo, to, sn) in segs:
                    stt("dve", pin[0:64, c, to:to + sn], lo[0:64, eo:eo + sn], invw[0:64, c:c + 1],
                        e[0:64, eo:eo + sn], ALU.mult, ALU.subtract, [klo, "invw", "exta"], ["pin"])
                    stt("dve", pin[64:128, c, to:to + sn], hi[64:128, eo:eo + sn], invw[64:128, c:c + 1],
                        e[64:128, eo:eo + sn], ALU.mult, ALU.subtract, [khi, "invw", "exta"], ["pin"])
                if ti == 0:
                    tt("dve", cvt[0:64, 0:16], lo[0:64, 15:31], invc[0:64, c, :], ALU.mult, [klo, "invc"], ["cvt"])
                    tt("dve", pin[0:64, c, 0:16], cvt[0:64, 0:16], e[0:64, 15:31], ALU.subtract, ["cvt", "exta"], ["pin"])
                    tt("dve", cvt[64:128, 0:16], hi[64:128, 15:31], invc[64:128, c, :], ALU.mult, [khi, "invc"], ["cvt"])
                    tt("dve", pin[64:128, c, 0:16], cvt[64:128, 0:16], e[64:128, 15:31], ALU.subtract, ["cvt", "exta"], ["pin"])
            for c in range(2):
                mm(bank(7)[:, 0:n], pwb[:, c, :], pin[:, c, 0:n], True, True, ["pwb", "pin"], [bkey(7)])
                ts1("dve", yain[:, c, 0:n], bank(7)[:, 0:n], pscale[:, c:c + 1], ALU.mult, [bkey(7), "pscale"], ["yain"])
            for c in range(2):
                x_ = extc[:, c, :]
                for (eo, to, sn) in csegs:
                    ts1("dve", cvt[:, to:to + sn], x_[:, eo - 2:eo - 2 + sn], cw[:, c, 0:1], ALU.mult, ["extc", "cw"], ["cvt"])
                    stt("dve", cvt[:, to:to + sn], x_[:, eo - 1:eo - 1 + sn], cw[:, c, 1:2], cvt[:, to:to + sn],
                        ALU.mult, ALU.add, ["extc", "cw", "cvt"], ["cvt"])
                    stt("dve", cvt[:, to:to + sn], x_[:, eo:eo + sn], cw[:, c, 2:3], cvt[:, to:to + sn],
                        ALU.mult, ALU.add, ["extc", "cw", "cvt"], ["cvt"])
                tt("dve", ybin[:, c, 0:n], cvt[:, 0:n], gbt[:, c, 0:n], ALU.mult, ["cvt", "gbt"], ["ybin"])
            if not tail:
                for c in range(2):
                    cp("dve", Sb[0][:, 0:15], exta[:, c, 512:527], ["exta"], ["Sb0"])
                    cp("dve", exta[:, c, 0:15], Sb[0][:, 0:15], ["Sb0"], ["exta"])
                    cp("dve", Sb[0][:, 0:2], extc[:, c, 512:514], ["extc"], ["Sb0"])
                    cp("dve", extc[:, c, 0:2], Sb[0][:, 0:2], ["Sb0"], ["extc"])
            if STOP < 5:
                return
            for bi in range(len(nblk)):
                if bi + 1 < len(nblk):
                    qkv_mm(bi + 1)
                qkv_chain(bi)
            if STOP < 7:
                return
            def qT_fn(off, nq):
                return lambda h: (QTo if h % 2 else QT)[:, h // 2, off:off + nq]

            def kblock_prompt(jb, nk, diag):
                def vload(s):
                    dma("sp", vsl[s][0:nk, :], vb_s.ap()[jb * 128:jb * 128 + nk, :], [("vbs", jb)], [("vs", s)], ("vs", s))
                return {"nk": nk, "diag": diag, "vload": vload, "kreads": [("KT", jb)],
                        "kT": lambda h: KT[:, h // 2, jb * 128:jb * 128 + nk]}

            def out_fn_mk(off, nq):
                def f():
                    OPv = OP[:, 0:8 * nq].rearrange("p (c two q) -> p c two q", two=2, q=nq)
                    cp("dve", OT[0:64, :, off:off + nq], OPv[0:64, :, 0, :], OK_, ["OT"])
                    cp("act", OT[64:128, :, off:off + nq], OPv[64:128, :, 1, :], OK_, ["OT"])
                return f

            att_barrier()
            if not tail:
                segs = []
                for qb in range(4):
                    gi_ = s0 // 128 + qb
                    kbl = [kblock_prompt(gi_, 128, True)] + [kblock_prompt(j, 128, False) for j in range(gi_ - 1, -1, -1)]
                    segs.append((qT_fn(qb * 128, 128), 128, kbl, out_fn_mk(qb * 128, 128)))
                attention(segs)
            else:
                kbl = [kblock_prompt(32, 16, True)] + [kblock_prompt(j, 128, False) for j in range(31, -1, -1)]
                attention([(qT_fn(0, 16), 16, kbl, out_fn_mk(0, 16))])
                for si in range(NS):
                    for jb in range(8):
                        s = cnt["ks"] % 2
                        cnt["ks"] += 1
                        dma("pool", ksl[s][:, :], ck[l, si, jb * 128:(jb + 1) * 128, :], [], [("ks", s)], ("ks", s))
                        bt = 6 + (jb % 2)
                        pb = bank(bt)
                        for c in range(4):
                            tr(pb[:, c * 128:(c + 1) * 128], ksl[s][:, c * 128:(c + 1) * 128], Ib[:, :],
                               [("ks", s), "Ib"], [bkey(bt)])
                        cp("act" if jb % 2 else "dve", KTs[:, :, jb * 128:(jb + 1) * 128],
                           pb.rearrange("p (c t) -> p c t", t=128), [bkey(bt)], ["KTs"])

                    def kb_cache(jb, si=si):
                        def vload(s):
                            dma("pool", vsl[s][:, :], cv[l, si, jb * 128:(jb + 1) * 128, :], [], [("vs", s)], ("vs", s))
                        return {"nk": 128, "diag": False, "vload": vload, "kreads": ["KTs"],
                                "kT": lambda h: KTs[:, h // 2, jb * 128:(jb + 1) * 128]}

                    def kb_new(si=si):
                        r = TP + 16 * si
                        def vload(s):
                            dma("sp", vsl[s][0:16, :], vb_s.ap()[r:r + 16, :], [("vbs", 32)], [("vs", s)], ("vs", s))
                        return {"nk": 16, "diag": True, "vload": vload, "kreads": ["KTt"],
                                "kT": lambda h: KTt[:, h // 2, 16 + 16 * si:32 + 16 * si]}
                    kbl = [kb_new()] + [kb_cache(j) for j in range(7, -1, -1)]
                    attention([(qT_fn(16 + 16 * si, 16), 16, kbl, out_fn_mk(16 + 16 * si, 16))])
            att_barrier()
            if STOP < 8:
                return
            for m in range(8):
                ms("pool", macc[:, 0:n], 0.0, ["macc"])
                for br in range(3):
                    pr = cnt["pair"] % 3
                    cnt["pair"] += 1
                    by, bg = 2 * pr, 2 * pr + 1
                    s = cnt["big"] % 2
                    cnt["big"] += 1
                    if br == 0:
                        dma("sp", bigs[s][:, 0:256], sc["pp", l].ap()[m], grp["pp", l], [("big", s)], ("big", s))
                        for c in range(2):
                            mm(bank(by)[:, 0:n], bigs[s][:, c * 128:(c + 1) * 128], yain[:, c, 0:n], c == 0, c == 1,
                               [("big", s), "yain"], [bkey(by)])
                    elif br == 1:
                        dma("sp", bigs[s][:, 0:256], sc["cp", l].ap()[m], grp["cp", l], [("big", s)], ("big", s))
                        for c in range(2):
                            mm(bank(by)[:, 0:n], bigs[s][:, c * 128:(c + 1) * 128], ybin[:, c, 0:n], c == 0, c == 1,
                               [("big", s), "ybin"], [bkey(by)])
                    else:
                        dma("sp", bigs[s][:, 0:512], sc["ap", l].ap()[m], grp["ap", l], [("big", s)], ("big", s))
                        for c in range(4):
                            mm(bank(by)[:, 0:n], bigs[s][:, c * 128:(c + 1) * 128], OT[:, c, 0:n], c == 0, c == 3,
                               [("big", s), "OT"], [bkey(by)])
                    fm_proj(8 + br * 8 + m, bg)
                    sg = sig[br % 2]
                    act(sg[:, 0:n], bank(bg)[:, 0:n], AF.Sigmoid, [bkey(bg)], [("sig", br % 2)])
                    tt("dve", mtmp[:, 0:n], sg[:, 0:n], bank(by)[:, 0:n], ALU.mult, [("sig", br % 2), bkey(by)], ["mtmp"])
                    if br < 2:
                        tt("pool", macc[:, 0:n], macc[:, 0:n], mtmp[:, 0:n], ALU.add, ["macc", "mtmp"], ["macc"])
                    else:
                        tt("pool", R[:, 8 + m, 0:n], macc[:, 0:n], mtmp[:, 0:n], ALU.add, ["macc", "mtmp"], [("R", 8 + m)])
            for m2 in range(8):
                s = cnt["big"] % 2
                cnt["big"] += 1
                dma("sp", bigs[s][:, 0:1024], sc["wo", l].ap()[m2], grp["wo", l], [("big", s)], ("big", s))
                bd = 6 + (m2 % 2)
                for m in range(8):
                    mm(bank(bd)[:, 0:n], bigs[s][:, m * 128:(m + 1) * 128], R[:, 8 + m, 0:n], m == 0, m == 7,
                       [("big", s), ("R", 8 + m)], [bkey(bd)])
                tt("dve", hT[:, m2, 0:n], hT[:, m2, 0:n], bank(bd)[:, 0:n], ALU.add, [hk[m2], bkey(bd)], [hk[m2]])
            rmsnorm(n, 2)
            ffn(l, "2", n)
            if l < depth - 1:
                for k in range(8):
                    dma("sp", hT_s.ap()[:, k, s0:s0 + n], hT[:, k, 0:n], [hk[k]], [("hTs", ti)], ("hst", k))
            else:
                for (bo, bn) in nblk:
                    for half in range(2):
                        bk = 4 + half
                        for j in range(4):
                            k = half * 4 + j
                            tr(bank(bk)[0:bn, j * 128:(j + 1) * 128], hT[:, k, bo:bo + bn], If[:, :], [hk[k], "If"], [bkey(bk)])
                        cp("act" if half else "dve", xl[0:bn, half * 512:(half + 1) * 512], bank(bk)[0:bn, :], [bkey(bk)],
                           ["xl%d" % half])
                        dma("sp", y_o[s0 + bo:s0 + bo + bn, half * 512:(half + 1) * 512], xl[0:bn, half * 512:(half + 1) * 512],
                            ["xl%d" % half], [], "yo%d" % half)

        if PROBE:
            dma("pool", QT[0:64, :, 0:128], pq[0:64, :, :], ["QT"], ["QT"], "pq")
            dma("pool", QTo[64:128, :, 0:128], pq[64:128, :, :], ["QT"], ["QT"], "pq")
            dma("pool", KT[:, :, 0:384], pk[:, :, :], [], [("KT", 0), ("KT", 1), ("KT", 2)], "pk")
            dma("pool", vb_s.ap()[0:384, :], pv[:, :], [], [("vbs", 0), ("vbs", 1), ("vbs", 2)], "pv")

            def qT_fn(off, nq):
                return lambda h: (QTo if h % 2 else QT)[:, h // 2, off:off + nq]

            def kblock_prompt(jb, nk, diag):
                def vload(s):
                    dma("sp", vsl[s][0:nk, :], vb_s.ap()[jb * 128:jb * 128 + nk, :], [("vbs", jb)], [("vs", s)], ("vs", s))
                return {"nk": nk, "diag": diag, "vload": vload, "kreads": [("KT", jb)],
                        "kT": lambda h: KT[:, h // 2, jb * 128:jb * 128 + nk]}

            def out_fn():
                for p in range(4):
                    cp("dve", OT[0:64, p, 0:128], OP[0:64, (2 * p) * 128:(2 * p) * 128 + 128], OK_, ["OT"])
                    cp("act", OT[64:128, p, 0:128], OP[64:128, (2 * p + 1) * 128:(2 * p + 1) * 128 + 128], OK_, ["OT"])
            att_barrier()
            attention([(qT_fn(0, 128), 128, [kblock_prompt(2, 128, True), kblock_prompt(1, 128, False), kblock_prompt(0, 128, False)], out_fn)])
            att_barrier()
            for ii, (nm, z) in enumerate((("SP", 0), ("SP", 1), ("T1", 0), ("T1", 1))):
                dma("sp", pdbg[ii], ATT[nm][z][:, 0:128], RALL, [], "pdbg")
            for p in range(4):
                cp("dve", xl[:, p * 128:(p + 1) * 128], OT[:, p, 0:128], ["OT"], ["xl"])
            dma("sp", pout[:, :, :], xl[:, 0:512].rearrange("p (a b) -> p a b", b=128), ["xl"], [], "pout")
            S.emit(nc, st)
            return nc
        for l in range(depth):
            if os.environ.get("MK_CONV", "1") == "1":
                convert(l)
        for l in range(min(depth, NLAYERS)):
            if os.environ.get("MK_LP", "1") == "1":
                layer_params(l)
            tl = list(range(len(TILES)))
            if NTILES < 9:
                tl = tl[:NTILES - 1] + [8] if NTILES > 1 else [0]
            for ti in tl:
                layer_tile(l, ti)
        S.emit(nc, st)
    return nc


_CACHE = {}


def _consts():
    j = np.arange(128)
    U = (j[:, None] > j[None, :]).astype(np.float32)
    L = (j[:, None] <= j[None, :]).astype(np.float32)
    I = np.eye(128, dtype=np.float32)
    T = (j[:, None] < j[None, :]).astype(np.float32)
    T8 = np.tile(T, (1, 8))
    ones = np.ones((128, 128), np.float32)
    wins = np.array([[2, 8], [4, 16]], np.float32)
    invw = np.zeros((128, 2), np.float32)
    invc = np.zeros((128, 2, 16), np.float32)
    pos = np.arange(16, dtype=np.float32)
    for half in range(2):
        for c in range(2):
            w = wins[half, c]
            invw[half * 64:(half + 1) * 64, c] = 1.0 / w
            invc[half * 64:(half + 1) * 64, c, :] = 1.0 / np.minimum(pos + 1.0, w)
    return {"cU": U, "cL": L, "cI": I, "cT": T8, "cOnes": ones, "cInvw": invw, "cInvc": invc}


def kernel(**inputs):
    f32 = lambda a: np.ascontiguousarray(np.asarray(a, dtype=np.float32))
    inp = {k: f32(v) for k, v in inputs.items()}
    if "nc" not in _CACHE:
        _CACHE["nc"] = build()
    nc = _CACHE["nc"]
    consts = _consts()
    in_maps = []
    for c in range(8):
        sq = c % 4
        xin = np.concatenate([inp["meta"], inp["x_prompt"][sq], inp["x_sample"][4 * c:4 * c + 4].reshape(64, D)], axis=0)
        m = {"xin": np.ascontiguousarray(xin),
             "ck": np.ascontiguousarray(inp["cache_k"][:, 4 * c:4 * c + 4].reshape(2, NS, PAST, 512)),
             "cv": np.ascontiguousarray(inp["cache_v"][:, 4 * c:4 * c + 4].reshape(2, NS, PAST, 512)),
             "spool": np.ascontiguousarray(inp["state_pool"][:, 4 * c:4 * c + 4]),
             "sconv": np.ascontiguousarray(inp["state_conv"][:, 4 * c:4 * c + 4])}
        for n in WNAMES:
            m[n] = inp[n]
        m.update(consts)
        in_maps.append(m)
    res = run_bass_kernel_spmd(nc, in_maps, core_ids=list(range(8)))
    r = res.results
    y_prompt = np.stack([r[b]["y"][NMETA:TP] for b in range(4)]).reshape(4, 4096, D)
    y_sample = np.concatenate([r[c]["y"][TP:NT].reshape(4, 16, D) for c in range(8)], axis=0)
    k_prompt = np.stack([np.stack([r[b]["ko"][l, 0:TP] for b in range(4)]) for l in range(2)]).reshape(2, 4, TP, 8, 64)
    v_prompt = np.stack([np.stack([r[b]["vo"][l, 0:TP] for b in range(4)]) for l in range(2)]).reshape(2, 4, TP, 8, 64)
    pool_prompt = np.stack([np.stack([r[b]["po"][l, 0] for b in range(4)]) for l in range(2)])
    conv_prompt = np.stack([np.stack([r[b]["co"][l, 0] for b in range(4)]) for l in range(2)])
    k_sample = np.stack([np.concatenate([r[c]["ko"][l, TP:NT].reshape(4, 16, 8, 64) for c in range(8)], axis=0) for l in range(2)])
    v_sample = np.stack([np.concatenate([r[c]["vo"][l, TP:NT].reshape(4, 16, 8, 64) for c in range(8)], axis=0) for l in range(2)])
    pool_sample = np.stack([np.concatenate([r[c]["po"][l, 1:5] for c in range(8)], axis=0) for l in range(2)])
    conv_sample = np.stack([np.concatenate([r[c]["co"][l, 1:5] for c in range(8)], axis=0) for l in range(2)])
    outs = (y_prompt, y_sample, k_prompt, v_prompt, pool_prompt, conv_prompt, k_sample, v_sample, pool_sample, conv_sample)
    return tuple(np.ascontiguousarray(o, dtype=np.float32) for o in outs)
```

```python
import numpy as np
from contextlib import ExitStack
import concourse.bass as bass
import concourse.mybir as mybir
from concourse.bass_utils import run_bass_kernel_spmd

F32 = mybir.dt.float32
BF16 = mybir.dt.bfloat16
AF = mybir.ActivationFunctionType
ALU = mybir.AluOpType
AX = mybir.AxisListType

D = 1024
DFF = 2816
NFF = 22
NMETA = 16
TP = 4112
NS = 4
TS = 16
NT = TP + NS * TS
PAST = 1024
DEPTH = 2
EPS = 1e-6
import os
STOP = int(os.environ.get("MK_STOP", "99"))
SUB = int(os.environ.get("MK_SUB", "99"))
ATL = int(os.environ.get("MK_ATT", "99"))
NTILES = int(os.environ.get("MK_TILES", "9"))
NLAYERS = int(os.environ.get("MK_LAYERS", "2"))
TILES = [(i * 512, 512) for i in range(8)] + [(4096, 80)]


class _Op:
    __slots__ = ("idx", "eng", "fn", "deps", "kind", "dkey", "sem", "val", "signal")


class Sched:
    ENGS = ("pe", "act", "dve", "pool", "sp")

    def __init__(self):
        self.ops = []
        self.lastw = {}
        self.readers = {}

    def add(self, eng, fn, reads=(), writes=(), kind="c", dkey=None):
        op = _Op()
        op.idx = len(self.ops)
        op.eng = eng
        op.fn = fn
        op.kind = kind
        op.dkey = dkey
        deps = set()
        excl = [k for k in reads if isinstance(k, tuple) and k[0] in ("bank", "b")]
        if excl:
            reads = [k for k in reads if k not in excl]
            writes = list(writes) + excl
        for k in reads:
            w = self.lastw.get(k)
            if w is not None:
                deps.add(w)
        for k in writes:
            w = self.lastw.get(k)
            if w is not None:
                deps.add(w)
            for r in self.readers.get(k, ()):
                deps.add(r)
        op.deps = deps
        for k in writes:
            self.lastw[k] = op.idx
            self.readers[k] = []
        for k in reads:
            self.readers.setdefault(k, []).append(op.idx)
        op.signal = kind != "c"
        op.sem = None
        op.val = 0
        self.ops.append(op)
        return op.idx

    def emit(self, nc, stack):
        ops = self.ops
        for op in ops:
            for d in op.deps:
                Dd = ops[d]
                if Dd.kind == "c" and Dd.eng == "pe" and op.eng == "pe" and op.kind == "c":
                    continue
                Dd.signal = True
        esem = {e: stack.enter_context(nc.semaphore("s_" + e)) for e in self.ENGS}
        dsem = {}
        cnt = {}
        for op in ops:
            if op.kind == "c":
                if op.signal:
                    cnt[op.eng] = cnt.get(op.eng, 0) + 1
                    op.sem = esem[op.eng]
                    op.val = cnt[op.eng]
            else:
                if op.dkey not in dsem:
                    dsem[op.dkey] = stack.enter_context(nc.semaphore("d_%d" % len(dsem)))
                cnt[("d", op.dkey)] = cnt.get(("d", op.dkey), 0) + 16
                op.sem = dsem[op.dkey]
                op.val = cnt[("d", op.dkey)]
        self.n_sems = len(dsem) + 5
        per = {e: [] for e in self.ENGS}
        for op in ops:
            per[op.eng].append(op)
        block = stack.enter_context(nc.Block())
        handles = {"pe": block.tensor, "act": block.scalar, "dve": block.vector,
                   "pool": block.gpsimd, "sp": block.sync}
        finals = {}
        for op in ops:
            if op.kind == "d":
                finals[id(op.sem)] = (op.sem, op.val)

        def mk(e):
            def body(eng):
                waited = {}
                for op in per[e]:
                    need = {}
                    for d in op.deps:
                        Dd = ops[d]
                        if Dd.kind == "c" and Dd.eng == "pe" and e == "pe" and op.kind == "c":
                            continue
                        k = id(Dd.sem)
                        if Dd.val > need.get(k, (None, 0))[1]:
                            need[k] = (Dd.sem, Dd.val)
                    for k, (s, v) in need.items():
                        if waited.get(k, 0) >= v:
                            continue
                        eng.wait_ge(s, v)
                        waited[k] = v
                    inst = op.fn(eng)
                    if op.signal:
                        inst.then_inc(op.sem, 1 if op.kind == "c" else 16)
                if e == "sp":
                    for s, v in finals.values():
                        eng.wait_ge(s, v)
            return body

        for e in self.ENGS:
            if per[e] or e == "sp":
                handles[e](mk(e))


WNAMES = ["ffn1_norm", "ffn1_w_gate", "ffn1_w_up", "ffn1_w_down", "mix_norm", "w_in", "pool_w",
          "pool_scale", "pool_proj", "conv_w", "conv_proj", "q_norm", "k_norm", "attn_proj",
          "w_out", "ffn2_norm", "ffn2_w_gate", "ffn2_w_up", "ffn2_w_down"]
WSHAPES = {
    "ffn1_norm": [2, D], "ffn1_w_gate": [2, D, DFF], "ffn1_w_up": [2, D, DFF], "ffn1_w_down": [2, DFF, D],
    "mix_norm": [2, D], "w_in": [2, D, 5632], "pool_w": [2, 4, 64, 64], "pool_scale": [2, 256],
    "pool_proj": [2, 256, D], "conv_w": [2, 3, 256], "conv_proj": [2, 256, D], "q_norm": [2, 8, 64],
    "k_norm": [2, 8, 64], "attn_proj": [2, 512, D], "w_out": [2, D, D], "ffn2_norm": [2, D],
    "ffn2_w_gate": [2, D, DFF], "ffn2_w_up": [2, D, DFF], "ffn2_w_down": [2, DFF, D],
}


def build(depth=DEPTH):
    nc = bass.Bass("TRN2", target_bir_lowering=False)
    S = Sched()

    def din(name, shape, dt=F32):
        return nc.dram_tensor(name, shape, dt, kind="ExternalInput").ap()

    def dout(name, shape):
        return nc.dram_tensor(name, shape, F32, kind="ExternalOutput").ap()

    PROBE = os.environ.get("MK_PROBE", "")
    if PROBE:
        pq = din("pq", [128, 4, 128])
        pk = din("pk", [128, 4, 384])
        pv = din("pv", [384, 512])
        pout = dout("pout", [128, 4, 128])
        pdbg = dout("pdbg", [4, 128, 128])
        xin = ck = cv = spool = sconv = None
        W = {}
    else:
        xin = din("xin", [NT, D])
        ck = din("ck", [2, NS, PAST, 512])
        cv = din("cv", [2, NS, PAST, 512])
        spool = din("spool", [2, NS, 15, 256])
        sconv = din("sconv", [2, NS, 2, 256])
        W = {n: din(n, WSHAPES[n]) for n in WNAMES}
    cU = din("cU", [128, 128])
    cL = din("cL", [128, 128])
    cI = din("cI", [128, 128])
    cT = din("cT", [128, 1024])
    cOnes = din("cOnes", [128, 128])
    cInvw = din("cInvw", [128, 2])
    cInvc = din("cInvc", [128, 2, 16])
    if not PROBE:
        y_o = dout("y", [NT, D])
        k_o = dout("ko", [2, NT, 512])
        v_o = dout("vo", [2, NT, 512])
        p_o = dout("po", [2, 5, 15, 256])
        c_o = dout("co", [2, 5, 2, 256])

    sc = {}
    for l in range(depth):
        for f in ("1", "2"):
            sc["g" + f, l] = nc.dram_tensor("sg%s_%d" % (f, l), [NFF, 128, 1024], BF16)
            sc["u" + f, l] = nc.dram_tensor("su%s_%d" % (f, l), [NFF, 128, 1024], BF16)
            sc["d" + f, l] = nc.dram_tensor("sd%s_%d" % (f, l), [8, 128, DFF], BF16)
        sc["wf", l] = nc.dram_tensor("swf_%d" % l, [32, 128, 1024], BF16)
        sc["qkv", l] = nc.dram_tensor("sqkv_%d" % l, [128, 8 * 1536], BF16)
        sc["wo", l] = nc.dram_tensor("swo_%d" % l, [8, 128, 1024], BF16)
        sc["ap", l] = nc.dram_tensor("sap_%d" % l, [8, 128, 512], BF16)
        sc["pp", l] = nc.dram_tensor("spp_%d" % l, [8, 128, 256], BF16)
        sc["cp", l] = nc.dram_tensor("scp_%d" % l, [8, 128, 256], BF16)
    hT_s = nc.dram_tensor("hT_s", [128, 8, NT], F32)
    vb_s = nc.dram_tensor("vb_s", [NT, 512], BF16)
    grp = {}

    with ExitStack() as st:
        def sb(name, shape, dt=F32):
            return st.enter_context(nc.sbuf_tensor(name, shape, dt))

        KT = sb("KT", [128, 4, 4224], BF16)
        KTs = sb("KTs", [128, 4, 1024], BF16)
        KTt = sb("KTt", [128, 4, 80], BF16)
        hT = sb("hT", [128, 8, 512])
        uB = sb("uB", [128, 8, 512], BF16)
        R = sb("R", [128, 32, 512], BF16)
        Rf = R[:, :, :].bitcast(F32)
        gu = [sb("gu%d" % i, [128, 2, 1024], BF16) for i in range(3)]
        bigs = [sb("big%d" % i, [128, DFF], BF16) for i in range(2)]
        wfs = [sb("wfs%d" % i, [128, 1024], BF16) for i in range(3)]
        exta = sb("exta", [128, 2, 160 + 512 - 145])
        extc = sb("extc", [128, 2, 514])
        Sb = [sb("Sb%d" % i, [128, 527]) for i in range(2)]
        gbt = sb("gbt", [128, 2, 512])
        xbt = sb("xbt", [128, 512])
        pin = sb("pin", [128, 2, 512], BF16)
        yain = sb("yain", [128, 2, 512], BF16)
        ybin = sb("ybin", [128, 2, 512], BF16)
        cvt = sb("cvt", [128, 512])
        tkf = [sb("tkf%d" % i, [128, 512]) for i in range(2)]
        tvf = [sb("tvf%d" % i, [128, 512]) for i in range(2)]
        tsq2 = [sb("tsq0", [128, 512]), xbt]
        tkb2 = [sb("tkb0", [128, 512], BF16), pin[:, 0, :]]
        tvb = [sb("tvb%d" % i, [128, 512], BF16) for i in range(2)]
        ssq82 = [sb("ssq8_%d" % i, [128, 8]) for i in range(2)]
        QT = sb("QT", [128, 4, 512], BF16)
        QTo = sb("QTo", [128, 4, 512], BF16)
        OT = sb("OT", [128, 4, 512], BF16)
        xl = sb("xl", [128, 1024])
        rstd = sb("rstd", [128, 512])
        sig = [sb("sig%d" % i, [128, 512]) for i in range(2)]
        mtmp = sb("mtmp", [128, 512])
        macc = sb("macc", [128, 512])
        vsl = [sb("vsl%d" % i, [128, 512], BF16) for i in range(3)]
        ksl = [sb("ksl%d" % i, [128, 512], BF16) for i in range(2)]
        If = sb("If", [128, 128])
        Ib = sb("Ib", [128, 128], BF16)
        Ub = sb("Ub", [128, 128], BF16)
        Lb = sb("Lb", [128, 128], BF16)
        Ob = sb("Ob", [128, 128], BF16)
        O1 = sb("O1", [128, 128], BF16)
        Tf = sb("Tf", [128, 128])
        Tb = sb("Tb", [128, 128], BF16)
        invw = sb("invw", [128, 2])
        invc = sb("invc", [128, 2, 16])
        gn = sb("gn", [128, 3, 8])
        qg = sb("qg", [128, 512])
        kg = sb("kg", [128, 512])
        pscale = sb("pscale", [128, 2])
        cw = sb("cw", [128, 2, 3])
        pwb = sb("pwb", [128, 2, 128], BF16)
        pwf = sb("pwf", [128, 2, 128])
        dummy = sb("dmy", [128, 8])
        P = [st.enter_context(nc.psum_tensor("P%d" % i, [128, 1024], F32)) for i in range(4)]

        def bank(i):
            return P[i // 2][:, (i % 2) * 512:(i % 2) * 512 + 512]

        def bkey(i):
            return ("bank", i)

        def dma(eng, out, in_, reads, writes, dkey, slow=False):
            if slow:
                S.add(eng, lambda e: e.dma_start(out=out, in_=in_, allow_slow_non_contiguous=True),
                      reads=reads, writes=writes, kind="d", dkey=dkey)
            else:
                S.add(eng, lambda e: e.dma_start(out=out, in_=in_), reads=reads, writes=writes, kind="d", dkey=dkey)

        def mm(out, lhsT, rhs, start, stop, reads, writes):
            S.add("pe", lambda e: e.matmul(out, lhsT=lhsT, rhs=rhs, start=start, stop=stop, skip_group_check=True),
                  reads=reads, writes=writes)

        def tr(out, in_, ident, reads, writes):
            S.add("pe", lambda e: e.matmul(out, lhsT=in_, rhs=ident, start=True, stop=True), reads=reads, writes=writes)

        def act(out, in_, func, reads, writes, bias=None, scale=None):
            kw = {}
            if bias is not None:
                kw["bias"] = bias
            if scale is not None:
                kw["scale"] = scale
            S.add("act", lambda e: e.activation(out=out, in_=in_, func=func, **kw), reads=reads, writes=writes)

        def tt(eng, out, a, b, op, reads, writes):
            S.add(eng, lambda e: e.tensor_tensor(out=out, in0=a, in1=b, op=op), reads=reads, writes=writes)

        def stt(eng, out, a, scalar, b, op0, op1, reads, writes):
            S.add(eng, lambda e: e.scalar_tensor_tensor(out=out, in0=a, scalar=scalar, in1=b, op0=op0, op1=op1),
                  reads=reads, writes=writes)

        def ts1(eng, out, a, scalar, op, reads, writes):
            S.add(eng, lambda e: e.tensor_scalar(out=out, in0=a, scalar1=scalar, scalar2=None, op0=op),
                  reads=reads, writes=writes)

        def cp(eng, out, in_, reads, writes):
            if eng == "act":
                S.add(eng, lambda e: e.activation(out=out, in_=in_, func=AF.Copy), reads=reads, writes=writes)
            else:
                S.add(eng, lambda e: e.tensor_copy(out=out, in_=in_), reads=reads, writes=writes)

        def ms(eng, ap, val, writes):
            S.add(eng, lambda e: e.memset(ap, val), writes=writes)

        dma("sp", If[:], cI[:, :], [], ["If"], "c0")
        dma("sp", Tf[:], cT[:, 0:128], [], ["Tf"], "c1")
        dma("sp", invw[:], cInvw[:, :], [], ["invw"], "c2")
        dma("sp", invc[:], cInvc[:, :, :], [], ["invc"], "c3")
        dma("pool", Ib[:], cI[:, :], [], ["Ib"], "c4")
        dma("pool", Ub[:], cU[:, :], [], ["Ub"], "c5")
        dma("pool", Lb[:], cL[:, :], [], ["Lb"], "c6")
        dma("pool", Tb[:], cT[:, 0:128], [], ["Tb"], "c7")
        dma("pool", O1[:], cOnes[:, :], [], ["O1"], "c8")
        ts1("dve", Ob[:], O1[:], 1.0 / 1024.0, ALU.mult, ["O1"], ["Ob"])
        ms("dve", QT[:], 0.0, ["QT"])
        ms("dve", QTo[:], 0.0, ["QT"])

        def convert(l):
            def cdma(key, idx, out, in_):
                k = (key, l, idx)
                grp.setdefault((key, l), []).append(k)
                fine = (l == 0 and key in ("g1", "u1"))
                dma("pool", out, in_, [], [k], ("cv", key, l))
            for f, (gname, uname, dname) in (("1", ("ffn1_w_gate", "ffn1_w_up", "ffn1_w_down")),
                                             ("2", ("ffn2_w_gate", "ffn2_w_up", "ffn2_w_down"))):
                srcs = {nm: W[wn][l].rearrange("(k p) (c m) -> c p k m", p=128, m=128) for nm, wn in (("g", gname), ("u", uname))}
                dsts = {nm: sc[nm + f, l].ap().rearrange("c p (k m) -> c p k m", m=128) for nm in ("g", "u")}
                for c in range(NFF):
                    for nm in ("g", "u"):
                        cdma(nm + f, c, dsts[nm][c], srcs[nm][c])
                src = W[dname][l].rearrange("(c p) (m n) -> m p c n", p=128, n=128)
                dst = sc["d" + f, l].ap().rearrange("m p (c n) -> m p c n", n=128)
                for m in range(8):
                    cdma("d" + f, m, dst[m], src[m])
                if f == "1":
                    src = W["w_in"][l].rearrange("(k p) (c m) -> c p k m", p=128, m=128)
                    dst = sc["wf", l].ap().rearrange("c p (k m) -> c p k m", m=128)
                    for i in range(8):
                        cdma("wf", i, dst[i], src[i])
                    cdma("qkv", 0, sc["qkv", l].ap().rearrange("p (k n) -> p k n", n=1536),
                         W["w_in"][l][:, 1024:2560].rearrange("(k p) n -> p k n", p=128))
                    for i in range(8, 32):
                        cdma("wf", i, dst[i], src[i + 12])
                    src = W["pool_proj"][l].rearrange("(c p) (m n) -> m p c n", p=128, n=128)
                    dst = sc["pp", l].ap().rearrange("m p (c n) -> m p c n", n=128)
                    for m in range(8):
                        cdma("pp", m, dst[m], src[m])
                    src = W["conv_proj"][l].rearrange("(c p) (m n) -> m p c n", p=128, n=128)
                    dst = sc["cp", l].ap().rearrange("m p (c n) -> m p c n", n=128)
                    for m in range(8):
                        cdma("cp", m, dst[m], src[m])
                    src = W["attn_proj"][l].rearrange("(c p) (m n) -> m p c n", p=128, n=128)
                    dst = sc["ap", l].ap().rearrange("m p (c n) -> m p c n", n=128)
                    for m in range(8):
                        cdma("ap", m, dst[m], src[m])
                    src = W["w_out"][l].rearrange("(c p) (m n) -> m p c n", p=128, n=128)
                    dst = sc["wo", l].ap().rearrange("m p (c n) -> m p c n", n=128)
                    for m in range(8):
                        cdma("wo", m, dst[m], src[m])

        def layer_params(l):
            for i, nm in enumerate(("ffn1_norm", "mix_norm", "ffn2_norm")):
                dma("sp", gn[:, i, :], W[nm][l].rearrange("(k p) -> p k", p=128), [], ["gn"], "lp0", slow=True)
            dma("sp", qg[:], W["q_norm"][l].rearrange("h d -> (h d)").partition_broadcast(128), [], ["qg"], "lp1")
            dma("sp", kg[:], W["k_norm"][l].rearrange("h d -> (h d)").partition_broadcast(128), [], ["kg"], "lp2")
            dma("sp", pscale[:], W["pool_scale"][l].rearrange("(c p) -> p c", p=128), [], ["pscale"], "lp3", slow=True)
            for c in range(2):
                dma("sp", cw[:, c, :], W["conv_w"][l][:, c * 128:(c + 1) * 128].rearrange("j p -> p j"), [], ["cw"], "lp4", slow=True)
            ms("dve", pwf[:], 0.0, ["pwf"])
            for g in range(4):
                dma("sp", pwf[(g % 2) * 64:(g % 2) * 64 + 64, g // 2, (g % 2) * 64:(g % 2) * 64 + 64],
                    W["pool_w"][l][g], [], ["pwf"], "lp5")
            cp("dve", pwb[:], pwf[:], ["pwf"], ["pwb"])
            ts1("dve", kg[:], kg[:], 8.0, ALU.mult, ["kg"], ["kg"])

        def rmsnorm(n, gi):
            hk = ["h%d" % k for k in range(8)]
            for k in range(8):
                act(R[:, k, 0:n], hT[:, k, 0:n], AF.Square, [hk[k]], [("R", k)])
            for k in range(8):
                mm(bank(6)[:, 0:n], Ob[:], R[:, k, 0:n], k == 0, k == 7, [("R", k), "Ob"], [bkey(6)])
            act(rstd[:, 0:n], bank(6)[:, 0:n], AF.Ln, [bkey(6)], ["rstd"], bias=EPS, scale=1.0)
            act(rstd[:, 0:n], rstd[:, 0:n], AF.Exp, ["rstd"], ["rstd"], scale=-0.5)
            for k in range(8):
                stt("dve", uB[:, k, 0:n], hT[:, k, 0:n], gn[:, gi, k:k + 1], rstd[:, 0:n], ALU.mult, ALU.mult,
                    [hk[k], "gn", "rstd"], [("u", k)])

        cnt = {"gu": 0, "big": 0, "wf": 0, "vs": 0, "ks": 0, "z": 0, "q3": 0, "pair": 0}

        def ffn(l, f, n):
            ukeys = [("u", k) for k in range(8)]
            for c in range(NFF):
                s = cnt["gu"] % 3
                cnt["gu"] += 1
                fine = (l == 0 and f == "1")
                dma("sp", gu[s][:, 0, :], sc["g" + f, l].ap()[c], grp["g" + f, l], [("gu", s, 0)], ("gu", s, 0))
                dma("sp", gu[s][:, 1, :], sc["u" + f, l].ap()[c], grp["u" + f, l], [("gu", s, 1)], ("gu", s, 1))
                bg, bu = (c % 2), 2 + (c % 2)
                for k in range(8):
                    mm(bank(bg)[:, 0:n], gu[s][:, 0, k * 128:(k + 1) * 128], uB[:, k, 0:n], k == 0, k == 7,
                       [("gu", s, 0), ukeys[k]], [bkey(bg)])
                for k in range(8):
                    mm(bank(bu)[:, 0:n], gu[s][:, 1, k * 128:(k + 1) * 128], uB[:, k, 0:n], k == 0, k == 7,
                       [("gu", s, 1), ukeys[k]], [bkey(bu)])
                sg = sig[c % 2]
                act(sg[:, 0:n], bank(bg)[:, 0:n], AF.Silu, [bkey(bg)], [("sig", c % 2)])
                tt("dve", R[:, c, 0:n], sg[:, 0:n], bank(bu)[:, 0:n], ALU.mult, [("sig", c % 2), bkey(bu)], [("R", c)])
            for m in range(8):
                s = cnt["big"] % 2
                cnt["big"] += 1
                dma("sp", bigs[s][:, :], sc["d" + f, l].ap()[m], grp["d" + f, l], [("big", s)], ("big", s))
                bd = 4 + (m % 2)
                for c in range(NFF):
                    mm(bank(bd)[:, 0:n], bigs[s][:, c * 128:(c + 1) * 128], R[:, c, 0:n], c == 0, c == NFF - 1,
                       [("big", s), ("R", c)], [bkey(bd)])
                stt("dve", hT[:, m, 0:n], bank(bd)[:, 0:n], 0.5, hT[:, m, 0:n], ALU.mult, ALU.add,
                    [bkey(bd), "h%d" % m], ["h%d" % m])

        ZP = [P[0], P[1]]
        ACC = P[2]
        OP = P[3]
        ATT = {}

        def att_views():
            f = R[:, :, :].bitcast(F32).rearrange("p a b -> p (a b)")
            b = R[:, :, :].rearrange("p a b -> p (a b)")
            ATT["SP"] = [f[:, 0:1024], f[:, 1024:2048]]
            ATT["T1"] = [f[:, 2048 + 1024 * i:3072 + 1024 * i] for i in range(3)]
            ATT["SPb"] = [b[:, 10240 + 1024 * i:11264 + 1024 * i] for i in range(3)]
            ATT["A"] = [b[:, 13312 + 1024 * i:14336 + 1024 * i] for i in range(3)]
        att_views()
        RALL = [("R", c) for c in range(32)]
        ZK = [[bkey(0), bkey(1)], [bkey(2), bkey(3)]]
        AK = [bkey(4), bkey(5)]
        OK_ = [bkey(6), bkey(7)]
        AKEYS = [(nm, z) for nm in ("aSP", "aT1", "aSPb", "aSPbh", "aA") for z in range(3)]

        def att_barrier():
            S.add("pool", lambda e: e.memset(dummy[:], 0.0), reads=RALL + AKEYS, writes=RALL + AKEYS)

        def attention(qT, nq, kblocks, out_fn):
            W8 = 8 * nq
            nb = len(kblocks)
            base = cnt["z"]
            cnt["z"] += nb

            def bufs(i):
                g = base + i
                return g % 2, g % 3

            def s_qk_mm(i):
                kb = kblocks[i]; nk = kb["nk"]; z, z3 = bufs(i)
                Z = ZP[z]
                for h in range(8):
                    mm(Z[0:nk, h * nq:(h + 1) * nq], kb["kT"](h), qT(h), True, True, kb["kreads"] + ["QT"], ZK[z])

            def s_qk_act(i):
                kb = kblocks[i]; nk = kb["nk"]; z, z3 = bufs(i)
                Z = ZP[z]
                SPt = ATT["SP"][z]
                act(SPt[0:nk, 0:W8], Z[0:nk, 0:W8], AF.Exp, ZK[z], [("aSP", z)])
                act(SPt[0:nk, 0:W8], SPt[0:nk, 0:W8], AF.Ln, [("aSP", z)], [("aSP", z)], bias=1.0, scale=1.0)

            def s_cast(i):
                kb = kblocks[i]; nk = kb["nk"]; z, z3 = bufs(i)
                Z = ZP[z]; SPt = ATT["SP"][z]; T1t = ATT["T1"][z3]; SPbt = ATT["SPb"][z3]
                kS, kT1, kSb = ("aSP", z), ("aT1", z3), ("aSPb", z3)
                if kb["diag"]:
                    tt("pool", SPbt[0:nk, 0:W8].rearrange("p (h q) -> p h q", q=nq),
                       SPt[0:nk, 0:W8].rearrange("p (h q) -> p h q", q=nq),
                       Tf[0:nk, 0:nq].unsqueeze(1).broadcast_to([nk, 8, nq]), ALU.mult, [kS, "Tf"], [kSb, (kSb[0] + "h", kSb[1])])
                elif W8 >= 1024:
                    cp("pool", SPbt[0:nk, 0:W8 // 2], SPt[0:nk, 0:W8 // 2], [kS], [kSb])
                    cp("dve", SPbt[0:nk, W8 // 2:W8], SPt[0:nk, W8 // 2:W8], [kS], [(kSb[0] + "h", kSb[1])])
                else:
                    cp("dve", SPbt[0:nk, 0:W8], SPt[0:nk, 0:W8], [kS], [kSb])
                tt("dve", T1t[0:nk, 0:W8], Z[0:nk, 0:W8], SPt[0:nk, 0:W8], ALU.subtract, ZK[z] + [kS], [kT1])

            def s_u(i):
                kb = kblocks[i]; nk = kb["nk"]; z, z3 = bufs(i)
                T1t = ATT["T1"][z3]; SPbt = ATT["SPb"][z3]
                kT1, kSb = ("aT1", z3), ("aSPb", z3)
                for half in range(0, W8, 512):
                    w = min(512, W8 - half)
                    mm(ACC[0:nk, half:half + w], Ub[0:nk, 0:nk], SPbt[0:nk, half:half + w], i == 0, True,
                       [kSb, ("aSPbh", z3), "Ub"], AK)
                tt("dve", T1t[0:nk, 0:W8], T1t[0:nk, 0:W8], ACC[0:nk, 0:W8], ALU.subtract, [kT1] + AK, [kT1])
                kb["vload"](z3)

            def s_lw(i):
                kb = kblocks[i]; nk = kb["nk"]; z, z3 = bufs(i)
                SPbt = ATT["SPb"][z3]; kSb = ("aSPb", z3)
                if i < nb - 1:
                    nk2 = kblocks[i + 1]["nk"]
                    for half in range(0, W8, 512):
                        w = min(512, W8 - half)
                        if nk2 == nk:
                            mm(ACC[0:nk, half:half + w], Lb[0:nk, 0:nk], SPbt[0:nk, half:half + w], False, True, [kSb, ("aSPbh", z3), "Lb"], AK)
                        else:
                            mm(ACC[0:nk2, half:half + w], O1[0:nk, 0:nk2], SPbt[0:nk, half:half + w], True, True, [kSb, ("aSPbh", z3), "O1"], AK)

            def s_expa(i):
                kb = kblocks[i]; nk = kb["nk"]; z, z3 = bufs(i)
                T1t = ATT["T1"][z3]; At = ATT["A"][z3]
                kT1, kA = ("aT1", z3), ("aA", z3)
                act(At[0:nk, 0:W8], T1t[0:nk, 0:W8], AF.Exp, [kT1], [kA])
                if kb["diag"]:
                    tt("pool", At[0:nk, 0:W8].rearrange("p (h q) -> p h q", q=nq),
                       At[0:nk, 0:W8].rearrange("p (h q) -> p h q", q=nq),
                       Tb[0:nk, 0:nq].unsqueeze(1).broadcast_to([nk, 8, nq]), ALU.mult, [kA, "Tb"], [kA])

            def s_av(i):
                kb = kblocks[i]; nk = kb["nk"]; z, z3 = bufs(i)
                At = ATT["A"][z3]; kA = ("aA", z3)
                for h in range(8):
                    mm(OP[:, h * nq:(h + 1) * nq], vsl[z3][0:nk, (h // 2) * 128:(h // 2) * 128 + 128],
                       At[0:nk, h * nq:(h + 1) * nq], i == 0 and (h * nq) % 512 == 0, i == nb - 1,
                       [kA, ("vs", z3)], OK_)

            for it in range(nb + 4):
                if it < nb:
                    s_qk_mm(it)
                if 0 <= it - 3 < nb:
                    s_lw(it - 3)
                if 0 <= it - 4 < nb:
                    s_av(it - 4)
                if 0 <= it - 3 < nb:
                    s_expa(it - 3)
                if it < nb:
                    s_qk_act(it)
                if 0 <= it - 1 < nb:
                    s_cast(it - 1)
                if 0 <= it - 2 < nb:
                    s_u(it - 2)
            out_fn()

        def layer_tile(l, ti):
            s0, n = TILES[ti]
            tail = (ti == 8)
            hk = ["h%d" % k for k in range(8)]
            nblk = [(0, 128), (128, 128), (256, 128), (384, 128)] if not tail else [(0, 80)]
            if ti == 0:
                ms("dve", exta[:], 0.0, ["exta"])
                ms("dve", extc[:], 0.0, ["extc"])
            if STOP < 1:
                return
            if l == 0:
                for (bo, bn) in nblk:
                    for half in range(2):
                        dma("sp", xl[0:bn, half * 512:(half + 1) * 512], xin[s0 + bo:s0 + bo + bn, half * 512:(half + 1) * 512],
                            [], ["xl%d" % half], "xl%d" % half)
                    for half in range(2):
                        bk = 6 + half
                        for j in range(4):
                            k = half * 4 + j
                            tr(bank(bk)[:, j * 128:j * 128 + bn], xl[0:bn, k * 128:(k + 1) * 128], If[0:bn, 0:bn],
                               ["xl%d" % half, "If"], [bkey(bk)])
                        cp("act" if half else "dve", hT[:, half * 4:half * 4 + 4, bo:bo + bn],
                           bank(bk).rearrange("p (c t) -> p c t", t=128)[:, :, 0:bn],
                           [bkey(bk)], [hk[half * 4 + j] for j in range(4)])
            else:
                for k in range(8):
                    dma("sp", hT[:, k, 0:n], hT_s.ap()[:, k, s0:s0 + n], [("hTs", ti)], [hk[k]], ("hld", k))
            if STOP < 2:
                return
            rmsnorm(n, 0)
            if STOP < 3:
                return
            ffn(l, "1", n)
            if STOP < 4:
                return
            rmsnorm(n, 1)
            ukeys = [("u", k) for k in range(8)]
            Rflat = R[:, :, :].rearrange("p a b -> p (a b)")
            for k in range(8):
                dma("sp", Rflat[:, k * 1536:(k + 1) * 1536], sc["qkv", l].ap()[:, k * 1536:(k + 1) * 1536], grp["qkv", l],
                    RALL[3 * k:3 * k + 3], ("qkvw", k))

            def wf_load(idx):
                s = cnt["wf"] % 3
                cnt["wf"] += 1
                dma("sp", wfs[s][:, :], sc["wf", l].ap()[idx], grp["wf", l], [("wf", s)], ("wf", s))
                return s

            def fm_proj(idx, bk):
                s = wf_load(idx)
                for k in range(8):
                    mm(bank(bk)[:, 0:n], wfs[s][:, k * 128:(k + 1) * 128], uB[:, k, 0:n], k == 0, k == 7,
                       [("wf", s), ukeys[k]], [bkey(bk)])

            if not tail:
                segs = [(15, 0, n)]
                W_a = 15 + n
            else:
                segs = [(31 * j + 15, 16 * j, 16) for j in range(5)]
                W_a = 155
            if tail:
                for c in range(2):
                    cp("dve", Sb[0][:, 0:15], exta[:, c, 512:527], ["exta"], ["Sb0"])
                    cp("dve", exta[:, c, 0:15], Sb[0][:, 0:15], ["Sb0"], ["exta"])
                    cp("dve", Sb[0][:, 0:2], extc[:, c, 512:514], ["extc"], ["Sb0"])
                    cp("dve", extc[:, c, 0:2], Sb[0][:, 0:2], ["Sb0"], ["extc"])
                for j in range(1, 5):
                    for c in range(2):
                        dma("sp", exta[:, c, 31 * j:31 * j + 15],
                            spool[l, j - 1][:, c * 128:(c + 1) * 128].rearrange("t p -> p t"),
                            [], ["exta"], "haloA", slow=True)
                        dma("sp", extc[:, c, 18 * j:18 * j + 2],
                            sconv[l, j - 1][:, c * 128:(c + 1) * 128].rearrange("t p -> p t"),
                            [], ["extc"], "haloC", slow=True)
            csegs = [(2, 0, n)] if not tail else [(18 * j + 2, 16 * j, 16) for j in range(5)]
            W_c = 2 + n if not tail else 90
            for c in range(2):
                fm_proj(c, c)
                for (eo, to, sn) in segs:
                    cp("act", exta[:, c, eo:eo + sn], bank(c)[:, to:to + sn], [bkey(c)], ["exta"])
            for c in range(2):
                fm_proj(2 + c, 2)
                cp("act", xbt[:, 0:n], bank(2)[:, 0:n], [bkey(2)], ["xbt"])
                fm_proj(4 + c, 3)
                cp("act", gbt[:, c, 0:n], bank(3)[:, 0:n], [bkey(3)], ["gbt"])
                fm_proj(6 + c, 4)
                for (eo, to, sn) in csegs:
                    tt("dve", extc[:, c, eo:eo + sn], bank(4)[:, to:to + sn], xbt[:, to:to + sn], ALU.mult,
                       [bkey(4), "xbt"], ["extc"])
            if ti == 7 or tail:
                for c in range(2):
                    if tail:
                        lst = [(j, 31 * j + 16, 18 * j + 16) for j in range(5)]
                    else:
                        lst = []
                    for (j, ao, co_) in lst:
                        tr(bank(7)[0:15, 0:128], exta[:, c, ao:ao + 15], If[:, :], ["exta", "If"], [bkey(7)])
                        cp("dve", cvt[0:15, 0:128], bank(7)[0:15, 0:128], [bkey(7)], ["cvt"])
                        dma("sp", p_o[l, j][:, c * 128:(c + 1) * 128], cvt[0:15, 0:128], ["cvt"], [], "po")
                        tr(bank(7)[0:2, 128:256], extc[:, c, co_:co_ + 2], If[:, :], ["extc", "If"], [bkey(7)])
                        cp("dve", cvt[0:2, 128:256], bank(7)[0:2, 128:256], [bkey(7)], ["cvt"])
                        dma("sp", c_o[l, j][:, c * 128:(c + 1) * 128], cvt[0:2, 128:256], ["cvt"], [], "co")
            Rw = R[:, 0:24, :].rearrange("p (k r) b -> p k (r b)", r=3)
            blk_banks = {}

            def qkv_mm(bi):
                bo, bn = nblk[bi]
                g3 = cnt["q3"] % 2
                cnt["q3"] += 1
                b3 = [3 * g3, 3 * g3 + 1, 3 * g3 + 2]
                blk_banks[bi] = b3
                for j in range(3):
                    for k in range(8):
                        mm(bank(b3[j])[0:bn, :], uB[:, k, bo:bo + bn], Rw[:, k, j * 512:(j + 1) * 512], k == 0, k == 7,
                           RALL[3 * k:3 * k + 3] + [ukeys[k]], [bkey(b3[j])])

            def qkv_chain(bi):
                bo, bn = nblk[bi]
                b3 = blk_banks[bi]
                r0 = s0 + bo
                QK = (0, 1)
                src = [bank(b3[j])[0:bn, :] for j in QK]
                tsq = [tsq2[j] for j in QK]
                ssq = [ssq82[j] for j in QK]
                tkb = [tkb2[j] for j in QK]
                ktsq = [("tsq", 0), "xbt"]
                kssq = [("ssq8", 0), ("ssq8", 1)]
                ktkb = [("tkb", 0), "pin"]
                kf = [tkf[bi % 2], tkf[(bi + 1) % 2]]
                kfk = [("tkf", bi % 2), ("tkf", (bi + 1) % 2)]
                gains = [(qg, "qg"), (kg, "kg")]
                for j in QK:
                    act(tsq[j][0:bn, :], src[j], AF.Square, [bkey(b3[j])], [ktsq[j]])
                vf = tvf[bi % 2]
                vb = tvb[bi % 2]
                cp("act", vf[0:bn, :], bank(b3[2])[0:bn, :], [bkey(b3[2])], [("tvf", bi % 2)])
                for j in QK:
                    S.add("dve", lambda e, bn=bn, t=tsq[j], q=ssq[j]: e.tensor_reduce(
                        out=q[0:bn, :], in_=t[0:bn, :].rearrange("p (h d) -> p h d", d=64), axis=AX.X, op=ALU.add),
                        reads=[ktsq[j]], writes=[kssq[j]])
                for j in QK:
                    act(ssq[j][0:bn, :], ssq[j][0:bn, :], AF.Ln, [kssq[j]], [kssq[j]], bias=64.0 * EPS, scale=1.0)
                    act(ssq[j][0:bn, :], ssq[j][0:bn, :], AF.Exp, [kssq[j]], [kssq[j]], scale=-0.5)
                for j in QK:
                    tt("dve", kf[j][0:bn, :].rearrange("p (h d) -> p h d", d=64), src[j].rearrange("p (h d) -> p h d", d=64),
                       ssq[j][0:bn, :].unsqueeze(2).broadcast_to([bn, 8, 64]), ALU.mult, [bkey(b3[j]), kssq[j]], [kfk[j]])
                tt("dve", tkb[0][0:bn, :], kf[0][0:bn, :], qg[0:bn, :], ALU.mult, [kfk[0], "qg"], [ktkb[0]])
                tt("dve", kf[1][0:bn, :], kf[1][0:bn, :], kg[0:bn, :], ALU.mult, [kfk[1], "kg"], [kfk[1]])
                cp("pool", tkb[1][0:bn, :], kf[1][0:bn, :], [kfk[1]], [ktkb[1]])
                dma("sp", k_o[l, r0:r0 + bn, :], kf[1][0:bn, :], [kfk[1]], [], ("ko", (bi + 1) % 2))
                cp("pool", vb[0:bn, :], vf[0:bn, :], [("tvf", bi % 2)], [("tvb", bi % 2)])
                dma("sp", v_o[l, r0:r0 + bn, :], vf[0:bn, :], [("tvf", bi % 2)], [], ("vo", bi % 2))
                dma("sp", vb_s.ap()[r0:r0 + bn, :], vb[0:bn, :], [("tvb", bi % 2)], [("vbs", r0 // 128)], ("vbs", bi % 2))
                for j in QK:
                    bt = 7 - j
                    for c in range(4):
                        tr(bank(bt)[:, c * 128:c * 128 + bn], tkb[j][0:bn, c * 128:(c + 1) * 128], Ib[0:bn, 0:bn],
                           [ktkb[j], "Ib"], [bkey(bt)])
                pq = bank(7).rearrange("p (c t) -> p c t", t=128)
                pk = bank(6).rearrange("p (c t) -> p c t", t=128)
                cp("act", QT[0:64, :, bo:bo + bn], pq[0:64, :, 0:bn], [bkey(7)], ["QT"])
                cp("act", QTo[64:128, :, bo:bo + bn], pq[64:128, :, 0:bn], [bkey(7)], ["QT"])
                if not tail:
                    cp("act", KT[:, :, r0:r0 + bn], pk[:, :, 0:bn], [bkey(6)], [("KT", r0 // 128)])
                else:
                    cp("act", KTt[:, :, 0:bn], pk[:, :, 0:bn], [bkey(6)], ["KTt"])
                    cp("act", KT[:, :, r0:r0 + 16], pk[:, :, 0:16], [bkey(6)], [("KT", 32)])

            qkv_mm(0)
            for c in range(2):
                e = exta[:, c, :]
                tt("dve", Sb[0][:, 1:W_a], e[:, 1:W_a], e[:, 0:W_a - 1], ALU.add, ["exta"], ["Sb0"])
                tt("dve", Sb[1][:, 3:W_a], Sb[0][:, 3:W_a], Sb[0][:, 1:W_a - 2], ALU.add, ["Sb0"], ["Sb1"])
                if c == 0:
                    lo, hi = Sb[0], Sb[1]
                    klo, khi = "Sb0", "Sb1"
                else:
                    tt("dve", Sb[0][:, 7:W_a], Sb[1][:, 7:W_a], Sb[1][:, 3:W_a - 4], ALU.add, ["Sb1"], ["Sb0"])
                    tt("dve", Sb[1][:, 15:W_a], Sb[0][:, 15:W_a], Sb[0][:, 7:W_a - 8], ALU.add, ["Sb0"], ["Sb1"])
                    lo, hi = Sb[0], Sb[1]
                    klo, khi = "Sb0", "Sb1"
                for (eo, to, sn) in segs:
                    stt("dve", pin[0:64, c, to:to + sn], lo[0:64, eo:eo + sn], invw[0:64, c:c + 1],
                        e[0:64, eo:eo + sn], ALU.mult, ALU.subtract, [klo, "invw", "exta"], ["pin"])
                    stt("dve", pin[64:128, c, to:to + sn], hi[64:128, eo:eo + sn], invw[64:128, c:c + 1],
                        e[64:128, eo:eo + sn], ALU.mult, ALU.subtract, [khi, "invw", "exta"], ["pin"])
                if ti == 0:
                    tt("dve", cvt[0:64, 0:16], lo[0:64, 15:31], invc[0:64, c, :], ALU.mult, [klo, "invc"], ["cvt"])
                    tt("dve", pin[0:64, c, 0:16], cvt[0:64, 0:16], e[0:64, 15:31], ALU.subtract, ["cvt", "exta"], ["pin"])
                    tt("dve", cvt[64:128, 0:16], hi[64:128, 15:31], invc[64:128, c, :], ALU.mult, [khi, "invc"], ["cvt"])
                    tt("dve", pin[64:128, c, 0:16], cvt[64:128, 0:16], e[64:128, 15:31], ALU.subtract, ["cvt", "exta"], ["pin"])
            for c in range(2):
                mm(bank(7)[:, 0:n], pwb[:, c, :], pin[:, c, 0:n], True, True, ["pwb", "pin"], [bkey(7)])
                ts1("dve", yain[:, c, 0:n], bank(7)[:, 0:n], pscale[:, c:c + 1], ALU.mult, [bkey(7), "pscale"], ["yain"])
            for c in range(2):
                x_ = extc[:, c, :]
                for (eo, to, sn) in csegs:
                    ts1("dve", cvt[:, to:to + sn], x_[:, eo - 2:eo - 2 + sn], cw[:, c, 0:1], ALU.mult, ["extc", "cw"], ["cvt"])
                    stt("dve", cvt[:, to:to + sn], x_[:, eo - 1:eo - 1 + sn], cw[:, c, 1:2], cvt[:, to:to + sn],
                        ALU.mult, ALU.add, ["extc", "cw", "cvt"], ["cvt"])
                    stt("dve", cvt[:, to:to + sn], x_[:, eo:eo + sn], cw[:, c, 2:3], cvt[:, to:to + sn],
                        ALU.mult, ALU.add, ["extc", "cw", "cvt"], ["cvt"])
                tt("dve", ybin[:, c, 0:n], cvt[:, 0:n], gbt[:, c, 0:n], ALU.mult, ["cvt", "gbt"], ["ybin"])
            if not tail:
                for c in range(2):
                    cp("dve", Sb[0][:, 0:15], exta[:, c, 512:527], ["exta"], ["Sb0"])
                    cp("dve", exta[:, c, 0:15], Sb[0][:, 0:15], ["Sb0"], ["exta"])
                    cp("dve", Sb[0][:, 0:2], extc[:, c, 512:514], ["extc"], ["Sb0"])
                    cp("dve", extc[:, c, 0:2], Sb[0][:, 0:2], ["Sb0"], ["extc"])
            if STOP < 5:
                return
            for bi in range(len(nblk)):
                if bi + 1 < len(nblk):
                    qkv_mm(bi + 1)
                qkv_chain(bi)
            if STOP < 7:
                return
            def qT_fn(off, nq):
                return lambda h: (QTo if h % 2 else QT)[:, h // 2, off:off + nq]

            def kblock_prompt(jb, nk, diag):
                def vload(s):
                    dma("sp", vsl[s][0:nk, :], vb_s.ap()[jb * 128:jb * 128 + nk, :], [("vbs", jb)], [("vs", s)], ("vs", s))
                return {"nk": nk, "diag": diag, "vload": vload, "kreads": [("KT", jb)],
                        "kT": lambda h: KT[:, h // 2, jb * 128:jb * 128 + nk]}

            def out_fn_mk(off, nq):
                def f():
                    OPv = OP[:, 0:8 * nq].rearrange("p (c two q) -> p c two q", two=2, q=nq)
                    cp("dve", OT[0:64, :, off:off + nq], OPv[0:64, :, 0, :], OK_, ["OT"])
                    cp("act", OT[64:128, :, off:off + nq], OPv[64:128, :, 1, :], OK_, ["OT"])
                return f

            att_barrier()
            if not tail:
                for qb in range(4):
                    gi_ = s0 // 128 + qb
                    kbl = [kblock_prompt(gi_, 128, True)] + [kblock_prompt(j, 128, False) for j in range(gi_ - 1, -1, -1)]
                    attention(qT_fn(qb * 128, 128), 128, kbl, out_fn_mk(qb * 128, 128))
            else:
                kbl = [kblock_prompt(32, 16, True)] + [kblock_prompt(j, 128, False) for j in range(31, -1, -1)]
                attention(qT_fn(0, 16), 16, kbl, out_fn_mk(0, 16))
                for si in range(NS):
                    for jb in range(8):
                        s = cnt["ks"] % 2
                        cnt["ks"] += 1
                        dma("pool", ksl[s][:, :], ck[l, si, jb * 128:(jb + 1) * 128, :], [], [("ks", s)], ("ks", s))
                        bt = 6 + (jb % 2)
                        pb = bank(bt)
                        for c in range(4):
                            tr(pb[:, c * 128:(c + 1) * 128], ksl[s][:, c * 128:(c + 1) * 128], Ib[:, :],
                               [("ks", s), "Ib"], [bkey(bt)])
                        cp("act" if jb % 2 else "dve", KTs[:, :, jb * 128:(jb + 1) * 128],
                           pb.rearrange("p (c t) -> p c t", t=128), [bkey(bt)], ["KTs"])

                    def kb_cache(jb, si=si):
                        def vload(s):
                            dma("pool", vsl[s][:, :], cv[l, si, jb * 128:(jb + 1) * 128, :], [], [("vs", s)], ("vs", s))
                        return {"nk": 128, "diag": False, "vload": vload, "kreads": ["KTs"],
                                "kT": lambda h: KTs[:, h // 2, jb * 128:(jb + 1) * 128]}

                    def kb_new(si=si):
                        r = TP + 16 * si
                        def vload(s):
                            dma("sp", vsl[s][0:16, :], vb_s.ap()[r:r + 16, :], [("vbs", 32)], [("vs", s)], ("vs", s))
                        return {"nk": 16, "diag": True, "vload": vload, "kreads": ["KTt"],
                                "kT": lambda h: KTt[:, h // 2, 16 + 16 * si:32 + 16 * si]}
                    kbl = [kb_new()] + [kb_cache(j) for j in range(7, -1, -1)]
                    attention(qT_fn(16 + 16 * si, 16), 16, kbl, out_fn_mk(16 + 16 * si, 16))
            att_barrier()
            if STOP < 8:
                return
            for m in range(8):
                ms("pool", macc[:, 0:n], 0.0, ["macc"])
                for br in range(3):
                    pr = cnt["pair"] % 3
                    cnt["pair"] += 1
                    by, bg = 2 * pr, 2 * pr + 1
                    s = cnt["big"] % 2
                    cnt["big"] += 1
                    if br == 0:
                        dma("sp", bigs[s][:, 0:256], sc["pp", l].ap()[m], grp["pp", l], [("big", s)], ("big", s))
                        for c in range(2):
                            mm(bank(by)[:, 0:n], bigs[s][:, c * 128:(c + 1) * 128], yain[:, c, 0:n], c == 0, c == 1,
                               [("big", s), "yain"], [bkey(by)])
                    elif br == 1:
                        dma("sp", bigs[s][:, 0:256], sc["cp", l].ap()[m], grp["cp", l], [("big", s)], ("big", s))
                        for c in range(2):
                            mm(bank(by)[:, 0:n], bigs[s][:, c * 128:(c + 1) * 128], ybin[:, c, 0:n], c == 0, c == 1,
                               [("big", s), "ybin"], [bkey(by)])
                    else:
                        dma("sp", bigs[s][:, 0:512], sc["ap", l].ap()[m], grp["ap", l], [("big", s)], ("big", s))
                        for c in range(4):
                            mm(bank(by)[:, 0:n], bigs[s][:, c * 128:(c + 1) * 128], OT[:, c, 0:n], c == 0, c == 3,
                               [("big", s), "OT"], [bkey(by)])
                    fm_proj(8 + br * 8 + m, bg)
                    sg = sig[br % 2]
                    act(sg[:, 0:n], bank(bg)[:, 0:n], AF.Sigmoid, [bkey(bg)], [("sig", br % 2)])
                    tt("dve", mtmp[:, 0:n], sg[:, 0:n], bank(by)[:, 0:n], ALU.mult, [("sig", br % 2), bkey(by)], ["mtmp"])
                    if br < 2:
                        tt("pool", macc[:, 0:n], macc[:, 0:n], mtmp[:, 0:n], ALU.add, ["macc", "mtmp"], ["macc"])
                    else:
                        tt("pool", R[:, 8 + m, 0:n], macc[:, 0:n], mtmp[:, 0:n], ALU.add, ["macc", "mtmp"], [("R", 8 + m)])
            for m2 in range(8):
                s = cnt["big"] % 2
                cnt["big"] += 1
                dma("sp", bigs[s][:, 0:1024], sc["wo", l].ap()[m2], grp["wo", l], [("big", s)], ("big", s))
                bd = 6 + (m2 % 2)
                for m in range(8):
                    mm(bank(bd)[:, 0:n], bigs[s][:, m * 128:(m + 1) * 128], R[:, 8 + m, 0:n], m == 0, m == 7,
                       [("big", s), ("R", 8 + m)], [bkey(bd)])
                tt("dve", hT[:, m2, 0:n], hT[:, m2, 0:n], bank(bd)[:, 0:n], ALU.add, [hk[m2], bkey(bd)], [hk[m2]])
            rmsnorm(n, 2)
            ffn(l, "2", n)
            if l < depth - 1:
                for k in range(8):
                    dma("sp", hT_s.ap()[:, k, s0:s0 + n], hT[:, k, 0:n], [hk[k]], [("hTs", ti)], ("hst", k))
            else:
                for (bo, bn) in nblk:
                    for half in range(2):
                        bk = 4 + half
                        for j in range(4):
                            k = half * 4 + j
                            tr(bank(bk)[0:bn, j * 128:(j + 1) * 128], hT[:, k, bo:bo + bn], If[:, :], [hk[k], "If"], [bkey(bk)])
                        cp("act" if half else "dve", xl[0:bn, half * 512:(half + 1) * 512], bank(bk)[0:bn, :], [bkey(bk)],
                           ["xl%d" % half])
                        dma("sp", y_o[s0 + bo:s0 + bo + bn, half * 512:(half + 1) * 512], xl[0:bn, half * 512:(half + 1) * 512],
                            ["xl%d" % half], [], "yo%d" % half)

        if PROBE:
            dma("pool", QT[0:64, :, 0:128], pq[0:64, :, :], ["QT"], ["QT"], "pq")
            dma("pool", QTo[64:128, :, 0:128], pq[64:128, :, :], ["QT"], ["QT"], "pq")
            dma("pool", KT[:, :, 0:384], pk[:, :, :], [], [("KT", 0), ("KT", 1), ("KT", 2)], "pk")
            dma("pool", vb_s.ap()[0:384, :], pv[:, :], [], [("vbs", 0), ("vbs", 1), ("vbs", 2)], "pv")

            def qT_fn(off, nq):
                return lambda h: (QTo if h % 2 else QT)[:, h // 2, off:off + nq]

            def kblock_prompt(jb, nk, diag):
                def vload(s):
                    dma("sp", vsl[s][0:nk, :], vb_s.ap()[jb * 128:jb * 128 + nk, :], [("vbs", jb)], [("vs", s)], ("vs", s))
                return {"nk": nk, "diag": diag, "vload": vload, "kreads": [("KT", jb)],
                        "kT": lambda h: KT[:, h // 2, jb * 128:jb * 128 + nk]}

            def out_fn():
                for p in range(4):
                    cp("dve", OT[0:64, p, 0:128], OP[0:64, (2 * p) * 128:(2 * p) * 128 + 128], OK_, ["OT"])
                    cp("act", OT[64:128, p, 0:128], OP[64:128, (2 * p + 1) * 128:(2 * p + 1) * 128 + 128], OK_, ["OT"])
            att_barrier()
            attention(qT_fn(0, 128), 128, [kblock_prompt(2, 128, True), kblock_prompt(1, 128, False), kblock_prompt(0, 128, False)], out_fn)
            att_barrier()
            for ii, (nm, z) in enumerate((("SP", 0), ("SP", 1), ("T1", 0), ("T1", 1))):
                dma("sp", pdbg[ii], ATT[nm][z][:, 0:128], RALL, [], "pdbg")
            for p in range(4):
                cp("dve", xl[:, p * 128:(p + 1) * 128], OT[:, p, 0:128], ["OT"], ["xl"])
            dma("sp", pout[:, :, :], xl[:, 0:512].rearrange("p (a b) -> p a b", b=128), ["xl"], [], "pout")
            S.emit(nc, st)
            return nc
        for l in range(depth):
            if os.environ.get("MK_CONV", "1") == "1":
                convert(l)
        for l in range(min(depth, NLAYERS)):
            if os.environ.get("MK_LP", "1") == "1":
                layer_params(l)
            tl = list(range(len(TILES)))
            if NTILES < 9:
                tl = tl[:NTILES - 1] + [8] if NTILES > 1 else [0]
            for ti in tl:
                layer_tile(l, ti)
        S.emit(nc, st)
    return nc


_CACHE = {}


def _consts():
    j = np.arange(128)
    U = (j[:, None] > j[None, :]).astype(np.float32)
    L = (j[:, None] <= j[None, :]).astype(np.float32)
    I = np.eye(128, dtype=np.float32)
    T = (j[:, None] < j[None, :]).astype(np.float32)
    T8 = np.tile(T, (1, 8))
    ones = np.ones((128, 128), np.float32)
    wins = np.array([[2, 8], [4, 16]], np.float32)
    invw = np.zeros((128, 2), np.float32)
    invc = np.zeros((128, 2, 16), np.float32)
    pos = np.arange(16, dtype=np.float32)
    for half in range(2):
        for c in range(2):
            w = wins[half, c]
            invw[half * 64:(half + 1) * 64, c] = 1.0 / w
            invc[half * 64:(half + 1) * 64, c, :] = 1.0 / np.minimum(pos + 1.0, w)
    return {"cU": U, "cL": L, "cI": I, "cT": T8, "cOnes": ones, "cInvw": invw, "cInvc": invc}


def kernel(**inputs):
    f32 = lambda a: np.ascontiguousarray(np.asarray(a, dtype=np.float32))
    inp = {k: f32(v) for k, v in inputs.items()}
    if "nc" not in _CACHE:
        _CACHE["nc"] = build()
    nc = _CACHE["nc"]
    consts = _consts()
    in_maps = []
    for c in range(8):
        sq = c % 4
        xin = np.concatenate([inp["meta"], inp["x_prompt"][sq], inp["x_sample"][4 * c:4 * c + 4].reshape(64, D)], axis=0)
        m = {"xin": np.ascontiguousarray(xin),
             "ck": np.ascontiguousarray(inp["cache_k"][:, 4 * c:4 * c + 4].reshape(2, NS, PAST, 512)),
             "cv": np.ascontiguousarray(inp["cache_v"][:, 4 * c:4 * c + 4].reshape(2, NS, PAST, 512)),
             "spool": np.ascontiguousarray(inp["state_pool"][:, 4 * c:4 * c + 4]),
             "sconv": np.ascontiguousarray(inp["state_conv"][:, 4 * c:4 * c + 4])}
        for n in WNAMES:
            m[n] = inp[n]
        m.update(consts)
        in_maps.append(m)
    res = run_bass_kernel_spmd(nc, in_maps, core_ids=list(range(8)))
    r = res.results
    y_prompt = np.stack([r[b]["y"][NMETA:TP] for b in range(4)]).reshape(4, 4096, D)
    y_sample = np.concatenate([r[c]["y"][TP:NT].reshape(4, 16, D) for c in range(8)], axis=0)
    k_prompt = np.stack([np.stack([r[b]["ko"][l, 0:TP] for b in range(4)]) for l in range(2)]).reshape(2, 4, TP, 8, 64)
    v_prompt = np.stack([np.stack([r[b]["vo"][l, 0:TP] for b in range(4)]) for l in range(2)]).reshape(2, 4, TP, 8, 64)
    pool_prompt = np.stack([np.stack([r[b]["po"][l, 0] for b in range(4)]) for l in range(2)])
    conv_prompt = np.stack([np.stack([r[b]["co"][l, 0] for b in range(4)]) for l in range(2)])
    k_sample = np.stack([np.concatenate([r[c]["ko"][l, TP:NT].reshape(4, 16, 8, 64) for c in range(8)], axis=0) for l in range(2)])
    v_sample = np.stack([np.concatenate([r[c]["vo"][l, TP:NT].reshape(4, 16, 8, 64) for c in range(8)], axis=0) for l in range(2)])
    pool_sample = np.stack([np.concatenate([r[c]["po"][l, 1:5] for c in range(8)], axis=0) for l in range(2)])
    conv_sample = np.stack([np.concatenate([r[c]["co"][l, 1:5] for c in range(8)], axis=0) for l in range(2)])
    outs = (y_prompt, y_sample, k_prompt, v_prompt, pool_prompt, conv_prompt, k_sample, v_sample, pool_sample, conv_sample)
    return tuple(np.ascontiguousarray(o, dtype=np.float32) for o in outs)
```
